# Optimizing a Trainium2 kernel written in Bass

```python
import math
import jax
import jax.numpy as jnp
from jax import lax
import numpy as np


D_MODEL = 1024
BATCH = 32
SEQ = 2048
DEPTH = 2

HEAD_DIM = 64
N_HEADS_MOBA = 4
N_HEADS_DIFF = 4
DIFF_DIM = HEAD_DIM // 2
N_HEADS_FOX = 4
N_HEADS_DSA = 4
N_IDX_HEADS = 4
IDX_DIM = 64
BRANCH_WIDTH = 4 * HEAD_DIM
N_BRANCHES = 4
MOBA_BLOCK = 256
MOBA_TOPK = 3
MOBA_Q_CHUNK = 16
DSA_TOPK_MAX = 256
DSA_Q_CHUNK = 64
Q_BLOCK = 128
ROPE_THETA = 10000.0
N_EXPERTS = 256
TOP_K = 8
N_GROUPS = 8
TOPK_GROUPS = 4
EXPERT_DIM = 256
SHARED_DIM = 256
ROUTED_SCALE = 2.5
MOE_BLOCK = 128
LN_EPS = 1e-5
DEEPNORM_ALPHA = (2 * DEPTH) ** 0.25
DEEPNORM_BETA = (8 * DEPTH) ** -0.25

IN_SEGMENTS = (
    ('moba_q', BRANCH_WIDTH), ('moba_k', BRANCH_WIDTH), ('moba_v', BRANCH_WIDTH),
    ('diff_q', BRANCH_WIDTH), ('diff_k', BRANCH_WIDTH), ('diff_v', BRANCH_WIDTH),
    ('fox_q', BRANCH_WIDTH), ('fox_k', BRANCH_WIDTH), ('fox_v', BRANCH_WIDTH), ('fox_f', N_HEADS_FOX),
    ('dsa_q', BRANCH_WIDTH), ('dsa_k', BRANCH_WIDTH), ('dsa_v', BRANCH_WIDTH),
    ('idx_q', N_IDX_HEADS * IDX_DIM), ('idx_k', IDX_DIM), ('idx_w', N_IDX_HEADS),
    ('gates', N_BRANCHES * D_MODEL),
)

kernel_name = 'hybrid_moba_diff_fox_dsa_moe_deepnorm'

F32 = jnp.float32


def _normal(key, shape, scale):
    return jax.random.normal(key, shape, F32) * scale


def rope(x, pos):
    half = x.shape[-1] // 2
    inv_freq = ROPE_THETA ** (-jnp.arange(half, dtype=F32) / half)
    ang = pos.astype(F32)[:, None] * inv_freq[None, :]
    cos = jnp.cos(ang)[None, :, None, :].astype(x.dtype)
    sin = jnp.sin(ang)[None, :, None, :].astype(x.dtype)
    x1, x2 = x[..., :half], x[..., half:]
    return jnp.concatenate([x1 * cos - x2 * sin, x1 * sin + x2 * cos], axis=-1)


def layer_norm(x, g, b):
    xf = x.astype(F32)
    mu = jnp.mean(xf, axis=-1, keepdims=True)
    xc = xf - mu
    var = jnp.mean(xc * xc, axis=-1, keepdims=True)
    return (xc * lax.rsqrt(var + LN_EPS) * g.astype(F32) + b.astype(F32)).astype(x.dtype)


def split_projection(proj):
    parts = {}
    off = 0
    for name, width in IN_SEGMENTS:
        parts[name] = proj[..., off:off + width]
        off += width
    return parts


def moba_attention(q, k, v):
    B, T, H, d = q.shape
    nb = -(-T // MOBA_BLOCK)
    pad = nb * MOBA_BLOCK - T
    kp = jnp.pad(k, ((0, 0), (0, pad), (0, 0), (0, 0)))
    vp = jnp.pad(v, ((0, 0), (0, pad), (0, 0), (0, 0)))
    kb = kp.reshape(B, nb, MOBA_BLOCK, H, d).transpose(0, 3, 1, 2, 4)
    vb = vp.reshape(B, nb, MOBA_BLOCK, H, d).transpose(0, 3, 1, 2, 4)
    kmean = jnp.mean(kb.astype(F32), axis=3)
    n_sel = min(MOBA_TOPK, nb - 1)
    scale = d ** -0.5
    bi = jnp.arange(B)[:, None, None, None]
    hi = jnp.arange(H)[None, :, None, None]
    blk_ids = jnp.arange(nb)

    def chunk(i):
        s0 = i * MOBA_Q_CHUNK
        qc = lax.dynamic_slice_in_dim(q, s0, MOBA_Q_CHUNK, axis=1).transpose(0, 2, 1, 3)
        qpos = s0 + jnp.arange(MOBA_Q_CHUNK)
        cur = s0 // MOBA_BLOCK
        ko = lax.dynamic_index_in_dim(kb, cur, axis=2, keepdims=False)
        vo = lax.dynamic_index_in_dim(vb, cur, axis=2, keepdims=False)
        kpos_own = cur * MOBA_BLOCK + jnp.arange(MOBA_BLOCK)
        lo = jnp.einsum('bhqd,bhkd->bhqk', qc, ko, preferred_element_type=F32) * scale
        lo = jnp.where(kpos_own[None, :] <= qpos[:, None], lo, -jnp.inf)
        if n_sel == 0:
            p = jax.nn.softmax(lo, axis=-1)
            out = jnp.einsum('bhqk,bhkd->bhqd', p.astype(v.dtype), vo)
            return out.transpose(0, 2, 1, 3)
        gate = jnp.einsum('bhqd,bhnd->bhqn', qc.astype(F32), kmean)
        gate = jnp.where(blk_ids < cur, gate, -jnp.inf)
        gval, gidx = lax.top_k(gate, n_sel)
        ks = kb[bi, hi, gidx]
        vs = vb[bi, hi, gidx]
        ls = jnp.einsum('bhqd,bhqnkd->bhqnk', qc, ks, preferred_element_type=F32) * scale
        ls = jnp.where((gval > -jnp.inf)[..., None], ls, -jnp.inf)
        logits = jnp.concatenate([lo, ls.reshape(B, H, MOBA_Q_CHUNK, n_sel * MOBA_BLOCK)], axis=-1)
        p = jax.nn.softmax(logits, axis=-1).astype(v.dtype)
        po = p[..., :MOBA_BLOCK]
        ps = p[..., MOBA_BLOCK:].reshape(B, H, MOBA_Q_CHUNK, n_sel, MOBA_BLOCK)
        out = jnp.einsum('bhqk,bhkd->bhqd', po, vo) + jnp.einsum('bhqnk,bhqnkd->bhqd', ps, vs)
        return out.transpose(0, 2, 1, 3)

    o = lax.map(chunk, jnp.arange(T // MOBA_Q_CHUNK))
    return o.transpose(1, 0, 2, 3, 4).reshape(B, T, H * d)


def diff_attention(q, k, v, lam, subln_g, lambda_init):
    B, T, H, _, dd = q.shape
    scale = dd ** -0.5
    kpos = jnp.arange(T)

    def block(i):
        s0 = i * Q_BLOCK
        qb = lax.dynamic_slice_in_dim(q, s0, Q_BLOCK, axis=1)
        logits = jnp.einsum('bqhcd,bkhcd->bhcqk', qb, k, preferred_element_type=F32) * scale
        qpos = s0 + jnp.arange(Q_BLOCK)
        logits = jnp.where(kpos[None, :] <= qpos[:, None], logits, -jnp.inf)
        p = jax.nn.softmax(logits, axis=-1)
        a = p[:, :, 0] - lam * p[:, :, 1]
        return jnp.einsum('bhqk,bkhd->bqhd', a.astype(v.dtype), v)

    o = lax.map(block, jnp.arange(T // Q_BLOCK))
    o = o.transpose(1, 0, 2, 3, 4).reshape(B, T, H, 2 * dd).astype(F32)
    o = o * lax.rsqrt(jnp.mean(o * o, axis=-1, keepdims=True) + LN_EPS) * subln_g.astype(F32)
    return (o * (1.0 - lambda_init)).astype(v.dtype).reshape(B, T, H * 2 * dd)


def fox_attention(q, k, v, logf):
    B, T, H, d = q.shape
    scale = d ** -0.5
    c = jnp.cumsum(logf, axis=1).transpose(0, 2, 1)
    kpos = jnp.arange(T)

    def block(i):
        s0 = i * Q_BLOCK
        qb = lax.dynamic_slice_in_dim(q, s0, Q_BLOCK, axis=1)
        cb = lax.dynamic_slice_in_dim(c, s0, Q_BLOCK, axis=2)
        logits = jnp.einsum('bqhd,bkhd->bhqk', qb, k, preferred_element_type=F32) * scale
        logits = logits + (cb[..., :, None] - c[..., None, :])
        qpos = s0 + jnp.arange(Q_BLOCK)
        logits = jnp.where(kpos[None, :] <= qpos[:, None], logits, -jnp.inf)
        p = jax.nn.softmax(logits, axis=-1)
        return jnp.einsum('bhqk,bkhd->bqhd', p.astype(v.dtype), v)

    o = lax.map(block, jnp.arange(T // Q_BLOCK))
    return o.transpose(1, 0, 2, 3, 4).reshape(B, T, H * d)


def dsa_attention(q, k, v, qi, ki, wi):
    B, T, H, d = q.shape
    n_keep = min(DSA_TOPK_MAX, T // 4)
    scale = d ** -0.5
    kpos = jnp.arange(T)
    bi = jnp.arange(B)[:, None, None]

    def chunk(i):
        s0 = i * DSA_Q_CHUNK
        qc = lax.dynamic_slice_in_dim(q, s0, DSA_Q_CHUNK, axis=1)
        qic = lax.dynamic_slice_in_dim(qi, s0, DSA_Q_CHUNK, axis=1)
        wic = lax.dynamic_slice_in_dim(wi, s0, DSA_Q_CHUNK, axis=1).astype(F32) * N_IDX_HEADS ** -0.5
        qpos = s0 + jnp.arange(DSA_Q_CHUNK)
        dots = jnp.einsum('bqhd,bkd->bqhk', qic, ki, preferred_element_type=F32) * IDX_DIM ** -0.5
        score = jnp.einsum('bqh,bqhk->bqk', wic, jax.nn.relu(dots))
        score = jnp.where(kpos[None, :] <= qpos[:, None], score, -jnp.inf)
        sval, sidx = lax.top_k(score, n_keep)
        ks = k[bi, sidx]
        vs = v[bi, sidx]
        logits = jnp.einsum('bqhd,bqnhd->bhqn', qc, ks, preferred_element_type=F32) * scale
        logits = jnp.where((sval > -jnp.inf)[:, None], logits, -jnp.inf)
        p = jax.nn.softmax(logits, axis=-1)
        return jnp.einsum('bhqn,bqnhd->bqhd', p.astype(v.dtype), vs)

    o = lax.map(chunk, jnp.arange(T // DSA_Q_CHUNK))
    return o.transpose(1, 0, 2, 3, 4).reshape(B, T, H * d)


def token_mixer(x, w_in, b_forget, diff_lambda, diff_subln, w_branch, w_out, lambda_init):
    B, T, D = x.shape
    pos = jnp.arange(T)
    p = split_projection(jnp.einsum('btd,de->bte', x, w_in))

    def heads(a, h):
        return a.reshape(B, T, h, -1)

    o_a = moba_attention(rope(heads(p['moba_q'], N_HEADS_MOBA), pos),
                         rope(heads(p['moba_k'], N_HEADS_MOBA), pos),
                         heads(p['moba_v'], N_HEADS_MOBA))
    dq = rope(p['diff_q'].reshape(B, T, 2 * N_HEADS_DIFF, DIFF_DIM), pos).reshape(B, T, N_HEADS_DIFF, 2, DIFF_DIM)
    dk = rope(p['diff_k'].reshape(B, T, 2 * N_HEADS_DIFF, DIFF_DIM), pos).reshape(B, T, N_HEADS_DIFF, 2, DIFF_DIM)
    dl = diff_lambda.astype(F32)
    lam = jnp.exp(jnp.sum(dl[0] * dl[1])) - jnp.exp(jnp.sum(dl[2] * dl[3])) + lambda_init
    o_b = diff_attention(dq, dk, heads(p['diff_v'], N_HEADS_DIFF), lam, diff_subln, lambda_init)
    logf = jax.nn.log_sigmoid(p['fox_f'].astype(F32) + b_forget.astype(F32))
    o_c = fox_attention(heads(p['fox_q'], N_HEADS_FOX), heads(p['fox_k'], N_HEADS_FOX),
                        heads(p['fox_v'], N_HEADS_FOX), logf)
    qi = rope(p['idx_q'].reshape(B, T, N_IDX_HEADS, IDX_DIM), pos)
    ki = rope(p['idx_k'].reshape(B, T, 1, IDX_DIM), pos)[:, :, 0]
    o_d = dsa_attention(rope(heads(p['dsa_q'], N_HEADS_DSA), pos), rope(heads(p['dsa_k'], N_HEADS_DSA), pos),
                        heads(p['dsa_v'], N_HEADS_DSA), qi, ki, p['idx_w'])
    gates = jax.nn.sigmoid(p['gates']).reshape(B, T, N_BRANCHES, D)
    branches = (o_a, o_b, o_c, o_d)
    merged = gates[:, :, 0] * jnp.einsum('btc,cd->btd', branches[0], w_branch[0])
    for i in range(1, N_BRANCHES):
        merged = merged + gates[:, :, i] * jnp.einsum('btc,cd->btd', branches[i], w_branch[i])
    return jnp.einsum('btd,de->bte', merged, w_out)


def swiglu(x, w_gate, w_up, w_down):
    return (jax.nn.silu(x @ w_gate) * (x @ w_up)) @ w_down


def routed_experts(xf, eidx, wsel, w_gate, w_up, w_down):
    N, D = xf.shape
    E = w_gate.shape[0]
    A = N * TOP_K
    flat_e = eidx.reshape(A)
    flat_tok = jnp.arange(A) // TOP_K
    flat_w = wsel.reshape(A)
    order = jnp.argsort(flat_e)
    se, stok, sw = flat_e[order], flat_tok[order], flat_w[order]
    counts = jnp.bincount(flat_e, length=E)
    start = jnp.cumsum(counts) - counts
    padded = (counts + MOE_BLOCK - 1) // MOE_BLOCK * MOE_BLOCK
    pend = jnp.cumsum(padded)
    pstart = pend - padded
    dest = pstart[se] + (jnp.arange(A) - start[se])
    n_blocks = (A + E * (MOE_BLOCK - 1)) // MOE_BLOCK + 1
    P = n_blocks * MOE_BLOCK
    row_tok = jnp.full((P,), N, jnp.int32).at[dest].set(stok.astype(jnp.int32))
    row_w = jnp.zeros((P,), F32).at[dest].set(sw)
    block_e = jnp.minimum(jnp.searchsorted(pend, jnp.arange(n_blocks) * MOE_BLOCK, side='right'), E - 1)
    x_pad = jnp.concatenate([xf, jnp.zeros((1, D), xf.dtype)], axis=0)

    def body(b, acc):
        e = block_e[b]
        tok = lax.dynamic_slice_in_dim(row_tok, b * MOE_BLOCK, MOE_BLOCK)
        wr = lax.dynamic_slice_in_dim(row_w, b * MOE_BLOCK, MOE_BLOCK).astype(xf.dtype)
        y = swiglu(x_pad[tok], w_gate[e], w_up[e], w_down[e])
        return acc.at[tok].add(y * wr[:, None])

    acc = lax.fori_loop(0, n_blocks, body, jnp.zeros((N + 1, D), xf.dtype))
    return acc[:N]


def moe_ffn(x, w_router, router_bias, w_exp_gate, w_exp_up, w_exp_down, w_sh_gate, w_sh_up, w_sh_down):
    B, T, D = x.shape
    N = B * T
    xf = x.reshape(N, D)
    scores = jax.nn.sigmoid(jnp.einsum('nd,de->ne', xf, w_router, preferred_element_type=F32))
    biased = scores + router_bias.astype(F32)
    per_group = N_EXPERTS // N_GROUPS
    group_score = jnp.sum(lax.top_k(biased.reshape(N, N_GROUPS, per_group), 2)[0], axis=-1)
    _, gsel = lax.top_k(group_score, TOPK_GROUPS)
    gmask = jnp.any(gsel[..., None] == jnp.arange(N_GROUPS), axis=1)
    masked = jnp.where(jnp.repeat(gmask, per_group, axis=1), biased, -jnp.inf)
    _, eidx = lax.top_k(masked, TOP_K)
    wsel = jnp.take_along_axis(scores, eidx, axis=-1)
    wsel = wsel / jnp.sum(wsel, axis=-1, keepdims=True) * ROUTED_SCALE
    routed = routed_experts(xf, eidx, wsel, w_exp_gate, w_exp_up, w_exp_down)
    shared = swiglu(xf, w_sh_gate, w_sh_up, w_sh_down)
    return (routed + shared).reshape(B, T, D)


def setup_inputs(seed: int = 0) -> dict:
    key = jax.random.key(seed)
    ks = jax.random.split(key, 20)
    D, E, F, FS = D_MODEL, N_EXPERTS, EXPERT_DIM, SHARED_DIM
    x = jax.random.normal(ks[0], (BATCH, SEQ, D), F32)
    seg_keys = jax.random.split(ks[1], len(IN_SEGMENTS))
    cols = []
    for (name, width), k in zip(IN_SEGMENTS, seg_keys):
        s = D ** -0.5 * (DEEPNORM_BETA if name.endswith('_v') else 1.0)
        cols.append(_normal(k, (DEPTH, D, width), s))
    w_in = jnp.concatenate(cols, axis=-1)
    b_forget = 2.0 + _normal(ks[2], (DEPTH, N_HEADS_FOX), 0.5)
    diff_lambda = _normal(ks[3], (DEPTH, 4, DIFF_DIM), 0.1)
    diff_subln = 1.0 + _normal(ks[4], (DEPTH, 2 * DIFF_DIM), 0.02)
    w_branch = _normal(ks[5], (DEPTH, N_BRANCHES, BRANCH_WIDTH, D), BRANCH_WIDTH ** -0.5)
    w_out = _normal(ks[6], (DEPTH, D, D), D ** -0.5 * DEEPNORM_BETA)
    ln1_g = 1.0 + _normal(ks[7], (DEPTH, D), 0.02)
    ln1_b = _normal(ks[8], (DEPTH, D), 0.02)
    w_router = _normal(ks[9], (DEPTH, D, E), D ** -0.5)
    router_bias = _normal(ks[10], (DEPTH, E), 0.01)
    w_exp_gate = _normal(ks[11], (DEPTH, E, D, F), D ** -0.5 * DEEPNORM_BETA)
    w_exp_up = _normal(ks[12], (DEPTH, E, D, F), D ** -0.5 * DEEPNORM_BETA)
    w_exp_down = _normal(ks[13], (DEPTH, E, F, D), F ** -0.5 * DEEPNORM_BETA)
    w_sh_gate = _normal(ks[14], (DEPTH, D, FS), D ** -0.5 * DEEPNORM_BETA)
    w_sh_up = _normal(ks[15], (DEPTH, D, FS), D ** -0.5 * DEEPNORM_BETA)
    w_sh_down = _normal(ks[16], (DEPTH, FS, D), FS ** -0.5 * DEEPNORM_BETA)
    ln2_g = 1.0 + _normal(ks[17], (DEPTH, D), 0.02)
    ln2_b = _normal(ks[18], (DEPTH, D), 0.02)
    return {'x': x, 'w_in': w_in, 'b_forget': b_forget, 'diff_lambda': diff_lambda,
            'diff_subln': diff_subln, 'w_branch': w_branch, 'w_out': w_out,
            'ln1_g': ln1_g, 'ln1_b': ln1_b, 'w_router': w_router, 'router_bias': router_bias,
            'w_exp_gate': w_exp_gate, 'w_exp_up': w_exp_up, 'w_exp_down': w_exp_down,
            'w_sh_gate': w_sh_gate, 'w_sh_up': w_sh_up, 'w_sh_down': w_sh_down,
            'ln2_g': ln2_g, 'ln2_b': ln2_b}


def reference(x, w_in, b_forget, diff_lambda, diff_subln, w_branch, w_out, ln1_g, ln1_b,
              w_router, router_bias, w_exp_gate, w_exp_up, w_exp_down,
              w_sh_gate, w_sh_up, w_sh_down, ln2_g, ln2_b):
    for l in range(DEPTH):
        lambda_init = 0.8 - 0.6 * math.exp(-0.3 * l)
        h = token_mixer(x, w_in[l], b_forget[l], diff_lambda[l], diff_subln[l],
                        w_branch[l], w_out[l], lambda_init)
        x = layer_norm(DEEPNORM_ALPHA * x + h, ln1_g[l], ln1_b[l])
        h = moe_ffn(x, w_router[l], router_bias[l], w_exp_gate[l], w_exp_up[l], w_exp_down[l],
                    w_sh_gate[l], w_sh_up[l], w_sh_down[l])
        x = layer_norm(DEEPNORM_ALPHA * x + h, ln2_g[l], ln2_b[l])
    return x
```

```python
import os
import numpy as np
from contextlib import ExitStack
import concourse.bass as bass
import concourse.mybir as mybir
from concourse.bass_utils import run_bass_kernel_spmd

F32 = mybir.dt.float32
BF16 = mybir.dt.bfloat16
I32 = mybir.dt.int32
U32 = mybir.dt.uint32
AF = mybir.ActivationFunctionType
ALU = mybir.AluOpType
AX = mybir.AxisListType

T = 2048
D = 1024
NEG = -30000.0
ALPHA = 4 ** 0.25
LN_EPS = 1e-5
NBIS = 18
CAP = 384
NE = 256

OFF = {}
_o = 0
for _n, _w in (('moba_q', 256), ('moba_k', 256), ('moba_v', 256), ('diff_q', 256), ('diff_k', 256), ('diff_v', 256),
               ('fox_q', 256), ('fox_k', 256), ('fox_v', 256), ('fox_f', 4), ('dsa_q', 256), ('dsa_k', 256), ('dsa_v', 256),
               ('idx_q', 256), ('idx_k', 64), ('idx_w', 4), ('gates', 4096)):
    OFF[_n] = _o
    _o += _w
WIN = _o

CC = {}
_o = 0
for _n, _w in (('C64', T), ('S64', T), ('C32', T), ('S32', T), ('R64', 128), ('R32', 128), ('ident', 128),
               ('tribias', 128), ('tribig', 128), ('esel', 1024), ('pow2', 32), ('ones', 128), ('eh4', 512), ('mask01', 2), ('lstrict', 128), ('ebase', 256)):
    CC[_n] = (_o, _o + _w)
    _o += _w
NCST = _o

PA = {'ln1_g': (0, 8), 'ln1_b': (8, 16), 'negb': (16, 17), 'dl': (20, 148), 'subln': (148, 149)}
NPA = 160
PB = {'rbias': (0, 256), 'ln2_g': (256, 1280), 'ln2_b': (1280, 2304)}
NPB = 2304


def make_consts():
    c = np.zeros((128, NCST), np.float32)
    pos = np.arange(T, dtype=np.float32)
    p = np.arange(128)
    inv64 = (10000.0 ** (-np.arange(32, dtype=np.float32) / 32)).astype(np.float32)
    inv32 = (10000.0 ** (-np.arange(16, dtype=np.float32) / 16)).astype(np.float32)
    a64 = (pos[None, :] * inv64[p % 32][:, None]).astype(np.float32)
    a32 = (pos[None, :] * inv32[p % 16][:, None]).astype(np.float32)
    c[:, CC['C64'][0]:CC['C64'][1]] = np.cos(a64.astype(np.float64))
    c[:, CC['S64'][0]:CC['S64'][1]] = np.sin(a64.astype(np.float64))
    c[:, CC['C32'][0]:CC['C32'][1]] = np.cos(a32.astype(np.float64))
    c[:, CC['S32'][0]:CC['S32'][1]] = np.sin(a32.astype(np.float64))
    R64 = np.zeros((128, 128), np.float32)
    R32 = np.zeros((128, 128), np.float32)
    for m in range(128):
        if m % 64 < 32:
            R64[m + 32, m] = -1.0
        else:
            R64[m - 32, m] = 1.0
        if m % 32 < 16:
            R32[m + 16, m] = -1.0
        else:
            R32[m - 16, m] = 1.0
    c[:, CC['R64'][0]:CC['R64'][1]] = R64
    c[:, CC['R32'][0]:CC['R32'][1]] = R32
    c[:, CC['ident'][0]:CC['ident'][1]] = np.eye(128, dtype=np.float32)
    r = np.arange(128)[:, None]
    q = np.arange(128)[None, :]
    c[:, CC['tribias'][0]:CC['tribias'][1]] = np.where(q < r, NEG, 0.0)
    c[:, CC['tribig'][0]:CC['tribig'][1]] = np.where(q > r, -1e30, 0.0)
    es = np.zeros((128, 8, 128), np.float32)
    for n in range(8):
        es[n, n, :] = 1.0
    c[:, CC['esel'][0]:CC['esel'][1]] = es.reshape(128, 1024)
    c[:, CC['pow2'][0]:CC['pow2'][1]] = (0.5 ** (np.arange(32) + 1))[None, :]
    c[:, CC['ones'][0]:CC['ones'][1]] = 1.0
    e4 = np.zeros((128, 4, 128), np.float32)
    for n in range(4):
        e4[n, n, :] = 1.0
    c[:, CC['eh4'][0]:CC['eh4'][1]] = e4.reshape(128, 512)
    c[:, CC['mask01'][0]] = (np.arange(128) % 64 < 32)
    c[:, CC['lstrict'][0]:CC['lstrict'][1]] = (np.arange(128)[:, None] < np.arange(128)[None, :])
    c[:, CC['ebase'][0]:CC['ebase'][1]] = (np.arange(256) * CAP + 1)[None, :]
    c[:, CC['mask01'][0] + 1] = (np.arange(128) % 64 >= 32)
    return c


class Buf:
    __slots__ = ('t', 'w', 'r', 'name', 'x')

    def __init__(self, t, name='', x=False):
        self.t = t
        self.w = None
        self.r = {}
        self.name = name
        self.x = x


class Ring:
    def __init__(self, bufs):
        self.b = bufs
        self.i = 0

    def next(self):
        b = self.b[self.i % len(self.b)]
        self.i += 1
        return b


class MK:
    def __init__(self, nc, es):
        self.nc = nc
        self.E = dict(pe=nc.tensor, act=nc.scalar, dve=nc.vector, pool=nc.gpsimd, sp=nc.sync)
        self.semobj = {}
        self.cnt = {}
        for k in self.E:
            self.semobj[k] = es.enter_context(nc.semaphore('s_' + k))
            self.cnt[k] = 0
        self.seen = {}
        self.dq = {}
        for q, n in (('sp', 28), ('pool', 28), ('act', 8)):
            keys = []
            for i in range(n):
                key = ('d', q, i)
                self.semobj[key] = es.enter_context(nc.semaphore('d_%s%d' % (q, i)))
                self.cnt[key] = 0
                keys.append(key)
            self.dq[q] = [keys, 0]
        self.ninst = 0

    def _wait(self, eng, key, val):
        if self.seen.get((eng, key), 0) >= val:
            return
        self.E[eng].wait_ge(self.semobj[key], val)
        self.seen[(eng, key)] = val

    def _deps(self, eng, reads, writes):
        need = {}
        for b in reads:
            if b.w is not None:
                k, v = b.w
                need[k] = max(need.get(k, 0), v)
            if b.x:
                for k, v in b.r.items():
                    if k != eng:
                        need[k] = max(need.get(k, 0), v)
        for b in writes:
            if b.w is not None:
                k, v = b.w
                if k != eng:
                    need[k] = max(need.get(k, 0), v)
            for k, v in b.r.items():
                if k != eng:
                    need[k] = max(need.get(k, 0), v)
        if eng == 'pe':
            need.pop('pe', None)
        for k, v in need.items():
            self._wait(eng, k, v)

    def _commit(self, tok, reads, writes):
        k, v = tok
        for b in reads:
            b.r[k] = max(b.r.get(k, 0), v)
        for b in writes:
            b.w = tok
            b.r = {}

    def op(self, eng, fn, reads=(), writes=()):
        self._deps(eng, reads, writes)
        ins = fn(self.E[eng])
        self.cnt[eng] += 1
        ins.then_inc(self.semobj[eng], 1)
        tok = (eng, self.cnt[eng])
        self._commit(tok, reads, writes)
        self.ninst += 1
        return tok

    def dma(self, q, fn, reads=(), writes=()):
        self._deps(q, reads, writes)
        keys, i = self.dq[q]
        key = keys[i % len(keys)]
        self.dq[q][1] = i + 1
        if self.cnt[key] > 0:
            self._wait(q, key, self.cnt[key])
        ins = fn(self.E[q])
        self.cnt[key] += 16
        ins.then_inc(self.semobj[key], 16)
        tok = (key, self.cnt[key])
        self._commit(tok, reads, writes)
        self.ninst += 1
        return tok

    def barrier(self, bufs=()):
        for eng in self.E:
            for k, v in self.cnt.items():
                if v > 0:
                    self._wait(eng, k, v)

    def wait_all_on(self, eng):
        for k, v in self.cnt.items():
            if v > 0:
                self._wait(eng, k, v)


class Ctx:
    pass


def build(cfg):
    NS = cfg.get('nseq', 4)
    NL = cfg.get('nlayer', 2)
    dbg = cfg.get('dbg', None)
    NT = NS * T
    nc = bass.Bass("TRN2", target_bir_lowering=False)
    K = Ctx()
    K.nc = nc
    K.cfg = cfg
    K.NS, K.NL, K.NT = NS, NL, NT
    dr = {}

    def dram(name, shape, dt, kind):
        dr[name] = Buf(nc.dram_tensor(name, shape, dt, kind=kind).ap(), name)
        return dr[name]

    dram('xT', [D, NT], F32, "ExternalInput")
    dram('w_in', [2, D, WIN], F32, "ExternalInput")
    dram('w_branch', [2, 4, 256, D], F32, "ExternalInput")
    dram('w_out', [2, D, D], F32, "ExternalInput")
    dram('w_router', [2, D, NE], F32, "ExternalInput")
    if cfg.get('stageB', True):
        NEX = cfg.get('ne', NE)
        dram('w_exp_gate', [2, NEX, D, 256], F32, "ExternalInput")
        dram('w_exp_up', [2, NEX, D, 256], F32, "ExternalInput")
        dram('w_exp_down', [2, NEX, 256, D], F32, "ExternalInput")
    dram('w_sh_gate', [2, D, 256], F32, "ExternalInput")
    dram('w_sh_up', [2, D, 256], F32, "ExternalInput")
    dram('w_sh_down', [2, 256, D], F32, "ExternalInput")
    dram('cst', [128, NCST], F32, "ExternalInput")
    dram('parA', [2, 128, NPA], F32, "ExternalInput")
    dram('parB', [2, 128, NPB], F32, "ExternalInput")
    dram('out', [NT, D], F32, "ExternalOutput")
    dram('x1T', [D, NT], F32, "Internal")
    dram('x1r', [NT, D], F32, "ExternalOutput" if dbg else "Internal")
    dram('x1rb', [NT + 128, D], BF16, "Internal")
    dram('x2T', [D, NT], F32, "Internal")
    dram('tokw', [NE * CAP, 2], I32, "Internal")
    dram('ye', [NE * CAP, D], BF16, "Internal")
    if dbg:
        for name, shape in dbg.items():
            dram(name, shape, F32, "ExternalOutput")
    K.dr = dr

    with ExitStack() as es:
        mk = MK(nc, es)
        K.mk = mk
        K.es = es

        K.uid = 0

        def sb(name, shape, dt, st=None):
            K.uid += 1
            nm = "%s_%d" % (name, K.uid)
            return Buf((st or es).enter_context(nc.sbuf_tensor(nm, shape, dt)), nm)

        K.sb = sb
        K.dumps = {}

        def dump(name, buf, ap, shape):
            if not cfg.get('dbg'):
                return
            if name in K.dumps:
                return
            d_ = Buf(nc.dram_tensor(name, shape, F32, kind="ExternalOutput").ap(), name)
            K.dumps[name] = d_
            mk.dma('pool', lambda e: e.dma_start(out=d_.t, in_=ap), reads=[buf], writes=[d_])

        K.dump = dump
        K.pst = [es.enter_context(nc.psum_tensor("ps%d" % i, [128, 1024], F32)) for i in range(4)]
        K.ps = []
        for i in range(4):
            for h in range(2):
                K.ps.append(Buf(K.pst[i][:, h * 512:(h + 1) * 512], "ps%d_%d" % (i, h), x=True))
        setup_consts(K)
        for l in range(NL):
            src = dr['xT'] if l == 0 else dr['x2T']
            if cfg.get('stageA', True):
                for s in range(NS):
                    stage_a(K, l, s, src)
            if cfg.get('stageB', True):
                stage_b(K, l, last=(l == NL - 1))
        mk.wait_all_on('sp')
        mk.wait_all_on('pool')
    K.ninst = mk.ninst
    return nc, K


def setup_consts(K):
    mk, sb, dr = K.mk, K.sb, K.dr
    cst = dr['cst']
    K.c = {}
    big = ('C64', 'S64', 'C32', 'S32')
    small = ('R64', 'R32', 'ident', 'tribias', 'esel', 'ones', 'eh4', 'lstrict')
    for name in big + small:
        a, b = CC[name]
        K.c[name] = sb('c_' + name, [128, b - a], BF16)
    for name in ('ident', 'tribig', 'pow2', 'ones', 'mask01', 'ebase'):
        a, b = CC[name]
        dst = sb('cf_' + name, [128, b - a], F32)
        mk.dma('sp', lambda e: e.dma_start(out=dst.t[:, :], in_=cst.t[:, a:b]), reads=[cst], writes=[dst])
        K.c[name + '_f'] = dst
    with ExitStack() as st:
        tmp = sb('cst_tmp', [128, T], F32, st)
        for name in big + small:
            a, b = CC[name]
            w = b - a
            dst = K.c[name]
            mk.dma('sp', lambda e: e.dma_start(out=tmp.t[:, 0:w], in_=cst.t[:, a:b]), reads=[cst], writes=[tmp])
            mk.op('act', lambda e: e.activation(out=dst.t[:, :], in_=tmp.t[:, 0:w], func=AF.Copy), reads=[tmp], writes=[dst])
        mk.barrier()


def stage_a(K, l, s, src):
    mk, sb, dr, nc = K.mk, K.sb, K.dr, K.nc
    ps = K.ps
    c = K.c
    w_in = dr['w_in']
    w_in_v = w_in.t[l].rearrange("(c p) n -> p c n", p=128)
    t0g = s * T
    lam_init = 0.8 - 0.6 * float(np.exp(-0.3 * l))
    dbg = K.cfg.get('dbg') or {}
    branches = K.cfg.get('branches', (0, 1, 2, 3))

    def loadw(wb, col0, ncol, dcol=0):
        mk.dma('pool', lambda e: e.dma_start(out=wb.t[:, :, dcol:dcol + ncol], in_=w_in_v[:, :, col0:col0 + ncol]),
               reads=[w_in], writes=[wb])

    with ExitStack() as sa:
        xb = sb('xb', [128, 8, T], BF16, sa)
        par = sb('parA', [128, NPA], F32, sa)
        oT = [sb('oT%d' % i, [128, 2, T], BF16, sa) for i in range(4)]
        mk.dma('sp', lambda e: e.dma_start(out=par.t[:, :], in_=dr['parA'].t[l]), reads=[dr['parA']], writes=[par])
        srcv = src.t.rearrange("(c p) n -> p c n", p=128)
        for cc in range(8):
            mk.dma('pool', lambda e: e.dma_start(out=xb.t[:, cc, :], in_=srcv[:, cc, t0g:t0g + T]), reads=[src], writes=[xb])

        with ExitStack() as sbr:
            wq = Ring([sb('wq%d' % i, [128, 8, 256], BF16, sbr) for i in range(2)])
            qT = sb('qT', [128, 2, T], BF16, sbr)
            kT = sb('kT', [128, 2, T], BF16, sbr)
            Vp = sb('Vp', [128, 16, 4, 128], BF16, sbr)
            ptr = Ring([sb('pt%d' % i, [128, 512], BF16, sbr) for i in range(3)])
            tq = Ring([sb('tq%d' % i, [128, 512], BF16, sbr) for i in range(2)])
            t1r = Ring([sb('t1_%d' % i, [128, 512], F32, sbr) for i in range(2)])
            t2r = Ring([sb('t2_%d' % i, [128, 512], F32, sbr) for i in range(2)])
            rsr = Ring([sb('rs%d' % i, [64, 512], F32, sbr) for i in range(2)])
            small = sb('small', [128, 64], F32, sbr)
            pp = Ring([ps[0], ps[1], ps[2], ps[3]])
            sr = Ring([ps[4], ps[5]])
            accr = Ring([ps[6], ps[7]])
            mk.op('pool', lambda e: e.memset(Vp.t[:, :, :, 64:128], 1.0), writes=[Vp])

            def proj_fm(col0, nchunk, dst, rope=None, wb=None, wcol0=0):
                if wb is None:
                    wb = wq.next()
                    loadw(wb, col0, nchunk * 128)
                for m in range(nchunk):
                    for tc in range(4):
                        pb = pp.next()
                        for cc in range(8):
                            mk.op('pe', lambda e: e.matmul(pb.t[:, :], lhsT=wb.t[:, cc, wcol0 + m * 128:wcol0 + (m + 1) * 128],
                                                           rhs=xb.t[:, cc, tc * 512:(tc + 1) * 512], start=(cc == 0), stop=(cc == 7)),
                                  reads=[wb, xb], writes=[pb])
                        dsl = dst.t[:, m, tc * 512:(tc + 1) * 512]
                        if rope is None:
                            mk.op('act', lambda e: e.activation(out=dsl, in_=pb.t[:, :], func=AF.Copy), reads=[pb], writes=[dst])
                        else:
                            Cb, Sb, Rb = rope
                            qs = tq.next()
                            mk.op('act', lambda e: e.activation(out=qs.t[:, :], in_=pb.t[:, :], func=AF.Copy), reads=[pb], writes=[qs])
                            rp = pp.next()
                            mk.op('pe', lambda e: e.matmul(rp.t[:, :], lhsT=Rb.t[:, :], rhs=qs.t[:, :], start=True, stop=True),
                                  reads=[Rb, qs], writes=[rp])
                            t1 = t1r.next()
                            t2 = t2r.next()
                            mk.op('dve', lambda e: e.tensor_tensor(out=t1.t[:, :], in0=rp.t[:, :], in1=Sb.t[:, tc * 512:(tc + 1) * 512], op=ALU.mult),
                                  reads=[rp, Sb], writes=[t1])
                            mk.op('pool', lambda e: e.tensor_tensor(out=t2.t[:, :], in0=qs.t[:, :], in1=Cb.t[:, tc * 512:(tc + 1) * 512], op=ALU.mult),
                                  reads=[qs, Cb], writes=[t2])
                            mk.op('dve', lambda e: e.tensor_tensor(out=dsl, in0=t1.t[:, :], in1=t2.t[:, :], op=ALU.add),
                                  reads=[t1, t2], writes=[dst])

            def proj_v(col0):
                wb = wq.next()
                loadw(wb, col0, 256)
                for tt in range(16):
                    pb = pp.next()
                    for cc in range(8):
                        mk.op('pe', lambda e: e.matmul(pb.t[:, 0:256], lhsT=xb.t[:, cc, tt * 128:(tt + 1) * 128], rhs=wb.t[:, cc, 0:256],
                                                       start=(cc == 0), stop=(cc == 7)), reads=[wb, xb], writes=[pb])
                    mk.op('act', lambda e: e.activation(out=Vp.t[:, tt, :, 0:64], in_=pb.t[:, 0:256].rearrange("p (h d) -> p h d", h=4), func=AF.Copy),
                          reads=[pb], writes=[Vp])

            def attention(nsm, kdim, qk_ops, bias_fn, act_bias_fn, vhead_fn, fin_fn, scale, cq_outer=False, kbuf_fn=None):
                steps = []
                if cq_outer:
                    for cq in range(4):
                        for s_ in range(nsm):
                            for j in range(4 * cq + 4):
                                steps.append((s_, cq, j))
                else:
                    for s_ in range(nsm):
                        for cq in range(4):
                            for j in range(4 * cq + 4):
                                steps.append((s_, cq, j))
                pend = None
                accs = {}
                for (s_, cq, j) in steps:
                    ch, base = qk_ops(s_)
                    q0 = max(512 * cq, 128 * j)
                    N = 512 * (cq + 1) - q0
                    off = q0 - 512 * cq
                    sbk = sr.next()
                    kb_ = kT if kbuf_fn is None else kbuf_fn(s_)
                    mk.op('pe', lambda e: e.matmul(sbk.t[:, 0:N], lhsT=kb_.t[base:base + kdim, ch, j * 128:(j + 1) * 128],
                                                   rhs=qT.t[base:base + kdim, ch, q0:q0 + N], start=True, stop=False, skip_group_check=True),
                          reads=[kb_, qT], writes=[sbk])
                    bias_fn(s_, cq, j, q0, N, sbk)
                    if pend is not None:
                        pend()
                    if j == 0:
                        accs[(s_, cq)] = accr.next()
                    acc = accs[(s_, cq)]
                    pt = ptr.next()
                    ab = act_bias_fn(s_, j)

                    def do_exp_pv(s_=s_, cq=cq, j=j, N=N, off=off, sbk=sbk, pt=pt, acc=acc, ab=ab):
                        if ab is None:
                            mk.op('act', lambda e: e.activation(out=pt.t[:, 0:N], in_=sbk.t[:, 0:N], func=AF.Exp, scale=scale),
                                  reads=[sbk], writes=[pt])
                        else:
                            abuf, aap = ab
                            mk.op('act', lambda e: e.activation(out=pt.t[:, 0:N], in_=sbk.t[:, 0:N], func=AF.Exp, scale=scale, bias=aap),
                                  reads=[sbk, abuf], writes=[pt])
                        h = vhead_fn(s_)
                        mk.op('pe', lambda e: e.matmul(acc.t[:, off:off + N], lhsT=Vp.t[:, j, h, :], rhs=pt.t[:, 0:N],
                                                       start=(j == 0), stop=(j == 4 * cq + 3), skip_group_check=True),
                              reads=[Vp, pt], writes=[acc])
                        if j == 4 * cq + 3:
                            fin_fn(s_, cq, acc)
                    pend = do_exp_pv
                if pend is not None:
                    pend()

            def causal_bias(sbk, j, cq, q0):
                if j >= 4 * cq:
                    mk.op('pe', lambda e: e.matmul(sbk.t[:, 0:128], lhsT=c['ident'].t[:, :], rhs=c['tribias'].t[:, :],
                                                   start=False, stop=True, skip_group_check=True),
                          reads=[c['ident'], c['tribias']], writes=[sbk])

            def fin_std(dst):
                def f(s_, cq, acc):
                    rs = rsr.next()
                    mk.op('act', lambda e: e.activation(out=rs.t[0:64, :], in_=acc.t[64:128, :], func=AF.Copy), reads=[acc], writes=[rs])
                    mk.op('dve', lambda e: e.reciprocal(out=rs.t[0:64, :], in_=rs.t[0:64, :]), reads=[rs], writes=[rs])
                    pb = (s_ % 2) * 64
                    mk.op('dve', lambda e: e.tensor_tensor(out=dst.t[pb:pb + 64, s_ // 2, cq * 512:(cq + 1) * 512], in0=acc.t[0:64, :],
                                                           in1=rs.t[0:64, :], op=ALU.mult), reads=[acc, rs], writes=[dst])
                return f

            if 0 in branches:
                rope64 = (c['C64'], c['S64'], c['R64'])
                proj_fm(OFF['moba_q'], 2, qT, rope64)
                proj_fm(OFF['moba_k'], 2, kT, rope64)
                proj_v(OFF['moba_v'])
                with ExitStack() as sm:
                    kmf = sb('kmf', [128, 2, 8], F32, sm)
                    kmb = sb('kmb', [128, 2, 8], BF16, sm)
                    biasT = sb('biasT', [8, 4, 1024], BF16, sm)
                    gm = sb('gm', [128, 8], F32, sm)
                    t8 = sb('t8', [128, 8], F32, sm)
                    bq = sb('bq', [128, 8], F32, sm)
                    mk.op('dve', lambda e: e.tensor_reduce(out=kmf.t[:, :, :], in_=kT.t[:, :, :].rearrange("p c (n k) -> p c n k", k=256),
                                                           axis=AX.X, op=ALU.add), reads=[kT], writes=[kmf])
                    mk.op('dve', lambda e: e.tensor_scalar(out=kmb.t[:, :, :], in0=kmf.t[:, :, :], scalar1=1.0 / 256, scalar2=None, op0=ALU.mult),
                          reads=[kmf], writes=[kmb])
                    for h in range(4):
                        ch, base = h // 2, (h % 2) * 64
                        for qt in range(8, 16):
                            cur = qt // 2
                            gp = pp.next()
                            mk.op('pe', lambda e: e.matmul(gp.t[:, 0:8], lhsT=qT.t[base:base + 64, ch, qt * 128:(qt + 1) * 128],
                                                           rhs=kmb.t[base:base + 64, ch, :], start=True, stop=True), reads=[qT, kmb], writes=[gp])
                            mk.op('dve', lambda e: e.memset(gm.t[:, :], -1e30), writes=[gm])
                            mk.op('dve', lambda e: e.tensor_copy(out=gm.t[:, 0:cur], in_=gp.t[:, 0:cur]), reads=[gp], writes=[gm])
                            mk.op('dve', lambda e: e.max(out=t8.t[:, :], in_=gm.t[:, :]), reads=[gm], writes=[t8])
                            mk.op('dve', lambda e: e.tensor_scalar(out=bq.t[:, :], in0=gm.t[:, :], scalar1=t8.t[:, 2:3], scalar2=NEG,
                                                                   op0=ALU.is_lt, op1=ALU.mult), reads=[gm, t8], writes=[bq])
                            mk.op('dve', lambda e: e.memset(bq.t[:, cur:8], 0.0), writes=[bq])
                            tp = pp.next()
                            mk.op('pe', lambda e: e.transpose(out=tp.t[0:8, 0:128], in_=bq.t[:, :], identity=c['ident_f'].t[:, :]),
                                  reads=[bq, c['ident_f']], writes=[tp])
                            mk.op('act', lambda e: e.activation(out=biasT.t[0:8, h, (qt - 8) * 128:(qt - 7) * 128], in_=tp.t[0:8, 0:128], func=AF.Copy),
                                  reads=[tp], writes=[biasT])

                    K.dump('d_biasT', biasT, biasT.t[0:8, :, :], [8, 4, 1024])
                    K.dump('d_kmb', kmb, kmb.t[:, :, :], [128, 2, 8])

                    def moba_bias(s_, cq, j, q0, N, sbk):
                        if cq >= 2:
                            n = j // 2
                            mk.op('pe', lambda e: e.matmul(sbk.t[:, 0:N], lhsT=c['esel'].t[0:8, n * 128:(n + 1) * 128],
                                                           rhs=biasT.t[0:8, s_, q0 - 1024:q0 - 1024 + N], start=False, stop=(j < 4 * cq),
                                                           skip_group_check=True), reads=[c['esel'], biasT], writes=[sbk])
                        causal_bias(sbk, j, cq, q0)

                    attention(4, 64, lambda s_: (s_ // 2, (s_ % 2) * 64), moba_bias, lambda s_, j: None, lambda s_: s_, fin_std(oT[0]), 0.125)

            if 1 in branches:
                rope32 = (c['C32'], c['S32'], c['R32'])
                proj_fm(OFF['diff_q'], 2, qT, rope32)
                proj_fm(OFF['diff_k'], 2, kT, rope32)
                proj_v(OFF['diff_v'])
                with ExitStack() as sm:
                    lamt = sb('lamt', [128, 8], F32, sm)
                    dtmp = sb('dtmp', [128, 128], F32, sm)
                    hold = sb('hold', [64, 512], F32, sm)
                    dd = sb('dd', [64, 512], F32, sm)
                    sq = sb('sq', [64, 512], F32, sm)
                    gs = sb('gs', [64, 1], F32, sm)
                    kT1 = sb('kT1', [128, 2, T], BF16, sm)
                    mk.op('pool', lambda e: e.tensor_scalar(out=kT1.t[:, :, :], in0=kT.t[:, :, :], scalar1=c['mask01_f'].t[:, 1:2], scalar2=None, op0=ALU.mult),
                          reads=[kT, c['mask01_f']], writes=[kT1])
                    mk.op('dve', lambda e: e.tensor_scalar(out=kT.t[:, :, :], in0=kT.t[:, :, :], scalar1=c['mask01_f'].t[:, 0:1], scalar2=None, op0=ALU.mult),
                          reads=[kT, c['mask01_f']], writes=[kT])
                    a, b = PA['dl']
                    mk.op('dve', lambda e: e.tensor_tensor(out=dtmp.t[:, 0:32], in0=par.t[:, a:a + 32], in1=par.t[:, a + 32:a + 64], op=ALU.mult),
                          reads=[par], writes=[dtmp])
                    mk.op('dve', lambda e: e.tensor_tensor(out=dtmp.t[:, 32:64], in0=par.t[:, a + 64:a + 96], in1=par.t[:, a + 96:a + 128], op=ALU.mult),
                          reads=[par], writes=[dtmp])
                    mk.op('dve', lambda e: e.tensor_reduce(out=lamt.t[:, 0:2], in_=dtmp.t[:, 0:64].rearrange("p (a b) -> p a b", a=2), axis=AX.X, op=ALU.add),
                          reads=[dtmp], writes=[lamt])
                    mk.op('act', lambda e: e.activation(out=lamt.t[:, 2:4], in_=lamt.t[:, 0:2], func=AF.Exp), reads=[lamt], writes=[lamt])
                    mk.op('dve', lambda e: e.tensor_tensor(out=lamt.t[:, 4:5], in0=lamt.t[:, 3:4], in1=lamt.t[:, 2:3], op=ALU.subtract),
                          reads=[lamt], writes=[lamt])
                    mk.op('dve', lambda e: e.tensor_scalar(out=lamt.t[:, 5:6], in0=lamt.t[:, 4:5], scalar1=-lam_init, scalar2=None, op0=ALU.add),
                          reads=[lamt], writes=[lamt])
                    sa_, sb_ = PA['subln']
                    mk.op('dve', lambda e: e.tensor_scalar(out=gs.t[:, :], in0=par.t[0:64, sa_:sb_], scalar1=(1.0 - lam_init), scalar2=None, op0=ALU.mult),
                          reads=[par], writes=[gs])

                    K.dump('d_lamt', lamt, lamt.t[:, :], [128, 8])
                    K.dump('d_gs', gs, gs.t[:, :], [64, 1])

                    def fin_diff(s_, cq, acc):
                        h, comp = s_ // 2, s_ % 2
                        rs = rsr.next()
                        mk.op('act', lambda e: e.activation(out=rs.t[0:64, :], in_=acc.t[64:128, :], func=AF.Copy), reads=[acc], writes=[rs])
                        mk.op('dve', lambda e: e.reciprocal(out=rs.t[0:64, :], in_=rs.t[0:64, :]), reads=[rs], writes=[rs])
                        if comp == 0:
                            mk.op('dve', lambda e: e.tensor_tensor(out=hold.t[:, :], in0=acc.t[0:64, :], in1=rs.t[0:64, :], op=ALU.mult),
                                  reads=[acc, rs], writes=[hold])
                            return
                        mk.op('dve', lambda e: e.tensor_tensor(out=dd.t[:, :], in0=acc.t[0:64, :], in1=rs.t[0:64, :], op=ALU.mult),
                              reads=[acc, rs], writes=[dd])
                        mk.op('dve', lambda e: e.scalar_tensor_tensor(out=dd.t[:, :], in0=dd.t[:, :], scalar=lamt.t[0:64, 5:6], in1=hold.t[:, :],
                                                                      op0=ALU.mult, op1=ALU.add), reads=[dd, lamt, hold], writes=[dd])
                        K.dump('d_hold', hold, hold.t[:, :], [64, 512])
                        K.dump('d_dd', dd, dd.t[:, :], [64, 512])
                        mk.op('pool', lambda e: e.tensor_tensor(out=sq.t[:, :], in0=dd.t[:, :], in1=dd.t[:, :], op=ALU.mult), reads=[dd], writes=[sq])
                        mp = pp.next()
                        mk.op('pe', lambda e: e.matmul(mp.t[0:64, :], lhsT=c['ones_f'].t[0:64, 0:64], rhs=sq.t[:, :], start=True, stop=True),
                              reads=[c['ones_f'], sq], writes=[mp])
                        mk.op('act', lambda e: e.activation(out=sq.t[:, :], in_=mp.t[0:64, :], func=AF.Ln, scale=1.0 / 64, bias=LN_EPS), reads=[mp], writes=[sq])
                        mk.op('act', lambda e: e.activation(out=sq.t[:, :], in_=sq.t[:, :], func=AF.Exp, scale=-0.5), reads=[sq], writes=[sq])
                        pb = (h % 2) * 64
                        mk.op('dve', lambda e: e.scalar_tensor_tensor(out=oT[1].t[pb:pb + 64, h // 2, cq * 512:(cq + 1) * 512], in0=dd.t[:, :],
                                                                      scalar=gs.t[:, 0:1], in1=sq.t[:, :], op0=ALU.mult, op1=ALU.mult),
                              reads=[dd, gs, sq], writes=[oT[1]])

                    def diff_bias(s_, cq, j, q0, N, sbk):
                        causal_bias(sbk, j, cq, q0)

                    attention(8, 64, lambda s_: (s_ // 4, ((s_ // 2) % 2) * 64), diff_bias, lambda s_, j: None, lambda s_: s_ // 2, fin_diff,
                              float(32 ** -0.5), cq_outer=True, kbuf_fn=lambda s_: (kT if s_ % 2 == 0 else kT1))

            if 2 in branches:
                proj_fm(OFF['fox_q'], 2, qT, None)
                proj_fm(OFF['fox_k'], 2, kT, None)
                proj_v(OFF['fox_v'])
                with ExitStack() as sm:
                    wf = sb('wf', [128, 8, 4], BF16, sm)
                    ncT = sb('ncT', [4, T], F32, sm)
                    e1 = sb('e1', [4, T], F32, sm)
                    cq8 = sb('cq8', [4, T], BF16, sm)
                    nck = sb('nck', [128, 16, 4], F32, sm)
                    loadw(wf, OFF['fox_f'], 4)
                    na, nb = PA['negb']
                    mk.op('dve', lambda e: e.tensor_scalar(out=small.t[0:4, 0:1], in0=par.t[0:4, na:nb], scalar1=-1.0, scalar2=None, op0=ALU.mult),
                          reads=[par], writes=[small])
                    for tc in range(4):
                        fp = pp.next()
                        for cc in range(8):
                            mk.op('pe', lambda e: e.matmul(fp.t[0:4, :], lhsT=wf.t[:, cc, 0:4], rhs=xb.t[:, cc, tc * 512:(tc + 1) * 512],
                                                           start=(cc == 0), stop=(cc == 7)), reads=[wf, xb], writes=[fp])
                        mk.op('act', lambda e: e.activation(out=e1.t[:, tc * 512:(tc + 1) * 512], in_=fp.t[0:4, :], func=AF.Exp, scale=-1.0,
                                                            bias=small.t[0:4, 0:1]), reads=[fp, small], writes=[e1])
                    mk.op('act', lambda e: e.activation(out=e1.t[:, :], in_=e1.t[:, :], func=AF.Ln, scale=1.0, bias=1.0), reads=[e1], writes=[e1])
                    mk.op('dve', lambda e: e.tensor_tensor_scan(out=ncT.t[:, :], data0=c['ones_f'].t[0:4, 0:1].to_broadcast([4, T]),
                                                                data1=e1.t[:, :], initial=0.0, op0=ALU.mult, op1=ALU.add),
                          reads=[e1, c['ones_f']], writes=[ncT])
                    mk.op('dve', lambda e: e.tensor_scalar(out=cq8.t[:, :], in0=ncT.t[:, :], scalar1=-8.0, scalar2=None, op0=ALU.mult),
                          reads=[ncT], writes=[cq8])
                    tp = pp.next()
                    for tt in range(16):
                        mk.op('pe', lambda e: e.transpose(out=tp.t[:, tt * 4:(tt + 1) * 4], in_=ncT.t[0:4, tt * 128:(tt + 1) * 128],
                                                          identity=c['ident_f'].t[0:4, 0:4]), reads=[ncT, c['ident_f']], writes=[tp])
                    mk.op('act', lambda e: e.activation(out=nck.t[:, :, :], in_=tp.t[:, 0:64].rearrange("p (a b) -> p a b", b=4), func=AF.Copy),
                          reads=[tp], writes=[nck])

                    def fox_bias(s_, cq, j, q0, N, sbk):
                        mk.op('pe', lambda e: e.matmul(sbk.t[:, 0:N], lhsT=c['eh4'].t[0:4, s_ * 128:(s_ + 1) * 128], rhs=cq8.t[0:4, q0:q0 + N],
                                                       start=False, stop=(j < 4 * cq), skip_group_check=True), reads=[c['eh4'], cq8], writes=[sbk])
                        causal_bias(sbk, j, cq, q0)

                    attention(4, 64, lambda s_: (s_ // 2, (s_ % 2) * 64), fox_bias, lambda s_, j: (nck, nck.t[:, j, s_:s_ + 1]), lambda s_: s_,
                              fin_std(oT[2]), 0.125)

            if 3 in branches:
                rope64 = (c['C64'], c['S64'], c['R64'])
                proj_fm(OFF['dsa_q'], 2, qT, rope64)
                proj_fm(OFF['dsa_k'], 2, kT, rope64)
                proj_v(OFF['dsa_v'])
                with ExitStack() as sm:
                    qiT = sb('qiT', [128, 2, T], BF16, sm)
                    kiT = sb('kiT', [128, 1, T], BF16, sm)
                    wk2 = sb('wk2', [128, 8, 128], BF16, sm)
                    ww = sb('ww', [128, 8, 4], BF16, sm)
                    wiT = sb('wiT', [4, T], F32, sm)
                    wi = sb('wi', [128, 16, 4], F32, sm)
                    score = sb('score', [128, T], F32, sm)
                    rl = Ring([sb('rl%d' % i, [128, 1024], F32, sm) for i in range(2)])
                    mb = sb('mb', [128, 4, T], BF16, sm)
                    bs = sb('bs', [128, 64], F32, sm)
                    proj_fm(OFF['idx_q'], 2, qiT, rope64)
                    loadw(wk2, OFF['idx_k'], 64, 0)
                    loadw(wk2, OFF['idx_k'], 64, 64)
                    proj_fm(0, 1, kiT, rope64, wb=wk2)
                    loadw(ww, OFF['idx_w'], 4)
                    for tc in range(4):
                        fp = pp.next()
                        for cc in range(8):
                            mk.op('pe', lambda e: e.matmul(fp.t[0:4, :], lhsT=ww.t[:, cc, 0:4], rhs=xb.t[:, cc, tc * 512:(tc + 1) * 512],
                                                           start=(cc == 0), stop=(cc == 7)), reads=[ww, xb], writes=[fp])
                        mk.op('act', lambda e: e.activation(out=wiT.t[:, tc * 512:(tc + 1) * 512], in_=fp.t[0:4, :], func=AF.Copy), reads=[fp], writes=[wiT])
                    tp = pp.next()
                    for tt in range(16):
                        mk.op('pe', lambda e: e.transpose(out=tp.t[:, tt * 4:(tt + 1) * 4], in_=wiT.t[0:4, tt * 128:(tt + 1) * 128],
                                                          identity=c['ident_f'].t[0:4, 0:4]), reads=[wiT, c['ident_f']], writes=[tp])
                    mk.op('act', lambda e: e.activation(out=wi.t[:, :, :], in_=tp.t[:, 0:64].rearrange("p (a b) -> p a b", b=4), func=AF.Copy),
                          reads=[tp], writes=[wi])
                    big = [(K.pst[0], [ps[0], ps[1]]), (K.pst[1], [ps[2], ps[3]])]

                    def indexer(cq):
                        for ql in range(4):
                            qt = 4 * cq + ql
                            KL = (qt + 1) * 128
                            for h in range(4):
                                ch, base = h // 2, (h % 2) * 64
                                r = rl.next()
                                for blk in range(0, KL, 1024):
                                    bt, bb = big[(blk // 1024) % 2]
                                    w_ = min(1024, KL - blk)
                                    for sub in range(0, w_, 512):
                                        sw = min(512, w_ - sub)
                                        mk.op('pe', lambda e: e.matmul(bt[:, sub:sub + sw], lhsT=qiT.t[base:base + 64, ch, qt * 128:(qt + 1) * 128],
                                                                       rhs=kiT.t[base:base + 64, 0, blk + sub:blk + sub + sw], start=True, stop=True),
                                              reads=[qiT, kiT], writes=[bb[sub // 512]])
                                    mk.op('act', lambda e: e.activation(out=r.t[:, 0:w_], in_=bt[:, 0:w_], func=AF.Relu), reads=bb, writes=[r])
                                    if h == 0:
                                        mk.op('dve', lambda e: e.tensor_scalar(out=score.t[:, blk:blk + w_], in0=r.t[:, 0:w_], scalar1=wi.t[:, qt, 0:1],
                                                                               scalar2=None, op0=ALU.mult), reads=[r, wi], writes=[score])
                                    else:
                                        mk.op('dve', lambda e: e.scalar_tensor_tensor(out=score.t[:, blk:blk + w_], in0=r.t[:, 0:w_],
                                                                                      scalar=wi.t[:, qt, h:h + 1], in1=score.t[:, blk:blk + w_],
                                                                                      op0=ALU.mult, op1=ALU.add), reads=[r, wi, score], writes=[score])
                            if qt >= 2:
                                mk.op('dve', lambda e: e.tensor_reduce(out=bs.t[:, 0:1], in_=score.t[:, 0:KL], axis=AX.X, op=ALU.max), reads=[score], writes=[bs])
                                mk.op('dve', lambda e: e.tensor_reduce(out=bs.t[:, 1:2], in_=score.t[:, 0:KL], axis=AX.X, op=ALU.min), reads=[score], writes=[bs])
                            mk.op('pool', lambda e: e.tensor_tensor(out=score.t[:, qt * 128:KL], in0=score.t[:, qt * 128:KL], in1=c['tribig_f'].t[:, :], op=ALU.add),
                                  reads=[score, c['tribig_f']], writes=[score])
                            if qt >= 2:
                                mk.op('dve', lambda e: e.tensor_scalar(out=bs.t[:, 2:3], in0=bs.t[:, 1:2], scalar1=-1.0, scalar2=None, op0=ALU.add), reads=[bs], writes=[bs])
                                mk.op('dve', lambda e: e.tensor_tensor(out=bs.t[:, 3:4], in0=bs.t[:, 0:1], in1=bs.t[:, 1:2], op=ALU.subtract), reads=[bs], writes=[bs])
                                mk.op('dve', lambda e: e.tensor_scalar(out=bs.t[:, 3:4], in0=bs.t[:, 3:4], scalar1=2.0, scalar2=None, op0=ALU.add), reads=[bs], writes=[bs])
                                mk.op('dve', lambda e: e.tensor_scalar(out=bs.t[:, 8:8 + NBIS], in0=c['pow2_f'].t[:, 0:NBIS], scalar1=bs.t[:, 3:4], scalar2=None, op0=ALU.mult),
                                      reads=[bs, c['pow2_f']], writes=[bs])
                                for it in range(NBIS):
                                    mk.op('dve', lambda e: e.tensor_tensor(out=bs.t[:, 4:5], in0=bs.t[:, 2:3], in1=bs.t[:, 8 + it:9 + it], op=ALU.add), reads=[bs], writes=[bs])
                                    mk.op('dve', lambda e: e.tensor_scalar(out=mb.t[:, ql, 0:KL], in0=score.t[:, 0:KL], scalar1=bs.t[:, 4:5], scalar2=None,
                                                                           op0=ALU.is_gt, op1=ALU.add, accum_out=bs.t[:, 5:6]), reads=[score, bs], writes=[mb, bs])
                                    mk.op('dve', lambda e: e.tensor_scalar(out=bs.t[:, 6:7], in0=bs.t[:, 5:6], scalar1=255.5, scalar2=bs.t[:, 8 + it:9 + it],
                                                                           op0=ALU.is_gt, op1=ALU.mult), reads=[bs], writes=[bs])
                                    mk.op('dve', lambda e: e.tensor_tensor(out=bs.t[:, 2:3], in0=bs.t[:, 2:3], in1=bs.t[:, 6:7], op=ALU.add), reads=[bs], writes=[bs])
                            else:
                                mk.op('dve', lambda e: e.memset(bs.t[:, 2:3], -1e29), writes=[bs])
                            mk.op('dve', lambda e: e.tensor_scalar(out=mb.t[:, ql, 0:KL], in0=score.t[:, 0:KL], scalar1=bs.t[:, 2:3], scalar2=NEG,
                                                                   op0=ALU.is_le, op1=ALU.mult), reads=[score, bs], writes=[mb])
                            if qt == 5:
                                K.dump('d_bs5', bs, bs.t[:, :], [128, 64])
                                K.dump('d_score5', score, score.t[:, 0:KL], [128, KL])
                                K.dump('d_mb5', mb, mb.t[:, ql, 0:KL], [128, KL])
                            if qt == 1:
                                K.dump('d_score1', score, score.t[:, 0:KL], [128, KL])
                                K.dump('d_mb1', mb, mb.t[:, ql, 0:KL], [128, KL])

                    def dsa_bias(s_, cq, j, q0, N, sbk):
                        if s_ == 0 and j == 0:
                            indexer(cq)
                        qts = [qt for qt in range(4 * cq, 4 * cq + 4) if qt >= j]
                        for i, qt in enumerate(qts):
                            o_ = qt * 128 - q0
                            mk.op('pe', lambda e: e.matmul(sbk.t[:, o_:o_ + 128], lhsT=mb.t[:, qt - 4 * cq, j * 128:(j + 1) * 128], rhs=c['ident'].t[:, :],
                                                           start=False, stop=(i == len(qts) - 1), skip_group_check=True), reads=[mb, c['ident']], writes=[sbk])

                    attention(4, 64, lambda s_: (s_ // 2, (s_ % 2) * 64), dsa_bias, lambda s_, j: None, lambda s_: s_, fin_std(oT[3]), 0.125, cq_outer=True)
            mk.barrier()

        if 'oT' in dbg:
            for i in range(4):
                for ch in range(2):
                    mk.dma('pool', lambda e: e.dma_start(out=dr['oT'].t[i * 256 + ch * 128:i * 256 + (ch + 1) * 128, :], in_=oT[i].t[:, ch, :]),
                           reads=[oT[i]], writes=[dr['oT']])
        mk.barrier()
        if not K.cfg.get('merge', True):
            return
        mg = sb('mg', [128, 8, T], BF16, sa)
        pp = Ring([ps[0], ps[1], ps[2], ps[3]])
        pq = Ring([ps[4], ps[5], ps[6], ps[7]])
        with ExitStack() as sm:
            wbr = sb('wbr', [128, 4, 2, D], BF16, sm)
            wg = Ring([sb('wg%d' % i, [128, 8, 4, 128], BF16, sm) for i in range(2)])
            sgr = Ring([sb('sg%d' % i, [128, 512], F32, sm) for i in range(2)])
            tmr = Ring([sb('tm%d' % i, [128, 512], F32, sm) for i in range(2)])
            acr = Ring([sb('ac%d' % i, [128, 512], F32, sm) for i in range(2)])
            wbv = dr['w_branch'].t[l].rearrange("i (kc p) n -> p i kc n", p=128)
            for i in range(4):
                mk.dma('pool', lambda e: e.dma_start(out=wbr.t[:, i, :, :], in_=wbv[:, i, :, :]), reads=[dr['w_branch']], writes=[wbr])
            g0 = OFF['gates']
            wgv = w_in_v[:, :, g0:g0 + 4096].rearrange("p c (i n) -> p c i n", i=4)
            for m in range(8):
                wgt = wg.next()
                for i in range(4):
                    mk.dma('pool', lambda e: e.dma_start(out=wgt.t[:, :, i, :], in_=wgv[:, :, i, m * 128:(m + 1) * 128]), reads=[w_in], writes=[wgt])
                for tc in range(4):
                    ac = acr.next()
                    for i in range(4):
                        gp = pp.next()
                        for cc in range(8):
                            mk.op('pe', lambda e: e.matmul(gp.t[:, :], lhsT=wgt.t[:, cc, i, :], rhs=xb.t[:, cc, tc * 512:(tc + 1) * 512],
                                                           start=(cc == 0), stop=(cc == 7)), reads=[wgt, xb], writes=[gp])
                        sg = sgr.next()
                        mk.op('act', lambda e: e.activation(out=sg.t[:, :], in_=gp.t[:, :], func=AF.Sigmoid), reads=[gp], writes=[sg])
                        bp = pq.next()
                        for kc in range(2):
                            mk.op('pe', lambda e: e.matmul(bp.t[:, :], lhsT=wbr.t[:, i, kc, m * 128:(m + 1) * 128], rhs=oT[i].t[:, kc, tc * 512:(tc + 1) * 512],
                                                           start=(kc == 0), stop=(kc == 1)), reads=[wbr, oT[i]], writes=[bp])
                        if i == 0:
                            mk.op('dve', lambda e: e.tensor_tensor(out=ac.t[:, :], in0=bp.t[:, :], in1=sg.t[:, :], op=ALU.mult), reads=[bp, sg], writes=[ac])
                        else:
                            tm = tmr.next()
                            mk.op('dve', lambda e: e.tensor_tensor(out=tm.t[:, :], in0=bp.t[:, :], in1=sg.t[:, :], op=ALU.mult), reads=[bp, sg], writes=[tm])
                            if i < 3:
                                mk.op('pool', lambda e: e.tensor_tensor(out=ac.t[:, :], in0=ac.t[:, :], in1=tm.t[:, :], op=ALU.add), reads=[ac, tm], writes=[ac])
                            else:
                                mk.op('pool', lambda e: e.tensor_tensor(out=mg.t[:, m, tc * 512:(tc + 1) * 512], in0=ac.t[:, :], in1=tm.t[:, :], op=ALU.add),
                                      reads=[ac, tm], writes=[mg])
            mk.barrier()
        if not K.cfg.get('ln1', True):
            return
        with ExitStack() as so:
            wo = sb('wo', [128, 8, D], BF16, so)
            xres = sb('xres', [128, 8, 512], F32, so)
            y = sb('y', [128, 8, 512], F32, so)
            sqr = Ring([sb('sq%d' % i, [128, 512], F32, so) for i in range(2)])
            mean = sb('mean', [128, 512], F32, so)
            rstd = sb('rstd', [128, 512], F32, so)
            msq = sb('msq', [128, 512], F32, so)
            rowf = Ring([sb('rowf%d' % i, [128, D], F32, so) for i in range(2)])
            rowb = Ring([sb('rowb%d' % i, [128, D], BF16, so) for i in range(2)])
            wov = dr['w_out'].t[l].rearrange("(kc p) n -> p kc n", p=128)
            for kc in range(8):
                mk.dma('pool', lambda e: e.dma_start(out=wo.t[:, kc, :], in_=wov[:, kc, :]), reads=[dr['w_out']], writes=[wo])
            x1Tv = dr['x1T'].t.rearrange("(c p) n -> p c n", p=128)
            ga, gb = PA['ln1_g']
            ba, bb_ = PA['ln1_b']
            big = [(K.pst[0], [ps[0], ps[1]]), (K.pst[1], [ps[2], ps[3]])]
            for tc in range(4):
                c0 = t0g + tc * 512
                mk.dma('sp', lambda e: e.dma_start(out=xres.t[:, :, :], in_=srcv[:, :, c0:c0 + 512]), reads=[src], writes=[xres])
                for m in range(8):
                    hp = pq.next()
                    for kc in range(8):
                        mk.op('pe', lambda e: e.matmul(hp.t[:, :], lhsT=wo.t[:, kc, m * 128:(m + 1) * 128], rhs=mg.t[:, kc, tc * 512:(tc + 1) * 512],
                                                       start=(kc == 0), stop=(kc == 7)), reads=[wo, mg], writes=[hp])
                    mk.op('dve', lambda e: e.scalar_tensor_tensor(out=y.t[:, m, :], in0=xres.t[:, m, :], scalar=ALPHA, in1=hp.t[:, :], op0=ALU.mult, op1=ALU.add),
                          reads=[xres, hp], writes=[y])
                s1, s2 = ps[4], ps[5]
                for m in range(8):
                    mk.op('pe', lambda e: e.matmul(s1.t[:, :], lhsT=c['ones_f'].t[:, :], rhs=y.t[:, m, :], start=(m == 0), stop=(m == 7)),
                          reads=[c['ones_f'], y], writes=[s1])
                for m in range(8):
                    sq = sqr.next()
                    mk.op('act', lambda e: e.activation(out=sq.t[:, :], in_=y.t[:, m, :], func=AF.Square), reads=[y], writes=[sq])
                    mk.op('pe', lambda e: e.matmul(s2.t[:, :], lhsT=c['ones_f'].t[:, :], rhs=sq.t[:, :], start=(m == 0), stop=(m == 7)),
                          reads=[c['ones_f'], sq], writes=[s2])
                mk.op('act', lambda e: e.activation(out=mean.t[:, :], in_=s1.t[:, :], func=AF.Copy, scale=1.0 / D), reads=[s1], writes=[mean])
                mk.op('pool', lambda e: e.tensor_tensor(out=msq.t[:, :], in0=mean.t[:, :], in1=mean.t[:, :], op=ALU.mult), reads=[mean], writes=[msq])
                mk.op('dve', lambda e: e.scalar_tensor_tensor(out=rstd.t[:, :], in0=s2.t[:, :], scalar=1.0 / D, in1=msq.t[:, :], op0=ALU.mult, op1=ALU.subtract),
                      reads=[s2, msq], writes=[rstd])
                mk.op('act', lambda e: e.activation(out=rstd.t[:, :], in_=rstd.t[:, :], func=AF.Ln, scale=1.0, bias=LN_EPS), reads=[rstd], writes=[rstd])
                mk.op('act', lambda e: e.activation(out=rstd.t[:, :], in_=rstd.t[:, :], func=AF.Exp, scale=-0.5), reads=[rstd], writes=[rstd])
                for m in range(8):
                    mk.op('dve', lambda e: e.tensor_tensor(out=y.t[:, m, :], in0=y.t[:, m, :], in1=mean.t[:, :], op=ALU.subtract), reads=[y, mean], writes=[y])
                    mk.op('pool', lambda e: e.tensor_tensor(out=y.t[:, m, :], in0=y.t[:, m, :], in1=rstd.t[:, :], op=ALU.mult), reads=[y, rstd], writes=[y])
                    mk.op('act', lambda e: e.activation(out=y.t[:, m, :], in_=y.t[:, m, :], func=AF.Identity, scale=par.t[:, ga + m:ga + m + 1],
                                                        bias=par.t[:, ba + m:ba + m + 1]), reads=[y, par], writes=[y])
                mk.dma('sp', lambda e: e.dma_start(out=x1Tv[:, :, c0:c0 + 512], in_=y.t[:, :, :]), reads=[y], writes=[dr['x1T']])
                for tt in range(4):
                    bt, bb2 = big[tt % 2]
                    for m in range(8):
                        mk.op('pe', lambda e: e.transpose(out=bt[:, m * 128:(m + 1) * 128], in_=y.t[:, m, tt * 128:(tt + 1) * 128], identity=c['ident_f'].t[:, :]),
                              reads=[y, c['ident_f']], writes=[bb2[m // 4]])
                    rf = rowf.next()
                    rb = rowb.next()
                    mk.op('act', lambda e: e.activation(out=rf.t[:, :], in_=bt[:, :], func=AF.Copy), reads=bb2, writes=[rf])
                    mk.op('pool', lambda e: e.tensor_copy(out=rb.t[:, :], in_=rf.t[:, :]), reads=[rf], writes=[rb])
                    r0 = c0 + tt * 128
                    mk.dma('sp', lambda e: e.dma_start(out=dr['x1r'].t[r0:r0 + 128, :], in_=rf.t[:, :]), reads=[rf], writes=[dr['x1r']])
                    mk.dma('sp', lambda e: e.dma_start(out=dr['x1rb'].t[r0:r0 + 128, :], in_=rb.t[:, :]), reads=[rb], writes=[dr['x1rb']])
            mk.barrier()


def stage_b(K, l, last):
    mk, sb, dr, nc, ps, c = K.mk, K.sb, K.dr, K.nc, K.ps, K.c
    NT = K.NT
    NTL = NT // 128
    NEX = K.cfg.get('ne', NE)
    IOA = bass.IndirectOffsetOnAxis
    pp = Ring([ps[0], ps[1], ps[2], ps[3]])
    pq = Ring([ps[4], ps[5], ps[6], ps[7]])
    big = [(K.pst[0], [ps[0], ps[1]]), (K.pst[1], [ps[2], ps[3]])]
    x1Tv = dr['x1T'].t.rearrange("(c p) n -> p c n", p=128)
    with ExitStack() as sbs:
        par = sb('parB', [128, NPB], F32, sbs)
        slotall = sb('slotall', [128, NTL, 8], I32, sbs)
        mk.dma('sp', lambda e: e.dma_start(out=par.t[:, :], in_=dr['parB'].t[l]), reads=[dr['parB']], writes=[par])
        with ExitStack() as s1:
            wr = sb('wr', [128, 8, 256], BF16, s1)
            carry = sb('carry', [128, 256], F32, s1)
            tokid = sb('tokid', [128, NTL], I32, s1)
            fill = sb('fill', [128, NE * CAP * 2 // 128], I32, s1)
            zrow = sb('zrow', [128, D], BF16, s1)
            xtbr = Ring([sb('xtb%d' % i, [128, 8, 128], BF16, s1) for i in range(2)])
            f256 = {n: sb(n, [128, 256], F32, s1) for n in ('sc', 'bi', 'eq', 'bi2', 'msk', 'sel', 'wd', 'pos', 'slotf', 'junk')}
            selb = sb('selb', [128, 256], BF16, s1)
            sm8 = sb('sm8', [128, 64], F32, s1)
            stg = sb('stg', [128, 8, 2], I32, s1)
            mk.dma('pool', lambda e: e.dma_start(out=wr.t[:, :, :], in_=dr['w_router'].t[l].rearrange("(c p) n -> p c n", p=128)),
                   reads=[dr['w_router']], writes=[wr])
            mk.op('dve', lambda e: e.memset(carry.t[:, :], 0.0), writes=[carry])
            mk.op('pool', lambda e: e.iota(tokid.t[:, :], pattern=[[128, NTL]], base=0, channel_multiplier=1), writes=[tokid])
            mk.op('dve', lambda e: e.memset(fill.t[:, :], 0), writes=[fill])
            mk.op('dve', lambda e: e.memset(fill.t[:, :].rearrange("p (a b) -> p a b", b=2)[:, :, 0], NT), writes=[fill])
            mk.op('pool', lambda e: e.memset(zrow.t[:, :], 0.0), writes=[zrow])
            mk.dma('sp', lambda e: e.dma_start(out=dr['tokw'].t.rearrange("(p a) b -> p (a b)", p=128), in_=fill.t[:, :]), reads=[fill], writes=[dr['tokw']])
            mk.dma('sp', lambda e: e.dma_start(out=dr['x1rb'].t[NT:NT + 128, :], in_=zrow.t[:, :]), reads=[zrow], writes=[dr['x1rb']])
            if NEX < NE:
                yz = dr['ye'].t.rearrange("(p a) n -> p a n", p=128)
                for a0 in range(0, NE * CAP // 128, 96):
                    mk.dma('sp', lambda e: e.dma_start(out=yz[:, a0:a0 + 96, :], in_=zrow.t[:, :].unsqueeze(1).to_broadcast([128, 96, D])), reads=[zrow], writes=[dr['ye']])
            sc, bi, eq, bi2, msk, sel, wd, pos, slotf, junk = [f256[n] for n in ('sc', 'bi', 'eq', 'bi2', 'msk', 'sel', 'wd', 'pos', 'slotf', 'junk')]
            ra, rb_ = PB['rbias']
            for tt in range(NTL):
                xtb = xtbr.next()
                mk.dma('pool', lambda e: e.dma_start(out=xtb.t[:, :, :], in_=x1Tv[:, :, tt * 128:(tt + 1) * 128]), reads=[dr['x1T']], writes=[xtb])
                lp = pp.next()
                for cc in range(8):
                    mk.op('pe', lambda e: e.matmul(lp.t[:, 0:256], lhsT=xtb.t[:, cc, :], rhs=wr.t[:, cc, :], start=(cc == 0), stop=(cc == 7)),
                          reads=[xtb, wr], writes=[lp])
                mk.op('act', lambda e: e.activation(out=sc.t[:, :], in_=lp.t[:, 0:256], func=AF.Sigmoid), reads=[lp], writes=[sc])
                mk.op('dve', lambda e: e.tensor_tensor(out=bi.t[:, :], in0=sc.t[:, :], in1=par.t[:, ra:rb_], op=ALU.add), reads=[sc, par], writes=[bi])
                bi3 = bi.t[:, :].rearrange("p (g k) -> p g k", k=32)
                mk.op('dve', lambda e: e.tensor_reduce(out=sm8.t[:, 0:8], in_=bi3, axis=AX.X, op=ALU.max), reads=[bi], writes=[sm8])
                mk.op('dve', lambda e: e.tensor_tensor(out=eq.t[:, :].rearrange("p (g k) -> p g k", k=32), in0=bi3,
                                                       in1=sm8.t[:, 0:8].unsqueeze(2).to_broadcast([128, 8, 32]), op=ALU.is_equal), reads=[bi, sm8], writes=[eq])
                mk.op('dve', lambda e: e.scalar_tensor_tensor(out=bi2.t[:, :], in0=eq.t[:, :], scalar=-1e30, in1=bi.t[:, :], op0=ALU.mult, op1=ALU.add),
                      reads=[eq, bi], writes=[bi2])
                mk.op('dve', lambda e: e.tensor_reduce(out=sm8.t[:, 8:16], in_=bi2.t[:, :].rearrange("p (g k) -> p g k", k=32), axis=AX.X, op=ALU.max),
                      reads=[bi2], writes=[sm8])
                mk.op('dve', lambda e: e.tensor_tensor(out=sm8.t[:, 16:24], in0=sm8.t[:, 0:8], in1=sm8.t[:, 8:16], op=ALU.add), reads=[sm8], writes=[sm8])
                mk.op('dve', lambda e: e.max(out=sm8.t[:, 24:32], in_=sm8.t[:, 16:24]), reads=[sm8], writes=[sm8])
                mk.op('dve', lambda e: e.tensor_scalar(out=sm8.t[:, 32:40], in0=sm8.t[:, 16:24], scalar1=sm8.t[:, 27:28], scalar2=-1e30, op0=ALU.is_lt, op1=ALU.mult),
                      reads=[sm8], writes=[sm8])
                mk.op('dve', lambda e: e.tensor_tensor(out=msk.t[:, :].rearrange("p (g k) -> p g k", k=32), in0=bi3,
                                                       in1=sm8.t[:, 32:40].unsqueeze(2).to_broadcast([128, 8, 32]), op=ALU.add), reads=[bi, sm8], writes=[msk])
                mk.op('dve', lambda e: e.max(out=sm8.t[:, 40:48], in_=msk.t[:, :]), reads=[msk], writes=[sm8])
                mk.op('dve', lambda e: e.tensor_scalar(out=sel.t[:, :], in0=msk.t[:, :], scalar1=sm8.t[:, 47:48], scalar2=None, op0=ALU.is_ge), reads=[msk, sm8], writes=[sel])
                mk.op('dve', lambda e: e.scalar_tensor_tensor(out=wd.t[:, :], in0=sel.t[:, :], scalar=1.0, in1=sc.t[:, :], op0=ALU.mult, op1=ALU.mult,
                                                              accum_out=sm8.t[:, 48:49]), reads=[sel, sc], writes=[wd, sm8])
                mk.op('dve', lambda e: e.reciprocal(out=sm8.t[:, 49:50], in_=sm8.t[:, 48:49]), reads=[sm8], writes=[sm8])
                mk.op('dve', lambda e: e.tensor_scalar(out=wd.t[:, :], in0=wd.t[:, :], scalar1=sm8.t[:, 49:50], scalar2=2.5, op0=ALU.mult, op1=ALU.mult),
                      reads=[wd, sm8], writes=[wd])
                mk.op('pool', lambda e: e.tensor_copy(out=selb.t[:, :], in_=sel.t[:, :]), reads=[sel], writes=[selb])
                p1 = pq.next()
                p2 = pq.next()
                mk.op('pe', lambda e: e.matmul(p1.t[:, 0:256], lhsT=c['lstrict'].t[:, :], rhs=selb.t[:, :], start=True, stop=True), reads=[c['lstrict'], selb], writes=[p1])
                mk.op('pe', lambda e: e.matmul(p2.t[:, 0:256], lhsT=c['ones'].t[:, :], rhs=selb.t[:, :], start=True, stop=True), reads=[c['ones'], selb], writes=[p2])
                mk.op('dve', lambda e: e.tensor_tensor(out=pos.t[:, :], in0=p1.t[:, 0:256], in1=carry.t[:, :], op=ALU.add), reads=[p1, carry], writes=[pos])
                mk.op('dve', lambda e: e.tensor_tensor(out=carry.t[:, :], in0=p2.t[:, 0:256], in1=carry.t[:, :], op=ALU.add), reads=[p2, carry], writes=[carry])
                mk.op('dve', lambda e: e.scalar_tensor_tensor(out=pos.t[:, :], in0=pos.t[:, :], scalar=float(CAP - 1), in1=c['ebase_f'].t[:, :], op0=ALU.min, op1=ALU.add),
                      reads=[pos, c['ebase_f']], writes=[pos])
                mk.op('dve', lambda e: e.tensor_tensor(out=slotf.t[:, :], in0=pos.t[:, :], in1=sel.t[:, :], op=ALU.mult), reads=[pos, sel], writes=[slotf])
                mk.op('dve', lambda e: e.max(out=sm8.t[:, 50:58], in_=slotf.t[:, :]), reads=[slotf], writes=[sm8])
                for k in range(8):
                    mk.op('dve', lambda e: e.scalar_tensor_tensor(out=junk.t[:, :], in0=slotf.t[:, :], scalar=sm8.t[:, 50 + k:51 + k], in1=wd.t[:, :],
                                                                  op0=ALU.is_equal, op1=ALU.mult, accum_out=stg.t[:, k, 1:2].bitcast(F32)),
                          reads=[slotf, sm8, wd], writes=[junk, stg])
                mk.op('dve', lambda e: e.tensor_scalar(out=slotall.t[:, tt, :], in0=sm8.t[:, 50:58], scalar1=-1.0, scalar2=None, op0=ALU.add), reads=[sm8], writes=[slotall])
                mk.op('pool', lambda e: e.tensor_copy(out=stg.t[:, :, 0], in_=tokid.t[:, tt:tt + 1].to_broadcast([128, 8])), reads=[tokid], writes=[stg])
                for k in range(8):
                    mk.dma('pool', lambda e: e.indirect_dma_start(out=dr['tokw'].t[:, :], out_offset=IOA(slotall.t[:, tt, k:k + 1], 0), in_=stg.t[:, k, :], in_offset=None),
                           reads=[stg, slotall], writes=[dr['tokw']])
            mk.barrier()
        with ExitStack() as s2:
            twr = Ring([sb('tw%d' % i, [128, 3, 2], I32, s2) for i in range(2)])
            xer = Ring([sb('xe%d' % i, [128, 3, D], BF16, s2) for i in range(2)])
            xeTr = Ring([sb('xeT%d' % i, [128, 8, CAP], BF16, s2) for i in range(2)])
            wgur = Ring([sb('wgu%d' % i, [128, 8, 512], BF16, s2) for i in range(2)])
            wdnr = Ring([sb('wdn%d' % i, [128, 2, D], BF16, s2) for i in range(2)])
            hTr = Ring([sb('hT%d' % i, [128, 2, CAP], BF16, s2) for i in range(2)])
            sgr = Ring([sb('sge%d' % i, [128, CAP], F32, s2) for i in range(2)])
            yor = Ring([sb('yo%d' % i, [128, 3, D], BF16, s2) for i in range(2)])
            twv = dr['tokw'].t.rearrange("(e rt p) c -> e p rt c", rt=3, p=128)
            yev = dr['ye'].t.rearrange("(e rt p) n -> e p rt n", rt=3, p=128)
            ev = 0
            for ex in range(NEX):
                tw, xe, xeT, wgu, wdn, hT, yo = twr.next(), xer.next(), xeTr.next(), wgur.next(), wdnr.next(), hTr.next(), yor.next()
                mk.dma('sp', lambda e: e.dma_start(out=tw.t[:, :, :], in_=twv[ex]), reads=[dr['tokw']], writes=[tw])
                mk.dma('pool', lambda e: e.dma_start(out=wgu.t[:, :, 0:256], in_=dr['w_exp_gate'].t[l, ex].rearrange("(c p) n -> p c n", p=128)),
                       reads=[dr['w_exp_gate']], writes=[wgu])
                mk.dma('pool', lambda e: e.dma_start(out=wgu.t[:, :, 256:512], in_=dr['w_exp_up'].t[l, ex].rearrange("(c p) n -> p c n", p=128)),
                       reads=[dr['w_exp_up']], writes=[wgu])
                mk.dma('pool', lambda e: e.dma_start(out=wdn.t[:, :, :], in_=dr['w_exp_down'].t[l, ex].rearrange("(c p) n -> p c n", p=128)),
                       reads=[dr['w_exp_down']], writes=[wdn])
                for rt in range(3):
                    mk.dma('pool', lambda e: e.indirect_dma_start(out=xe.t[:, rt, :], out_offset=None, in_=dr['x1rb'].t[:, :], in_offset=IOA(tw.t[:, rt, 0:1], 0)),
                           reads=[tw, dr['x1rb']], writes=[xe])
                for rt in range(3):
                    bk = pp.next()
                    vb = bk.t.bitcast(BF16)
                    for cc in range(8):
                        mk.op('pe', lambda e: e.transpose(out=vb[:, cc * 128:(cc + 1) * 128], in_=xe.t[:, rt, cc * 128:(cc + 1) * 128], identity=c['ident'].t[:, :]),
                              reads=[xe, c['ident']], writes=[bk])
                    ev += 1
                    if ev % 2 == 0:
                        mk.op('act', lambda e: e.activation(out=xeT.t[:, :, rt * 128:(rt + 1) * 128], in_=vb.rearrange("p (c r) -> p c r", c=8), func=AF.Copy),
                              reads=[bk], writes=[xeT])
                    else:
                        mk.op('dve', lambda e: e.tensor_copy(out=xeT.t[:, :, rt * 128:(rt + 1) * 128], in_=vb.rearrange("p (c r) -> p c r", c=8)), reads=[bk], writes=[xeT])
                for fc in range(2):
                    gp = pq.next()
                    up_ = pq.next()
                    for cc in range(8):
                        mk.op('pe', lambda e: e.matmul(gp.t[:, 0:CAP], lhsT=wgu.t[:, cc, fc * 128:(fc + 1) * 128], rhs=xeT.t[:, cc, :], start=(cc == 0), stop=(cc == 7)),
                              reads=[wgu, xeT], writes=[gp])
                    for cc in range(8):
                        mk.op('pe', lambda e: e.matmul(up_.t[:, 0:CAP], lhsT=wgu.t[:, cc, 256 + fc * 128:256 + (fc + 1) * 128], rhs=xeT.t[:, cc, :], start=(cc == 0), stop=(cc == 7)),
                              reads=[wgu, xeT], writes=[up_])
                    sg = sgr.next()
                    mk.op('act', lambda e: e.activation(out=sg.t[:, :], in_=gp.t[:, 0:CAP], func=AF.Silu), reads=[gp], writes=[sg])
                    mk.op('dve', lambda e: e.tensor_tensor(out=hT.t[:, fc, :], in0=up_.t[:, 0:CAP], in1=sg.t[:, :], op=ALU.mult), reads=[up_, sg], writes=[hT])
                twf = tw.t[:, :, :].bitcast(F32)
                for rt in range(3):
                    for half in range(2):
                        yp = pp.next()
                        for fc in range(2):
                            mk.op('pe', lambda e: e.matmul(yp.t[:, :], lhsT=hT.t[:, fc, rt * 128:(rt + 1) * 128], rhs=wdn.t[:, fc, half * 512:(half + 1) * 512],
                                                           start=(fc == 0), stop=(fc == 1)), reads=[hT, wdn], writes=[yp])
                        ev += 1
                        if ev % 2 == 0:
                            mk.op('act', lambda e: e.activation(out=yo.t[:, rt, half * 512:(half + 1) * 512], in_=yp.t[:, :], func=AF.Copy, scale=twf[:, rt, 1:2]),
                                  reads=[yp, tw], writes=[yo])
                        else:
                            mk.op('dve', lambda e: e.tensor_scalar(out=yo.t[:, rt, half * 512:(half + 1) * 512], in0=yp.t[:, :], scalar1=twf[:, rt, 1:2], scalar2=None, op0=ALU.mult),
                                  reads=[yp, tw], writes=[yo])
                mk.dma('sp', lambda e: e.dma_start(out=yev[ex], in_=yo.t[:, :, :]), reads=[yo], writes=[dr['ye']])
            mk.barrier()
        with ExitStack() as s3:
            wsgu = sb('wsgu', [128, 8, 512], BF16, s3)
            wsd = sb('wsd', [128, 2, D], BF16, s3)
            xrr = Ring([sb('xr%d' % i, [128, D], F32, s3) for i in range(2)])
            xtbr = Ring([sb('xtc%d' % i, [128, 8, 128], BF16, s3) for i in range(2)])
            ygr = Ring([sb('yg%d' % i, [128, D], BF16, s3) for i in range(4)])
            accr = Ring([sb('acc%d' % i, [128, D], F32, s3) for i in range(2)])
            hs = sb('hs', [128, 2, 128], BF16, s3)
            sgs = sb('sgs', [128, 128], F32, s3)
            st = sb('st', [128, 16], F32, s3)
            jk = sb('jk', [128, D], F32, s3)
            xo = sb('xo', [128, 8, 128], F32, s3)
            mk.dma('pool', lambda e: e.dma_start(out=wsgu.t[:, :, 0:256], in_=dr['w_sh_gate'].t[l].rearrange("(c p) n -> p c n", p=128)), reads=[dr['w_sh_gate']], writes=[wsgu])
            mk.dma('pool', lambda e: e.dma_start(out=wsgu.t[:, :, 256:512], in_=dr['w_sh_up'].t[l].rearrange("(c p) n -> p c n", p=128)), reads=[dr['w_sh_up']], writes=[wsgu])
            mk.dma('pool', lambda e: e.dma_start(out=wsd.t[:, :, :], in_=dr['w_sh_down'].t[l].rearrange("(c p) n -> p c n", p=128)), reads=[dr['w_sh_down']], writes=[wsd])
            g0, g1 = PB['ln2_g']
            b0, b1 = PB['ln2_b']
            x2Tv = dr['x2T'].t.rearrange("(c p) n -> p c n", p=128)
            for tt in range(NTL):
                xr, xtb, acc = xrr.next(), xtbr.next(), accr.next()
                mk.dma('sp', lambda e: e.dma_start(out=xr.t[:, :], in_=dr['x1r'].t[tt * 128:(tt + 1) * 128, :]), reads=[dr['x1r']], writes=[xr])
                mk.dma('pool', lambda e: e.dma_start(out=xtb.t[:, :, :], in_=x1Tv[:, :, tt * 128:(tt + 1) * 128]), reads=[dr['x1T']], writes=[xtb])
                for fc in range(2):
                    gp = pq.next()
                    up_ = pq.next()
                    for cc in range(8):
                        mk.op('pe', lambda e: e.matmul(gp.t[:, 0:128], lhsT=wsgu.t[:, cc, fc * 128:(fc + 1) * 128], rhs=xtb.t[:, cc, :], start=(cc == 0), stop=(cc == 7)),
                              reads=[wsgu, xtb], writes=[gp])
                    for cc in range(8):
                        mk.op('pe', lambda e: e.matmul(up_.t[:, 0:128], lhsT=wsgu.t[:, cc, 256 + fc * 128:256 + (fc + 1) * 128], rhs=xtb.t[:, cc, :], start=(cc == 0), stop=(cc == 7)),
                              reads=[wsgu, xtb], writes=[up_])
                    mk.op('act', lambda e: e.activation(out=sgs.t[:, :], in_=gp.t[:, 0:128], func=AF.Silu), reads=[gp], writes=[sgs])
                    mk.op('dve', lambda e: e.tensor_tensor(out=hs.t[:, fc, :], in0=up_.t[:, 0:128], in1=sgs.t[:, :], op=ALU.mult), reads=[up_, sgs], writes=[hs])
                for half in range(2):
                    yp = pp.next()
                    for fc in range(2):
                        mk.op('pe', lambda e: e.matmul(yp.t[:, :], lhsT=hs.t[:, fc, :], rhs=wsd.t[:, fc, half * 512:(half + 1) * 512], start=(fc == 0), stop=(fc == 1)),
                              reads=[hs, wsd], writes=[yp])
                    mk.op('dve', lambda e: e.scalar_tensor_tensor(out=acc.t[:, half * 512:(half + 1) * 512], in0=xr.t[:, half * 512:(half + 1) * 512], scalar=ALPHA,
                                                                  in1=yp.t[:, :], op0=ALU.mult, op1=ALU.add), reads=[xr, yp], writes=[acc])
                for k in range(8):
                    yg = ygr.next()
                    mk.dma('pool', lambda e: e.indirect_dma_start(out=yg.t[:, :], out_offset=None, in_=dr['ye'].t[:, :], in_offset=IOA(slotall.t[:, tt, k:k + 1], 0)),
                           reads=[slotall, dr['ye']], writes=[yg])
                    mk.op('dve', lambda e: e.tensor_tensor(out=acc.t[:, :], in0=acc.t[:, :], in1=yg.t[:, :], op=ALU.add), reads=[acc, yg], writes=[acc])
                mk.op('act', lambda e: e.activation(out=jk.t[:, :], in_=acc.t[:, :], func=AF.Copy, accum_out=st.t[:, 0:1]), reads=[acc], writes=[jk, st])
                mk.op('act', lambda e: e.activation(out=jk.t[:, :], in_=acc.t[:, :], func=AF.Square, accum_out=st.t[:, 1:2]), reads=[acc], writes=[jk, st])
                mk.op('dve', lambda e: e.tensor_scalar(out=st.t[:, 2:4], in0=st.t[:, 0:2], scalar1=1.0 / D, scalar2=None, op0=ALU.mult), reads=[st], writes=[st])
                mk.op('dve', lambda e: e.tensor_tensor(out=st.t[:, 4:5], in0=st.t[:, 2:3], in1=st.t[:, 2:3], op=ALU.mult), reads=[st], writes=[st])
                mk.op('dve', lambda e: e.tensor_tensor(out=st.t[:, 5:6], in0=st.t[:, 3:4], in1=st.t[:, 4:5], op=ALU.subtract), reads=[st], writes=[st])
                mk.op('act', lambda e: e.activation(out=st.t[:, 6:7], in_=st.t[:, 5:6], func=AF.Ln, scale=1.0, bias=LN_EPS), reads=[st], writes=[st])
                mk.op('act', lambda e: e.activation(out=st.t[:, 7:8], in_=st.t[:, 6:7], func=AF.Exp, scale=-0.5), reads=[st], writes=[st])
                mk.op('dve', lambda e: e.tensor_scalar(out=acc.t[:, :], in0=acc.t[:, :], scalar1=st.t[:, 2:3], scalar2=st.t[:, 7:8], op0=ALU.subtract, op1=ALU.mult),
                      reads=[acc, st], writes=[acc])
                mk.op('pool', lambda e: e.tensor_tensor(out=acc.t[:, :], in0=acc.t[:, :], in1=par.t[:, g0:g1], op=ALU.mult), reads=[acc, par], writes=[acc])
                mk.op('dve', lambda e: e.tensor_tensor(out=acc.t[:, :], in0=acc.t[:, :], in1=par.t[:, b0:b1], op=ALU.add), reads=[acc, par], writes=[acc])
                if last:
                    mk.dma('sp', lambda e: e.dma_start(out=dr['out'].t[tt * 128:(tt + 1) * 128, :], in_=acc.t[:, :]), reads=[acc], writes=[dr['out']])
                else:
                    bt, bb2 = big[tt % 2]
                    for m in range(8):
                        mk.op('pe', lambda e: e.transpose(out=bt[:, m * 128:(m + 1) * 128], in_=acc.t[:, m * 128:(m + 1) * 128], identity=c['ident_f'].t[:, :]),
                              reads=[acc, c['ident_f']], writes=[bb2[m // 4]])
                    mk.op('act', lambda e: e.activation(out=xo.t[:, :, :], in_=bt[:, :].rearrange("p (c r) -> p c r", c=8), func=AF.Copy), reads=bb2, writes=[xo])
                    mk.dma('sp', lambda e: e.dma_start(out=x2Tv[:, :, tt * 128:(tt + 1) * 128], in_=xo.t[:, :, :]), reads=[xo], writes=[dr['x2T']])
            mk.barrier()


def host_prep(inputs, NS=4, experts=True):
    x = np.asarray(inputs['x'])
    cst = make_consts()
    parA = np.zeros((2, 128, NPA), np.float32)
    parB = np.zeros((2, 128, NPB), np.float32)
    for l in range(2):
        parA[l, :, 0:8] = np.asarray(inputs['ln1_g'][l]).reshape(8, 128).T
        parA[l, :, 8:16] = np.asarray(inputs['ln1_b'][l]).reshape(8, 128).T
        parA[l, 0:4, 16] = np.asarray(inputs['b_forget'][l])
        parA[l, :, 20:148] = np.asarray(inputs['diff_lambda'][l]).reshape(1, 128)
        parA[l, 0:64, 148] = np.asarray(inputs['diff_subln'][l])
        parB[l, :, 0:256] = np.asarray(inputs['router_bias'][l])[None, :]
        parB[l, :, 256:1280] = np.asarray(inputs['ln2_g'][l])[None, :]
        parB[l, :, 1280:2304] = np.asarray(inputs['ln2_b'][l])[None, :]
    shared = {k: np.ascontiguousarray(np.asarray(inputs[k], dtype=np.float32)) for k in
              ('w_in', 'w_branch', 'w_out', 'w_router', 'w_exp_gate', 'w_exp_up', 'w_exp_down', 'w_sh_gate', 'w_sh_up', 'w_sh_down')}
    if not experts:
        for k in ('w_exp_gate', 'w_exp_up', 'w_exp_down'):
            shared.pop(k)
    shared.update(cst=cst, parA=parA, parB=parB)
    maps = []
    for core in range(8):
        xs = x[core * 4:core * 4 + NS]
        xT = np.ascontiguousarray(xs.reshape(NS * T, D).T)
        m = dict(shared)
        m['xT'] = xT
        maps.append(m)
    return maps


_CACHE = {}


def kernel(**inputs):
    cfg = {}
    if 'nc' not in _CACHE:
        _CACHE['nc'] = build(cfg)
    nc, K = _CACHE['nc']
    maps = host_prep(inputs)
    res = run_bass_kernel_spmd(nc, maps, core_ids=list(range(8)))
    out = np.concatenate([np.asarray(r['out']).reshape(4, T, D) for r in res.results], axis=0)
    return out.astype(np.float32)
```

```python
import os
import numpy as np
from contextlib import ExitStack
import concourse.bass as bass
import concourse.mybir as mybir
from concourse.bass_utils import run_bass_kernel_spmd

F32 = mybir.dt.float32
BF16 = mybir.dt.bfloat16
I32 = mybir.dt.int32
U32 = mybir.dt.uint32
AF = mybir.ActivationFunctionType
ALU = mybir.AluOpType
AX = mybir.AxisListType

T = 2048
D = 1024
NEG = -30000.0
ALPHA = 4 ** 0.25
LN_EPS = 1e-5
NBIS = 18
CAP = 384
NE = 256

OFF = {}
_o = 0
for _n, _w in (('moba_q', 256), ('moba_k', 256), ('moba_v', 256), ('diff_q', 256), ('diff_k', 256), ('diff_v', 256),
               ('fox_q', 256), ('fox_k', 256), ('fox_v', 256), ('fox_f', 4), ('dsa_q', 256), ('dsa_k', 256), ('dsa_v', 256),
               ('idx_q', 256), ('idx_k', 64), ('idx_w', 4), ('gates', 4096)):
    OFF[_n] = _o
    _o += _w
WIN = _o

CC = {}
_o = 0
for _n, _w in (('C64', T), ('S64', T), ('C32', T), ('S32', T), ('R64', 128), ('R32', 128), ('ident', 128),
               ('tribias', 128), ('tribig', 128), ('esel', 1024), ('pow2', 32), ('ones', 128), ('eh4', 512), ('mask01', 2), ('lstrict', 128), ('ebase', 256)):
    CC[_n] = (_o, _o + _w)
    _o += _w
NCST = _o

PA = {'ln1_g': (0, 8), 'ln1_b': (8, 16), 'negb': (16, 17), 'dl': (20, 148), 'subln': (148, 149)}
NPA = 160
PB = {'rbias': (0, 256), 'ln2_g': (256, 1280), 'ln2_b': (1280, 2304)}
NPB = 2304


def make_consts():
    c = np.zeros((128, NCST), np.float32)
    pos = np.arange(T, dtype=np.float32)
    p = np.arange(128)
    inv64 = (10000.0 ** (-np.arange(32, dtype=np.float32) / 32)).astype(np.float32)
    inv32 = (10000.0 ** (-np.arange(16, dtype=np.float32) / 16)).astype(np.float32)
    a64 = (pos[None, :] * inv64[p % 32][:, None]).astype(np.float32)
    a32 = (pos[None, :] * inv32[p % 16][:, None]).astype(np.float32)
    c[:, CC['C64'][0]:CC['C64'][1]] = np.cos(a64.astype(np.float64))
    c[:, CC['S64'][0]:CC['S64'][1]] = np.sin(a64.astype(np.float64))
    c[:, CC['C32'][0]:CC['C32'][1]] = np.cos(a32.astype(np.float64))
    c[:, CC['S32'][0]:CC['S32'][1]] = np.sin(a32.astype(np.float64))
    R64 = np.zeros((128, 128), np.float32)
    R32 = np.zeros((128, 128), np.float32)
    for m in range(128):
        if m % 64 < 32:
            R64[m + 32, m] = -1.0
        else:
            R64[m - 32, m] = 1.0
        if m % 32 < 16:
            R32[m + 16, m] = -1.0
        else:
            R32[m - 16, m] = 1.0
    c[:, CC['R64'][0]:CC['R64'][1]] = R64
    c[:, CC['R32'][0]:CC['R32'][1]] = R32
    c[:, CC['ident'][0]:CC['ident'][1]] = np.eye(128, dtype=np.float32)
    r = np.arange(128)[:, None]
    q = np.arange(128)[None, :]
    c[:, CC['tribias'][0]:CC['tribias'][1]] = np.where(q < r, NEG, 0.0)
    c[:, CC['tribig'][0]:CC['tribig'][1]] = np.where(q > r, -1e30, 0.0)
    es = np.zeros((128, 8, 128), np.float32)
    for n in range(8):
        es[n, n, :] = 1.0
    c[:, CC['esel'][0]:CC['esel'][1]] = es.reshape(128, 1024)
    c[:, CC['pow2'][0]:CC['pow2'][1]] = (0.5 ** (np.arange(32) + 1))[None, :]
    c[:, CC['ones'][0]:CC['ones'][1]] = 1.0
    e4 = np.zeros((128, 4, 128), np.float32)
    for n in range(4):
        e4[n, n, :] = 1.0
    c[:, CC['eh4'][0]:CC['eh4'][1]] = e4.reshape(128, 512)
    c[:, CC['mask01'][0]] = (np.arange(128) % 64 < 32)
    c[:, CC['lstrict'][0]:CC['lstrict'][1]] = (np.arange(128)[:, None] < np.arange(128)[None, :])
    c[:, CC['ebase'][0]:CC['ebase'][1]] = (np.arange(256) * CAP + 1)[None, :]
    c[:, CC['mask01'][0] + 1] = (np.arange(128) % 64 >= 32)
    return c


class Buf:
    __slots__ = ('t', 'w', 'r', 'name', 'x')

    def __init__(self, t, name='', x=False):
        self.t = t
        self.w = None
        self.r = {}
        self.name = name
        self.x = x


class Ring:
    def __init__(self, bufs):
        self.b = bufs
        self.i = 0

    def next(self):
        b = self.b[self.i % len(self.b)]
        self.i += 1
        return b


class MK:
    def __init__(self, nc, es):
        self.nc = nc
        self.E = dict(pe=nc.tensor, act=nc.scalar, dve=nc.vector, pool=nc.gpsimd, sp=nc.sync)
        self.semobj = {}
        self.cnt = {}
        for k in self.E:
            self.semobj[k] = es.enter_context(nc.semaphore('s_' + k))
            self.cnt[k] = 0
        self.seen = {}
        self.dq = {}
        for q, n in (('sp', 28), ('pool', 28), ('act', 8)):
            keys = []
            for i in range(n):
                key = ('d', q, i)
                self.semobj[key] = es.enter_context(nc.semaphore('d_%s%d' % (q, i)))
                self.cnt[key] = 0
                keys.append(key)
            self.dq[q] = [keys, 0]
        self.ninst = 0

    def _wait(self, eng, key, val):
        if self.seen.get((eng, key), 0) >= val:
            return
        self.E[eng].wait_ge(self.semobj[key], val)
        self.seen[(eng, key)] = val

    def _deps(self, eng, reads, writes):
        need = {}
        for b in reads:
            if b.w is not None:
                k, v = b.w
                need[k] = max(need.get(k, 0), v)
            if b.x:
                for k, v in b.r.items():
                    if k != eng:
                        need[k] = max(need.get(k, 0), v)
        for b in writes:
            if b.w is not None:
                k, v = b.w
                if k != eng:
                    need[k] = max(need.get(k, 0), v)
            for k, v in b.r.items():
                if k != eng:
                    need[k] = max(need.get(k, 0), v)
        if eng == 'pe':
            need.pop('pe', None)
        for k, v in need.items():
            self._wait(eng, k, v)

    def _commit(self, tok, reads, writes):
        k, v = tok
        for b in reads:
            b.r[k] = max(b.r.get(k, 0), v)
        for b in writes:
            b.w = tok
            b.r = {}

    def op(self, eng, fn, reads=(), writes=()):
        self._deps(eng, reads, writes)
        ins = fn(self.E[eng])
        self.cnt[eng] += 1
        ins.then_inc(self.semobj[eng], 1)
        tok = (eng, self.cnt[eng])
        self._commit(tok, reads, writes)
        self.ninst += 1
        return tok

    def dma(self, q, fn, reads=(), writes=()):
        self._deps(q, reads, writes)
        keys, i = self.dq[q]
        key = keys[i % len(keys)]
        self.dq[q][1] = i + 1
        if self.cnt[key] > 0:
            self._wait(q, key, self.cnt[key])
        ins = fn(self.E[q])
        self.cnt[key] += 16
        ins.then_inc(self.semobj[key], 16)
        tok = (key, self.cnt[key])
        self._commit(tok, reads, writes)
        self.ninst += 1
        return tok

    def barrier(self, bufs=()):
        for eng in self.E:
            for k, v in self.cnt.items():
                if v > 0:
                    self._wait(eng, k, v)

    def wait_all_on(self, eng):
        for k, v in self.cnt.items():
            if v > 0:
                self._wait(eng, k, v)


class Ctx:
    pass


def build(cfg):
    NS = cfg.get('nseq', 4)
    NL = cfg.get('nlayer', 2)
    dbg = cfg.get('dbg', None)
    NT = NS * T
    nc = bass.Bass("TRN2", target_bir_lowering=False)
    K = Ctx()
    K.nc = nc
    K.cfg = cfg
    K.NS, K.NL, K.NT = NS, NL, NT
    dr = {}

    def dram(name, shape, dt, kind):
        dr[name] = Buf(nc.dram_tensor(name, shape, dt, kind=kind).ap(), name)
        return dr[name]

    dram('xT', [D, NT], F32, "ExternalInput")
    dram('w_in', [2, D, WIN], F32, "ExternalInput")
    dram('w_branch', [2, 4, 256, D], F32, "ExternalInput")
    dram('w_out', [2, D, D], F32, "ExternalInput")
    dram('w_router', [2, D, NE], F32, "ExternalInput")
    if cfg.get('stageB', True):
        NEX = cfg.get('ne', NE)
        dram('w_exp_gate', [2, NEX, D, 256], F32, "ExternalInput")
        dram('w_exp_up', [2, NEX, D, 256], F32, "ExternalInput")
        dram('w_exp_down', [2, NEX, 256, D], F32, "ExternalInput")
    dram('w_sh_gate', [2, D, 256], F32, "ExternalInput")
    dram('w_sh_up', [2, D, 256], F32, "ExternalInput")
    dram('w_sh_down', [2, 256, D], F32, "ExternalInput")
    dram('cst', [128, NCST], F32, "ExternalInput")
    dram('parA', [2, 128, NPA], F32, "ExternalInput")
    dram('parB', [2, 128, NPB], F32, "ExternalInput")
    dram('out', [NT, D], F32, "ExternalOutput")
    dram('x1T', [D, NT], F32, "Internal")
    dram('x1r', [NT, D], F32, "ExternalOutput" if dbg else "Internal")
    dram('x1rb', [NT + 128, D], BF16, "Internal")
    dram('x2T', [D, NT], F32, "Internal")
    dram('tokw', [NE * CAP, 2], I32, "Internal")
    dram('ye', [NE * CAP, D], BF16, "Internal")
    if dbg:
        for name, shape in dbg.items():
            dram(name, shape, F32, "ExternalOutput")
    K.dr = dr

    with ExitStack() as es:
        mk = MK(nc, es)
        K.mk = mk
        K.es = es

        K.uid = 0

        def sb(name, shape, dt, st=None):
            K.uid += 1
            nm = "%s_%d" % (name, K.uid)
            return Buf((st or es).enter_context(nc.sbuf_tensor(nm, shape, dt)), nm)

        K.sb = sb
        K.dumps = {}

        def dump(name, buf, ap, shape):
            if not cfg.get('dbg'):
                return
            if name in K.dumps:
                return
            d_ = Buf(nc.dram_tensor(name, shape, F32, kind="ExternalOutput").ap(), name)
            K.dumps[name] = d_
            mk.dma('pool', lambda e: e.dma_start(out=d_.t, in_=ap), reads=[buf], writes=[d_])

        K.dump = dump
        K.pst = [es.enter_context(nc.psum_tensor("ps%d" % i, [128, 1024], F32)) for i in range(4)]
        K.ps = []
        for i in range(4):
            for h in range(2):
                K.ps.append(Buf(K.pst[i][:, h * 512:(h + 1) * 512], "ps%d_%d" % (i, h), x=True))
        setup_consts(K)
        for l in range(NL):
            src = dr['xT'] if l == 0 else dr['x2T']
            if cfg.get('stageA', True):
                for s in range(NS):
                    stage_a(K, l, s, src)
            if cfg.get('stageB', True):
                stage_b(K, l, last=(l == NL - 1))
        mk.wait_all_on('sp')
        mk.wait_all_on('pool')
    K.ninst = mk.ninst
    return nc, K


def setup_consts(K):
    mk, sb, dr = K.mk, K.sb, K.dr
    cst = dr['cst']
    K.c = {}
    big = ('C64', 'S64', 'C32', 'S32')
    small = ('R64', 'R32', 'ident', 'tribias', 'esel', 'ones', 'eh4', 'lstrict')
    for name in big + small:
        a, b = CC[name]
        K.c[name] = sb('c_' + name, [128, b - a], BF16)
    for name in ('ident', 'tribig', 'pow2', 'ones', 'mask01', 'ebase'):
        a, b = CC[name]
        dst = sb('cf_' + name, [128, b - a], F32)
        mk.dma('sp', lambda e: e.dma_start(out=dst.t[:, :], in_=cst.t[:, a:b]), reads=[cst], writes=[dst])
        K.c[name + '_f'] = dst
    with ExitStack() as st:
        tmp = sb('cst_tmp', [128, T], F32, st)
        for name in big + small:
            a, b = CC[name]
            w = b - a
            dst = K.c[name]
            mk.dma('sp', lambda e: e.dma_start(out=tmp.t[:, 0:w], in_=cst.t[:, a:b]), reads=[cst], writes=[tmp])
            mk.op('act', lambda e: e.activation(out=dst.t[:, :], in_=tmp.t[:, 0:w], func=AF.Copy), reads=[tmp], writes=[dst])
        mk.barrier()


def stage_a(K, l, s, src):
    mk, sb, dr, nc = K.mk, K.sb, K.dr, K.nc
    ps = K.ps
    c = K.c
    w_in = dr['w_in']
    w_in_v = w_in.t[l].rearrange("(c p) n -> p c n", p=128)
    t0g = s * T
    lam_init = 0.8 - 0.6 * float(np.exp(-0.3 * l))
    dbg = K.cfg.get('dbg') or {}
    branches = K.cfg.get('branches', (0, 1, 2, 3))

    def loadw(wb, col0, ncol, dcol=0):
        mk.dma('pool', lambda e: e.dma_start(out=wb.t[:, :, dcol:dcol + ncol], in_=w_in_v[:, :, col0:col0 + ncol]),
               reads=[w_in], writes=[wb])

    with ExitStack() as sa:
        xb = sb('xb', [128, 8, T], BF16, sa)
        par = sb('parA', [128, NPA], F32, sa)
        oT = [sb('oT%d' % i, [128, 2, T], BF16, sa) for i in range(4)]
        mk.dma('sp', lambda e: e.dma_start(out=par.t[:, :], in_=dr['parA'].t[l]), reads=[dr['parA']], writes=[par])
        srcv = src.t.rearrange("(c p) n -> p c n", p=128)
        for cc in range(8):
            mk.dma('pool', lambda e: e.dma_start(out=xb.t[:, cc, :], in_=srcv[:, cc, t0g:t0g + T]), reads=[src], writes=[xb])

        with ExitStack() as sbr:
            wq = Ring([sb('wq%d' % i, [128, 8, 256], BF16, sbr) for i in range(2)])
            qT = sb('qT', [128, 2, T], BF16, sbr)
            kT = sb('kT', [128, 2, T], BF16, sbr)
            Vp = sb('Vp', [128, 16, 4, 128], BF16, sbr)
            ptr = Ring([sb('pt%d' % i, [128, 512], BF16, sbr) for i in range(4)])
            tq = Ring([sb('tq%d' % i, [128, 512], BF16, sbr) for i in range(2)])
            t1r = Ring([sb('t1_%d' % i, [128, 512], F32, sbr) for i in range(2)])
            t2r = Ring([sb('t2_%d' % i, [128, 512], F32, sbr) for i in range(2)])
            rsr = Ring([sb('rs%d' % i, [64, 512], F32, sbr) for i in range(2)])
            small = sb('small', [128, 64], F32, sbr)
            pp = Ring([ps[0], ps[1], ps[2], ps[3]])
            sr = Ring([ps[4], ps[5], ps[2], ps[3]])
            accr = Ring([ps[6], ps[7]])
            mk.op('pool', lambda e: e.memset(Vp.t[:, :, :, 64:128], 1.0), writes=[Vp])

            def proj_fm(col0, nchunk, dst, rope=None, wb=None, wcol0=0):
                if wb is None:
                    wb = wq.next()
                    loadw(wb, col0, nchunk * 128)
                for m in range(nchunk):
                    for tc in range(4):
                        pb = pp.next()
                        for cc in range(8):
                            mk.op('pe', lambda e: e.matmul(pb.t[:, :], lhsT=wb.t[:, cc, wcol0 + m * 128:wcol0 + (m + 1) * 128],
                                                           rhs=xb.t[:, cc, tc * 512:(tc + 1) * 512], start=(cc == 0), stop=(cc == 7)),
                                  reads=[wb, xb], writes=[pb])
                        dsl = dst.t[:, m, tc * 512:(tc + 1) * 512]
                        if rope is None:
                            mk.op('act', lambda e: e.activation(out=dsl, in_=pb.t[:, :], func=AF.Copy), reads=[pb], writes=[dst])
                        else:
                            Cb, Sb, Rb = rope
                            qs = tq.next()
                            mk.op('act', lambda e: e.activation(out=qs.t[:, :], in_=pb.t[:, :], func=AF.Copy), reads=[pb], writes=[qs])
                            rp = pp.next()
                            mk.op('pe', lambda e: e.matmul(rp.t[:, :], lhsT=Rb.t[:, :], rhs=qs.t[:, :], start=True, stop=True),
                                  reads=[Rb, qs], writes=[rp])
                            t1 = t1r.next()
                            t2 = t2r.next()
                            mk.op('dve', lambda e: e.tensor_tensor(out=t1.t[:, :], in0=rp.t[:, :], in1=Sb.t[:, tc * 512:(tc + 1) * 512], op=ALU.mult),
                                  reads=[rp, Sb], writes=[t1])
                            mk.op('pool', lambda e: e.tensor_tensor(out=t2.t[:, :], in0=qs.t[:, :], in1=Cb.t[:, tc * 512:(tc + 1) * 512], op=ALU.mult),
                                  reads=[qs, Cb], writes=[t2])
                            mk.op('dve', lambda e: e.tensor_tensor(out=dsl, in0=t1.t[:, :], in1=t2.t[:, :], op=ALU.add),
                                  reads=[t1, t2], writes=[dst])

            def proj_v(col0):
                wb = wq.next()
                loadw(wb, col0, 256)
                for tt in range(16):
                    pb = pp.next()
                    for cc in range(8):
                        mk.op('pe', lambda e: e.matmul(pb.t[:, 0:256], lhsT=xb.t[:, cc, tt * 128:(tt + 1) * 128], rhs=wb.t[:, cc, 0:256],
                                                       start=(cc == 0), stop=(cc == 7)), reads=[wb, xb], writes=[pb])
                    mk.op('act', lambda e: e.activation(out=Vp.t[:, tt, :, 0:64], in_=pb.t[:, 0:256].rearrange("p (h d) -> p h d", h=4), func=AF.Copy),
                          reads=[pb], writes=[Vp])

            def attention(nsm, kdim, qk_ops, bias_fn, act_bias_fn, vhead_fn, fin_fn, scale, cq_outer=False, kbuf_fn=None):
                steps = []
                if cq_outer:
                    for cq in range(4):
                        for s_ in range(nsm):
                            for j in range(4 * cq + 4):
                                steps.append((s_, cq, j))
                else:
                    for s_ in range(nsm):
                        for cq in range(4):
                            for j in range(4 * cq + 4):
                                steps.append((s_, cq, j))
                pend = []
                accs = {}
                for (s_, cq, j) in steps:
                    ch, base = qk_ops(s_)
                    q0 = max(512 * cq, 128 * j)
                    N = 512 * (cq + 1) - q0
                    off = q0 - 512 * cq
                    sbk = sr.next()
                    kb_ = kT if kbuf_fn is None else kbuf_fn(s_)
                    mk.op('pe', lambda e: e.matmul(sbk.t[:, 0:N], lhsT=kb_.t[base:base + kdim, ch, j * 128:(j + 1) * 128],
                                                   rhs=qT.t[base:base + kdim, ch, q0:q0 + N], start=True, stop=False, skip_group_check=True),
                          reads=[kb_, qT], writes=[sbk])
                    bias_fn(s_, cq, j, q0, N, sbk)
                    if len(pend) >= 2:
                        pend.pop(0)()
                    if j == 0:
                        accs[(s_, cq)] = accr.next()
                    acc = accs[(s_, cq)]
                    pt = ptr.next()
                    ab = act_bias_fn(s_, j)

                    if ab is None:
                        mk.op('act', lambda e: e.activation(out=pt.t[:, 0:N], in_=sbk.t[:, 0:N], func=AF.Exp, scale=scale),
                              reads=[sbk], writes=[pt])
                    else:
                        abuf, aap = ab
                        mk.op('act', lambda e: e.activation(out=pt.t[:, 0:N], in_=sbk.t[:, 0:N], func=AF.Exp, scale=scale, bias=aap),
                              reads=[sbk, abuf], writes=[pt])

                    def do_exp_pv(s_=s_, cq=cq, j=j, N=N, off=off, sbk=sbk, pt=pt, acc=acc, ab=ab):
                        h = vhead_fn(s_)
                        mk.op('pe', lambda e: e.matmul(acc.t[:, off:off + N], lhsT=Vp.t[:, j, h, :], rhs=pt.t[:, 0:N],
                                                       start=(j == 0), stop=(j == 4 * cq + 3), skip_group_check=True),
                              reads=[Vp, pt], writes=[acc])
                        if j == 4 * cq + 3:
                            fin_fn(s_, cq, acc)
                    pend.append(do_exp_pv)
                while pend:
                    pend.pop(0)()

            def causal_bias(sbk, j, cq, q0):
                if j >= 4 * cq:
                    mk.op('pe', lambda e: e.matmul(sbk.t[:, 0:128], lhsT=c['ident'].t[:, :], rhs=c['tribias'].t[:, :],
                                                   start=False, stop=True, skip_group_check=True),
                          reads=[c['ident'], c['tribias']], writes=[sbk])

            def fin_std(dst):
                def f(s_, cq, acc):
                    rs = rsr.next()
                    mk.op('act', lambda e: e.activation(out=rs.t[0:64, :], in_=acc.t[64:128, :], func=AF.Copy), reads=[acc], writes=[rs])
                    mk.op('dve', lambda e: e.reciprocal(out=rs.t[0:64, :], in_=rs.t[0:64, :]), reads=[rs], writes=[rs])
                    pb = (s_ % 2) * 64
                    mk.op('dve', lambda e: e.tensor_tensor(out=dst.t[pb:pb + 64, s_ // 2, cq * 512:(cq + 1) * 512], in0=acc.t[0:64, :],
                                                           in1=rs.t[0:64, :], op=ALU.mult), reads=[acc, rs], writes=[dst])
                return f

            if 0 in branches:
                rope64 = (c['C64'], c['S64'], c['R64'])
                proj_fm(OFF['moba_q'], 2, qT, rope64)
                proj_fm(OFF['moba_k'], 2, kT, rope64)
                proj_v(OFF['moba_v'])
                with ExitStack() as sm:
                    kmf = sb('kmf', [128, 2, 8], F32, sm)
                    kmb = sb('kmb', [128, 2, 8], BF16, sm)
                    biasT = sb('biasT', [8, 4, 1024], BF16, sm)
                    gm = sb('gm', [128, 8], F32, sm)
                    t8 = sb('t8', [128, 8], F32, sm)
                    bq = sb('bq', [128, 8], F32, sm)
                    mk.op('dve', lambda e: e.tensor_reduce(out=kmf.t[:, :, :], in_=kT.t[:, :, :].rearrange("p c (n k) -> p c n k", k=256),
                                                           axis=AX.X, op=ALU.add), reads=[kT], writes=[kmf])
                    mk.op('dve', lambda e: e.tensor_scalar(out=kmb.t[:, :, :], in0=kmf.t[:, :, :], scalar1=1.0 / 256, scalar2=None, op0=ALU.mult),
                          reads=[kmf], writes=[kmb])
                    for h in range(4):
                        ch, base = h // 2, (h % 2) * 64
                        for qt in range(8, 16):
                            cur = qt // 2
                            gp = pp.next()
                            mk.op('pe', lambda e: e.matmul(gp.t[:, 0:8], lhsT=qT.t[base:base + 64, ch, qt * 128:(qt + 1) * 128],
                                                           rhs=kmb.t[base:base + 64, ch, :], start=True, stop=True), reads=[qT, kmb], writes=[gp])
                            mk.op('dve', lambda e: e.memset(gm.t[:, :], -1e30), writes=[gm])
                            mk.op('dve', lambda e: e.tensor_copy(out=gm.t[:, 0:cur], in_=gp.t[:, 0:cur]), reads=[gp], writes=[gm])
                            mk.op('dve', lambda e: e.max(out=t8.t[:, :], in_=gm.t[:, :]), reads=[gm], writes=[t8])
                            mk.op('dve', lambda e: e.tensor_scalar(out=bq.t[:, :], in0=gm.t[:, :], scalar1=t8.t[:, 2:3], scalar2=NEG,
                                                                   op0=ALU.is_lt, op1=ALU.mult), reads=[gm, t8], writes=[bq])
                            mk.op('dve', lambda e: e.memset(bq.t[:, cur:8], 0.0), writes=[bq])
                            tp = pp.next()
                            mk.op('pe', lambda e: e.transpose(out=tp.t[0:8, 0:128], in_=bq.t[:, :], identity=c['ident_f'].t[:, :]),
                                  reads=[bq, c['ident_f']], writes=[tp])
                            mk.op('act', lambda e: e.activation(out=biasT.t[0:8, h, (qt - 8) * 128:(qt - 7) * 128], in_=tp.t[0:8, 0:128], func=AF.Copy),
                                  reads=[tp], writes=[biasT])

                    K.dump('d_biasT', biasT, biasT.t[0:8, :, :], [8, 4, 1024])
                    K.dump('d_kmb', kmb, kmb.t[:, :, :], [128, 2, 8])

                    def moba_bias(s_, cq, j, q0, N, sbk):
                        if cq >= 2:
                            n = j // 2
                            mk.op('pe', lambda e: e.matmul(sbk.t[:, 0:N], lhsT=c['esel'].t[0:8, n * 128:(n + 1) * 128],
                                                           rhs=biasT.t[0:8, s_, q0 - 1024:q0 - 1024 + N], start=False, stop=(j < 4 * cq),
                                                           skip_group_check=True), reads=[c['esel'], biasT], writes=[sbk])
                        causal_bias(sbk, j, cq, q0)

                    attention(4, 64, lambda s_: (s_ // 2, (s_ % 2) * 64), moba_bias, lambda s_, j: None, lambda s_: s_, fin_std(oT[0]), 0.125)

            if 1 in branches:
                rope32 = (c['C32'], c['S32'], c['R32'])
                proj_fm(OFF['diff_q'], 2, qT, rope32)
                proj_fm(OFF['diff_k'], 2, kT, rope32)
                proj_v(OFF['diff_v'])
                with ExitStack() as sm:
                    lamt = sb('lamt', [128, 8], F32, sm)
                    dtmp = sb('dtmp', [128, 128], F32, sm)
                    hold = sb('hold', [64, 512], F32, sm)
                    dd = sb('dd', [64, 512], F32, sm)
                    sq = sb('sq', [64, 512], F32, sm)
                    gs = sb('gs', [64, 1], F32, sm)
                    kT1 = sb('kT1', [128, 2, T], BF16, sm)
                    mk.op('pool', lambda e: e.tensor_scalar(out=kT1.t[:, :, :], in0=kT.t[:, :, :], scalar1=c['mask01_f'].t[:, 1:2], scalar2=None, op0=ALU.mult),
                          reads=[kT, c['mask01_f']], writes=[kT1])
                    mk.op('dve', lambda e: e.tensor_scalar(out=kT.t[:, :, :], in0=kT.t[:, :, :], scalar1=c['mask01_f'].t[:, 0:1], scalar2=None, op0=ALU.mult),
                          reads=[kT, c['mask01_f']], writes=[kT])
                    a, b = PA['dl']
                    mk.op('dve', lambda e: e.tensor_tensor(out=dtmp.t[:, 0:32], in0=par.t[:, a:a + 32], in1=par.t[:, a + 32:a + 64], op=ALU.mult),
                          reads=[par], writes=[dtmp])
                    mk.op('dve', lambda e: e.tensor_tensor(out=dtmp.t[:, 32:64], in0=par.t[:, a + 64:a + 96], in1=par.t[:, a + 96:a + 128], op=ALU.mult),
                          reads=[par], writes=[dtmp])
                    mk.op('dve', lambda e: e.tensor_reduce(out=lamt.t[:, 0:2], in_=dtmp.t[:, 0:64].rearrange("p (a b) -> p a b", a=2), axis=AX.X, op=ALU.add),
                          reads=[dtmp], writes=[lamt])
                    mk.op('act', lambda e: e.activation(out=lamt.t[:, 2:4], in_=lamt.t[:, 0:2], func=AF.Exp), reads=[lamt], writes=[lamt])
                    mk.op('dve', lambda e: e.tensor_tensor(out=lamt.t[:, 4:5], in0=lamt.t[:, 3:4], in1=lamt.t[:, 2:3], op=ALU.subtract),
                          reads=[lamt], writes=[lamt])
                    mk.op('dve', lambda e: e.tensor_scalar(out=lamt.t[:, 5:6], in0=lamt.t[:, 4:5], scalar1=-lam_init, scalar2=None, op0=ALU.add),
                          reads=[lamt], writes=[lamt])
                    sa_, sb_ = PA['subln']
                    mk.op('dve', lambda e: e.tensor_scalar(out=gs.t[:, :], in0=par.t[0:64, sa_:sb_], scalar1=(1.0 - lam_init), scalar2=None, op0=ALU.mult),
                          reads=[par], writes=[gs])

                    K.dump('d_lamt', lamt, lamt.t[:, :], [128, 8])
                    K.dump('d_gs', gs, gs.t[:, :], [64, 1])

                    def fin_diff(s_, cq, acc):
                        h, comp = s_ // 2, s_ % 2
                        rs = rsr.next()
                        mk.op('act', lambda e: e.activation(out=rs.t[0:64, :], in_=acc.t[64:128, :], func=AF.Copy), reads=[acc], writes=[rs])
                        mk.op('dve', lambda e: e.reciprocal(out=rs.t[0:64, :], in_=rs.t[0:64, :]), reads=[rs], writes=[rs])
                        if comp == 0:
                            mk.op('dve', lambda e: e.tensor_tensor(out=hold.t[:, :], in0=acc.t[0:64, :], in1=rs.t[0:64, :], op=ALU.mult),
                                  reads=[acc, rs], writes=[hold])
                            return
                        mk.op('dve', lambda e: e.tensor_tensor(out=dd.t[:, :], in0=acc.t[0:64, :], in1=rs.t[0:64, :], op=ALU.mult),
                              reads=[acc, rs], writes=[dd])
                        mk.op('dve', lambda e: e.scalar_tensor_tensor(out=dd.t[:, :], in0=dd.t[:, :], scalar=lamt.t[0:64, 5:6], in1=hold.t[:, :],
                                                                      op0=ALU.mult, op1=ALU.add), reads=[dd, lamt, hold], writes=[dd])
                        K.dump('d_hold', hold, hold.t[:, :], [64, 512])
                        K.dump('d_dd', dd, dd.t[:, :], [64, 512])
                        mk.op('pool', lambda e: e.tensor_tensor(out=sq.t[:, :], in0=dd.t[:, :], in1=dd.t[:, :], op=ALU.mult), reads=[dd], writes=[sq])
                        mp = pp.next()
                        mk.op('pe', lambda e: e.matmul(mp.t[0:64, :], lhsT=c['ones_f'].t[0:64, 0:64], rhs=sq.t[:, :], start=True, stop=True),
                              reads=[c['ones_f'], sq], writes=[mp])
                        mk.op('act', lambda e: e.activation(out=sq.t[:, :], in_=mp.t[0:64, :], func=AF.Ln, scale=1.0 / 64, bias=LN_EPS), reads=[mp], writes=[sq])
                        mk.op('act', lambda e: e.activation(out=sq.t[:, :], in_=sq.t[:, :], func=AF.Exp, scale=-0.5), reads=[sq], writes=[sq])
                        pb = (h % 2) * 64
                        mk.op('dve', lambda e: e.scalar_tensor_tensor(out=oT[1].t[pb:pb + 64, h // 2, cq * 512:(cq + 1) * 512], in0=dd.t[:, :],
                                                                      scalar=gs.t[:, 0:1], in1=sq.t[:, :], op0=ALU.mult, op1=ALU.mult),
                              reads=[dd, gs, sq], writes=[oT[1]])

                    def diff_bias(s_, cq, j, q0, N, sbk):
                        causal_bias(sbk, j, cq, q0)

                    attention(8, 64, lambda s_: (s_ // 4, ((s_ // 2) % 2) * 64), diff_bias, lambda s_, j: None, lambda s_: s_ // 2, fin_diff,
                              float(32 ** -0.5), cq_outer=True, kbuf_fn=lambda s_: (kT if s_ % 2 == 0 else kT1))

            if 2 in branches:
                proj_fm(OFF['fox_q'], 2, qT, None)
                proj_fm(OFF['fox_k'], 2, kT, None)
                proj_v(OFF['fox_v'])
                with ExitStack() as sm:
                    wf = sb('wf', [128, 8, 4], BF16, sm)
                    ncT = sb('ncT', [4, T], F32, sm)
                    e1 = sb('e1', [4, T], F32, sm)
                    cq8 = sb('cq8', [4, T], BF16, sm)
                    nck = sb('nck', [128, 16, 4], F32, sm)
                    loadw(wf, OFF['fox_f'], 4)
                    na, nb = PA['negb']
                    mk.op('dve', lambda e: e.tensor_scalar(out=small.t[0:4, 0:1], in0=par.t[0:4, na:nb], scalar1=-1.0, scalar2=None, op0=ALU.mult),
                          reads=[par], writes=[small])
                    for tc in range(4):
                        fp = pp.next()
                        for cc in range(8):
                            mk.op('pe', lambda e: e.matmul(fp.t[0:4, :], lhsT=wf.t[:, cc, 0:4], rhs=xb.t[:, cc, tc * 512:(tc + 1) * 512],
                                                           start=(cc == 0), stop=(cc == 7)), reads=[wf, xb], writes=[fp])
                        mk.op('act', lambda e: e.activation(out=e1.t[:, tc * 512:(tc + 1) * 512], in_=fp.t[0:4, :], func=AF.Exp, scale=-1.0,
                                                            bias=small.t[0:4, 0:1]), reads=[fp, small], writes=[e1])
                    mk.op('act', lambda e: e.activation(out=e1.t[:, :], in_=e1.t[:, :], func=AF.Ln, scale=1.0, bias=1.0), reads=[e1], writes=[e1])
                    mk.op('dve', lambda e: e.tensor_tensor_scan(out=ncT.t[:, :], data0=c['ones_f'].t[0:4, 0:1].to_broadcast([4, T]),
                                                                data1=e1.t[:, :], initial=0.0, op0=ALU.mult, op1=ALU.add),
                          reads=[e1, c['ones_f']], writes=[ncT])
                    mk.op('dve', lambda e: e.tensor_scalar(out=cq8.t[:, :], in0=ncT.t[:, :], scalar1=-8.0, scalar2=None, op0=ALU.mult),
                          reads=[ncT], writes=[cq8])
                    tp = pp.next()
                    for tt in range(16):
                        mk.op('pe', lambda e: e.transpose(out=tp.t[:, tt * 4:(tt + 1) * 4], in_=ncT.t[0:4, tt * 128:(tt + 1) * 128],
                                                          identity=c['ident_f'].t[0:4, 0:4]), reads=[ncT, c['ident_f']], writes=[tp])
                    mk.op('act', lambda e: e.activation(out=nck.t[:, :, :], in_=tp.t[:, 0:64].rearrange("p (a b) -> p a b", b=4), func=AF.Copy),
                          reads=[tp], writes=[nck])

                    def fox_bias(s_, cq, j, q0, N, sbk):
                        mk.op('pe', lambda e: e.matmul(sbk.t[:, 0:N], lhsT=c['eh4'].t[0:4, s_ * 128:(s_ + 1) * 128], rhs=cq8.t[0:4, q0:q0 + N],
                                                       start=False, stop=(j < 4 * cq), skip_group_check=True), reads=[c['eh4'], cq8], writes=[sbk])
                        causal_bias(sbk, j, cq, q0)

                    attention(4, 64, lambda s_: (s_ // 2, (s_ % 2) * 64), fox_bias, lambda s_, j: (nck, nck.t[:, j, s_:s_ + 1]), lambda s_: s_,
                              fin_std(oT[2]), 0.125)

            if 3 in branches:
                rope64 = (c['C64'], c['S64'], c['R64'])
                proj_fm(OFF['dsa_q'], 2, qT, rope64)
                proj_fm(OFF['dsa_k'], 2, kT, rope64)
                proj_v(OFF['dsa_v'])
                with ExitStack() as sm:
                    qiT = sb('qiT', [128, 2, T], BF16, sm)
                    kiT = sb('kiT', [128, 1, T], BF16, sm)
                    wk2 = sb('wk2', [128, 8, 128], BF16, sm)
                    ww = sb('ww', [128, 8, 4], BF16, sm)
                    wiT = sb('wiT', [4, T], F32, sm)
                    wi = sb('wi', [128, 16, 4], F32, sm)
                    score = sb('score', [128, T], F32, sm)
                    rl = Ring([sb('rl%d' % i, [128, 1024], F32, sm) for i in range(2)])
                    mb = sb('mb', [128, 4, T], BF16, sm)
                    bs = sb('bs', [128, 64], F32, sm)
                    proj_fm(OFF['idx_q'], 2, qiT, rope64)
                    loadw(wk2, OFF['idx_k'], 64, 0)
                    loadw(wk2, OFF['idx_k'], 64, 64)
                    proj_fm(0, 1, kiT, rope64, wb=wk2)
                    loadw(ww, OFF['idx_w'], 4)
                    for tc in range(4):
                        fp = pp.next()
                        for cc in range(8):
                            mk.op('pe', lambda e: e.matmul(fp.t[0:4, :], lhsT=ww.t[:, cc, 0:4], rhs=xb.t[:, cc, tc * 512:(tc + 1) * 512],
                                                           start=(cc == 0), stop=(cc == 7)), reads=[ww, xb], writes=[fp])
                        mk.op('act', lambda e: e.activation(out=wiT.t[:, tc * 512:(tc + 1) * 512], in_=fp.t[0:4, :], func=AF.Copy), reads=[fp], writes=[wiT])
                    tp = pp.next()
                    for tt in range(16):
                        mk.op('pe', lambda e: e.transpose(out=tp.t[:, tt * 4:(tt + 1) * 4], in_=wiT.t[0:4, tt * 128:(tt + 1) * 128],
                                                          identity=c['ident_f'].t[0:4, 0:4]), reads=[wiT, c['ident_f']], writes=[tp])
                    mk.op('act', lambda e: e.activation(out=wi.t[:, :, :], in_=tp.t[:, 0:64].rearrange("p (a b) -> p a b", b=4), func=AF.Copy),
                          reads=[tp], writes=[wi])
                    big = [(K.pst[0], [ps[0], ps[1]]), (K.pst[1], [ps[2], ps[3]])]

                    def indexer(cq):
                        for ql in range(4):
                            qt = 4 * cq + ql
                            KL = (qt + 1) * 128
                            for h in range(4):
                                ch, base = h // 2, (h % 2) * 64
                                r = rl.next()
                                for blk in range(0, KL, 1024):
                                    bt, bb = big[(blk // 1024) % 2]
                                    w_ = min(1024, KL - blk)
                                    for sub in range(0, w_, 512):
                                        sw = min(512, w_ - sub)
                                        mk.op('pe', lambda e: e.matmul(bt[:, sub:sub + sw], lhsT=qiT.t[base:base + 64, ch, qt * 128:(qt + 1) * 128],
                                                                       rhs=kiT.t[base:base + 64, 0, blk + sub:blk + sub + sw], start=True, stop=True),
                                              reads=[qiT, kiT], writes=[bb[sub // 512]])
                                    mk.op('act', lambda e: e.activation(out=r.t[:, 0:w_], in_=bt[:, 0:w_], func=AF.Relu), reads=bb, writes=[r])
                                    if h == 0:
                                        mk.op('dve', lambda e: e.tensor_scalar(out=score.t[:, blk:blk + w_], in0=r.t[:, 0:w_], scalar1=wi.t[:, qt, 0:1],
                                                                               scalar2=None, op0=ALU.mult), reads=[r, wi], writes=[score])
                                    else:
                                        mk.op('dve', lambda e: e.scalar_tensor_tensor(out=score.t[:, blk:blk + w_], in0=r.t[:, 0:w_],
                                                                                      scalar=wi.t[:, qt, h:h + 1], in1=score.t[:, blk:blk + w_],
                                                                                      op0=ALU.mult, op1=ALU.add), reads=[r, wi, score], writes=[score])
                            if qt >= 2:
                                mk.op('dve', lambda e: e.tensor_reduce(out=bs.t[:, 0:1], in_=score.t[:, 0:KL], axis=AX.X, op=ALU.max), reads=[score], writes=[bs])
                                mk.op('dve', lambda e: e.tensor_reduce(out=bs.t[:, 1:2], in_=score.t[:, 0:KL], axis=AX.X, op=ALU.min), reads=[score], writes=[bs])
                            mk.op('pool', lambda e: e.tensor_tensor(out=score.t[:, qt * 128:KL], in0=score.t[:, qt * 128:KL], in1=c['tribig_f'].t[:, :], op=ALU.add),
                                  reads=[score, c['tribig_f']], writes=[score])
                            if qt >= 2:
                                mk.op('dve', lambda e: e.tensor_scalar(out=bs.t[:, 2:3], in0=bs.t[:, 1:2], scalar1=-1.0, scalar2=None, op0=ALU.add), reads=[bs], writes=[bs])
                                mk.op('dve', lambda e: e.tensor_tensor(out=bs.t[:, 3:4], in0=bs.t[:, 0:1], in1=bs.t[:, 1:2], op=ALU.subtract), reads=[bs], writes=[bs])
                                mk.op('dve', lambda e: e.tensor_scalar(out=bs.t[:, 3:4], in0=bs.t[:, 3:4], scalar1=2.0, scalar2=None, op0=ALU.add), reads=[bs], writes=[bs])
                                mk.op('dve', lambda e: e.tensor_scalar(out=bs.t[:, 8:8 + NBIS], in0=c['pow2_f'].t[:, 0:NBIS], scalar1=bs.t[:, 3:4], scalar2=None, op0=ALU.mult),
                                      reads=[bs, c['pow2_f']], writes=[bs])
                                for it in range(NBIS):
                                    mk.op('dve', lambda e: e.tensor_tensor(out=bs.t[:, 4:5], in0=bs.t[:, 2:3], in1=bs.t[:, 8 + it:9 + it], op=ALU.add), reads=[bs], writes=[bs])
                                    mk.op('dve', lambda e: e.tensor_scalar(out=mb.t[:, ql, 0:KL], in0=score.t[:, 0:KL], scalar1=bs.t[:, 4:5], scalar2=None,
                                                                           op0=ALU.is_gt, op1=ALU.add, accum_out=bs.t[:, 5:6]), reads=[score, bs], writes=[mb, bs])
                                    mk.op('dve', lambda e: e.tensor_scalar(out=bs.t[:, 6:7], in0=bs.t[:, 5:6], scalar1=255.5, scalar2=bs.t[:, 8 + it:9 + it],
                                                                           op0=ALU.is_gt, op1=ALU.mult), reads=[bs], writes=[bs])
                                    mk.op('dve', lambda e: e.tensor_tensor(out=bs.t[:, 2:3], in0=bs.t[:, 2:3], in1=bs.t[:, 6:7], op=ALU.add), reads=[bs], writes=[bs])
                            else:
                                mk.op('dve', lambda e: e.memset(bs.t[:, 2:3], -1e29), writes=[bs])
                            mk.op('dve', lambda e: e.tensor_scalar(out=mb.t[:, ql, 0:KL], in0=score.t[:, 0:KL], scalar1=bs.t[:, 2:3], scalar2=NEG,
                                                                   op0=ALU.is_le, op1=ALU.mult), reads=[score, bs], writes=[mb])
                            if qt == 5:
                                K.dump('d_bs5', bs, bs.t[:, :], [128, 64])
                                K.dump('d_score5', score, score.t[:, 0:KL], [128, KL])
                                K.dump('d_mb5', mb, mb.t[:, ql, 0:KL], [128, KL])
                            if qt == 1:
                                K.dump('d_score1', score, score.t[:, 0:KL], [128, KL])
                                K.dump('d_mb1', mb, mb.t[:, ql, 0:KL], [128, KL])

                    def dsa_bias(s_, cq, j, q0, N, sbk):
                        if s_ == 0 and j == 0:
                            indexer(cq)
                        qts = [qt for qt in range(4 * cq, 4 * cq + 4) if qt >= j]
                        for i, qt in enumerate(qts):
                            o_ = qt * 128 - q0
                            mk.op('pe', lambda e: e.matmul(sbk.t[:, o_:o_ + 128], lhsT=mb.t[:, qt - 4 * cq, j * 128:(j + 1) * 128], rhs=c['ident'].t[:, :],
                                                           start=False, stop=(i == len(qts) - 1), skip_group_check=True), reads=[mb, c['ident']], writes=[sbk])

                    attention(4, 64, lambda s_: (s_ // 2, (s_ % 2) * 64), dsa_bias, lambda s_, j: None, lambda s_: s_, fin_std(oT[3]), 0.125, cq_outer=True)
            mk.barrier()

        if 'oT' in dbg:
            for i in range(4):
                for ch in range(2):
                    mk.dma('pool', lambda e: e.dma_start(out=dr['oT'].t[i * 256 + ch * 128:i * 256 + (ch + 1) * 128, :], in_=oT[i].t[:, ch, :]),
                           reads=[oT[i]], writes=[dr['oT']])
        mk.barrier()
        if not K.cfg.get('merge', True):
            return
        mg = sb('mg', [128, 8, T], BF16, sa)
        pp = Ring([ps[0], ps[1], ps[2], ps[3]])
        pq = Ring([ps[4], ps[5], ps[6], ps[7]])
        with ExitStack() as sm:
            wbr = sb('wbr', [128, 4, 2, D], BF16, sm)
            wg = Ring([sb('wg%d' % i, [128, 8, 4, 128], BF16, sm) for i in range(2)])
            sgr = Ring([sb('sg%d' % i, [128, 512], F32, sm) for i in range(2)])
            tmr = Ring([sb('tm%d' % i, [128, 512], F32, sm) for i in range(2)])
            acr = Ring([sb('ac%d' % i, [128, 512], F32, sm) for i in range(2)])
            wbv = dr['w_branch'].t[l].rearrange("i (kc p) n -> p i kc n", p=128)
            for i in range(4):
                mk.dma('pool', lambda e: e.dma_start(out=wbr.t[:, i, :, :], in_=wbv[:, i, :, :]), reads=[dr['w_branch']], writes=[wbr])
            g0 = OFF['gates']
            wgv = w_in_v[:, :, g0:g0 + 4096].rearrange("p c (i n) -> p c i n", i=4)
            for m in range(8):
                wgt = wg.next()
                for i in range(4):
                    mk.dma('pool', lambda e: e.dma_start(out=wgt.t[:, :, i, :], in_=wgv[:, :, i, m * 128:(m + 1) * 128]), reads=[w_in], writes=[wgt])
                for tc in range(4):
                    ac = acr.next()
                    for i in range(4):
                        gp = pp.next()
                        for cc in range(8):
                            mk.op('pe', lambda e: e.matmul(gp.t[:, :], lhsT=wgt.t[:, cc, i, :], rhs=xb.t[:, cc, tc * 512:(tc + 1) * 512],
                                                           start=(cc == 0), stop=(cc == 7)), reads=[wgt, xb], writes=[gp])
                        sg = sgr.next()
                        mk.op('act', lambda e: e.activation(out=sg.t[:, :], in_=gp.t[:, :], func=AF.Sigmoid), reads=[gp], writes=[sg])
                        bp = pq.next()
                        for kc in range(2):
                            mk.op('pe', lambda e: e.matmul(bp.t[:, :], lhsT=wbr.t[:, i, kc, m * 128:(m + 1) * 128], rhs=oT[i].t[:, kc, tc * 512:(tc + 1) * 512],
                                                           start=(kc == 0), stop=(kc == 1)), reads=[wbr, oT[i]], writes=[bp])
                        if i == 0:
                            mk.op('dve', lambda e: e.tensor_tensor(out=ac.t[:, :], in0=bp.t[:, :], in1=sg.t[:, :], op=ALU.mult), reads=[bp, sg], writes=[ac])
                        else:
                            tm = tmr.next()
                            mk.op('dve', lambda e: e.tensor_tensor(out=tm.t[:, :], in0=bp.t[:, :], in1=sg.t[:, :], op=ALU.mult), reads=[bp, sg], writes=[tm])
                            if i < 3:
                                mk.op('pool', lambda e: e.tensor_tensor(out=ac.t[:, :], in0=ac.t[:, :], in1=tm.t[:, :], op=ALU.add), reads=[ac, tm], writes=[ac])
                            else:
                                mk.op('pool', lambda e: e.tensor_tensor(out=mg.t[:, m, tc * 512:(tc + 1) * 512], in0=ac.t[:, :], in1=tm.t[:, :], op=ALU.add),
                                      reads=[ac, tm], writes=[mg])
            mk.barrier()
        if not K.cfg.get('ln1', True):
            return
        with ExitStack() as so:
            wo = sb('wo', [128, 8, D], BF16, so)
            xres = sb('xres', [128, 8, 512], F32, so)
            y = sb('y', [128, 8, 512], F32, so)
            sqr = Ring([sb('sq%d' % i, [128, 512], F32, so) for i in range(2)])
            mean = sb('mean', [128, 512], F32, so)
            rstd = sb('rstd', [128, 512], F32, so)
            msq = sb('msq', [128, 512], F32, so)
            rowf = Ring([sb('rowf%d' % i, [128, D], F32, so) for i in range(2)])
            rowb = Ring([sb('rowb%d' % i, [128, D], BF16, so) for i in range(2)])
            wov = dr['w_out'].t[l].rearrange("(kc p) n -> p kc n", p=128)
            for kc in range(8):
                mk.dma('pool', lambda e: e.dma_start(out=wo.t[:, kc, :], in_=wov[:, kc, :]), reads=[dr['w_out']], writes=[wo])
            x1Tv = dr['x1T'].t.rearrange("(c p) n -> p c n", p=128)
            ga, gb = PA['ln1_g']
            ba, bb_ = PA['ln1_b']
            big = [(K.pst[0], [ps[0], ps[1]]), (K.pst[1], [ps[2], ps[3]])]
            for tc in range(4):
                c0 = t0g + tc * 512
                mk.dma('sp', lambda e: e.dma_start(out=xres.t[:, :, :], in_=srcv[:, :, c0:c0 + 512]), reads=[src], writes=[xres])
                for m in range(8):
                    hp = pq.next()
                    for kc in range(8):
                        mk.op('pe', lambda e: e.matmul(hp.t[:, :], lhsT=wo.t[:, kc, m * 128:(m + 1) * 128], rhs=mg.t[:, kc, tc * 512:(tc + 1) * 512],
                                                       start=(kc == 0), stop=(kc == 7)), reads=[wo, mg], writes=[hp])
                    mk.op('dve', lambda e: e.scalar_tensor_tensor(out=y.t[:, m, :], in0=xres.t[:, m, :], scalar=ALPHA, in1=hp.t[:, :], op0=ALU.mult, op1=ALU.add),
                          reads=[xres, hp], writes=[y])
                s1, s2 = ps[4], ps[5]
                for m in range(8):
                    mk.op('pe', lambda e: e.matmul(s1.t[:, :], lhsT=c['ones_f'].t[:, :], rhs=y.t[:, m, :], start=(m == 0), stop=(m == 7)),
                          reads=[c['ones_f'], y], writes=[s1])
                for m in range(8):
                    sq = sqr.next()
                    mk.op('act', lambda e: e.activation(out=sq.t[:, :], in_=y.t[:, m, :], func=AF.Square), reads=[y], writes=[sq])
                    mk.op('pe', lambda e: e.matmul(s2.t[:, :], lhsT=c['ones_f'].t[:, :], rhs=sq.t[:, :], start=(m == 0), stop=(m == 7)),
                          reads=[c['ones_f'], sq], writes=[s2])
                mk.op('act', lambda e: e.activation(out=mean.t[:, :], in_=s1.t[:, :], func=AF.Copy, scale=1.0 / D), reads=[s1], writes=[mean])
                mk.op('pool', lambda e: e.tensor_tensor(out=msq.t[:, :], in0=mean.t[:, :], in1=mean.t[:, :], op=ALU.mult), reads=[mean], writes=[msq])
                mk.op('dve', lambda e: e.scalar_tensor_tensor(out=rstd.t[:, :], in0=s2.t[:, :], scalar=1.0 / D, in1=msq.t[:, :], op0=ALU.mult, op1=ALU.subtract),
                      reads=[s2, msq], writes=[rstd])
                mk.op('act', lambda e: e.activation(out=rstd.t[:, :], in_=rstd.t[:, :], func=AF.Ln, scale=1.0, bias=LN_EPS), reads=[rstd], writes=[rstd])
                mk.op('act', lambda e: e.activation(out=rstd.t[:, :], in_=rstd.t[:, :], func=AF.Exp, scale=-0.5), reads=[rstd], writes=[rstd])
                for m in range(8):
                    mk.op('dve', lambda e: e.tensor_tensor(out=y.t[:, m, :], in0=y.t[:, m, :], in1=mean.t[:, :], op=ALU.subtract), reads=[y, mean], writes=[y])
                    mk.op('pool', lambda e: e.tensor_tensor(out=y.t[:, m, :], in0=y.t[:, m, :], in1=rstd.t[:, :], op=ALU.mult), reads=[y, rstd], writes=[y])
                    mk.op('act', lambda e: e.activation(out=y.t[:, m, :], in_=y.t[:, m, :], func=AF.Identity, scale=par.t[:, ga + m:ga + m + 1],
                                                        bias=par.t[:, ba + m:ba + m + 1]), reads=[y, par], writes=[y])
                mk.dma('sp', lambda e: e.dma_start(out=x1Tv[:, :, c0:c0 + 512], in_=y.t[:, :, :]), reads=[y], writes=[dr['x1T']])
                for tt in range(4):
                    bt, bb2 = big[tt % 2]
                    for m in range(8):
                        mk.op('pe', lambda e: e.transpose(out=bt[:, m * 128:(m + 1) * 128], in_=y.t[:, m, tt * 128:(tt + 1) * 128], identity=c['ident_f'].t[:, :]),
                              reads=[y, c['ident_f']], writes=[bb2[m // 4]])
                    rf = rowf.next()
                    rb = rowb.next()
                    mk.op('act', lambda e: e.activation(out=rf.t[:, :], in_=bt[:, :], func=AF.Copy), reads=bb2, writes=[rf])
                    mk.op('pool', lambda e: e.tensor_copy(out=rb.t[:, :], in_=rf.t[:, :]), reads=[rf], writes=[rb])
                    r0 = c0 + tt * 128
                    mk.dma('sp', lambda e: e.dma_start(out=dr['x1r'].t[r0:r0 + 128, :], in_=rf.t[:, :]), reads=[rf], writes=[dr['x1r']])
                    mk.dma('sp', lambda e: e.dma_start(out=dr['x1rb'].t[r0:r0 + 128, :], in_=rb.t[:, :]), reads=[rb], writes=[dr['x1rb']])
            mk.barrier()


def stage_b(K, l, last):
    mk, sb, dr, nc, ps, c = K.mk, K.sb, K.dr, K.nc, K.ps, K.c
    NT = K.NT
    NTL = NT // 128
    NEX = K.cfg.get('ne', NE)
    IOA = bass.IndirectOffsetOnAxis
    pp = Ring([ps[0], ps[1], ps[2], ps[3]])
    pq = Ring([ps[4], ps[5], ps[6], ps[7]])
    big = [(K.pst[0], [ps[0], ps[1]]), (K.pst[1], [ps[2], ps[3]])]
    x1Tv = dr['x1T'].t.rearrange("(c p) n -> p c n", p=128)
    with ExitStack() as sbs:
        par = sb('parB', [128, NPB], F32, sbs)
        slotall = sb('slotall', [128, NTL, 8], I32, sbs)
        mk.dma('sp', lambda e: e.dma_start(out=par.t[:, :], in_=dr['parB'].t[l]), reads=[dr['parB']], writes=[par])
        with ExitStack() as s1:
            wr = sb('wr', [128, 8, 256], BF16, s1)
            carry = sb('carry', [128, 256], F32, s1)
            tokid = sb('tokid', [128, NTL], I32, s1)
            fill = sb('fill', [128, NE * CAP * 2 // 128], I32, s1)
            zrow = sb('zrow', [128, D], BF16, s1)
            xtbr = Ring([sb('xtb%d' % i, [128, 8, 128], BF16, s1) for i in range(2)])
            f256 = {n: sb(n, [128, 256], F32, s1) for n in ('sc', 'bi', 'eq', 'bi2', 'msk', 'sel', 'wd', 'pos', 'slotf', 'junk')}
            selb = sb('selb', [128, 256], BF16, s1)
            sm8 = sb('sm8', [128, 64], F32, s1)
            stg = sb('stg', [128, 8, 2], I32, s1)
            mk.dma('pool', lambda e: e.dma_start(out=wr.t[:, :, :], in_=dr['w_router'].t[l].rearrange("(c p) n -> p c n", p=128)),
                   reads=[dr['w_router']], writes=[wr])
            mk.op('dve', lambda e: e.memset(carry.t[:, :], 0.0), writes=[carry])
            mk.op('pool', lambda e: e.iota(tokid.t[:, :], pattern=[[128, NTL]], base=0, channel_multiplier=1), writes=[tokid])
            mk.op('dve', lambda e: e.memset(fill.t[:, :], 0), writes=[fill])
            mk.op('dve', lambda e: e.memset(fill.t[:, :].rearrange("p (a b) -> p a b", b=2)[:, :, 0], NT), writes=[fill])
            mk.op('pool', lambda e: e.memset(zrow.t[:, :], 0.0), writes=[zrow])
            mk.dma('sp', lambda e: e.dma_start(out=dr['tokw'].t.rearrange("(p a) b -> p (a b)", p=128), in_=fill.t[:, :]), reads=[fill], writes=[dr['tokw']])
            mk.dma('sp', lambda e: e.dma_start(out=dr['x1rb'].t[NT:NT + 128, :], in_=zrow.t[:, :]), reads=[zrow], writes=[dr['x1rb']])
            if NEX < NE:
                yz = dr['ye'].t.rearrange("(p a) n -> p a n", p=128)
                for a0 in range(0, NE * CAP // 128, 96):
                    mk.dma('sp', lambda e: e.dma_start(out=yz[:, a0:a0 + 96, :], in_=zrow.t[:, :].unsqueeze(1).to_broadcast([128, 96, D])), reads=[zrow], writes=[dr['ye']])
            sc, bi, eq, bi2, msk, sel, wd, pos, slotf, junk = [f256[n] for n in ('sc', 'bi', 'eq', 'bi2', 'msk', 'sel', 'wd', 'pos', 'slotf', 'junk')]
            ra, rb_ = PB['rbias']
            for tt in range(NTL):
                xtb = xtbr.next()
                mk.dma('pool', lambda e: e.dma_start(out=xtb.t[:, :, :], in_=x1Tv[:, :, tt * 128:(tt + 1) * 128]), reads=[dr['x1T']], writes=[xtb])
                lp = pp.next()
                for cc in range(8):
                    mk.op('pe', lambda e: e.matmul(lp.t[:, 0:256], lhsT=xtb.t[:, cc, :], rhs=wr.t[:, cc, :], start=(cc == 0), stop=(cc == 7)),
                          reads=[xtb, wr], writes=[lp])
                mk.op('act', lambda e: e.activation(out=sc.t[:, :], in_=lp.t[:, 0:256], func=AF.Sigmoid), reads=[lp], writes=[sc])
                mk.op('dve', lambda e: e.tensor_tensor(out=bi.t[:, :], in0=sc.t[:, :], in1=par.t[:, ra:rb_], op=ALU.add), reads=[sc, par], writes=[bi])
                bi3 = bi.t[:, :].rearrange("p (g k) -> p g k", k=32)
                mk.op('dve', lambda e: e.tensor_reduce(out=sm8.t[:, 0:8], in_=bi3, axis=AX.X, op=ALU.max), reads=[bi], writes=[sm8])
                mk.op('dve', lambda e: e.tensor_tensor(out=eq.t[:, :].rearrange("p (g k) -> p g k", k=32), in0=bi3,
                                                       in1=sm8.t[:, 0:8].unsqueeze(2).to_broadcast([128, 8, 32]), op=ALU.is_equal), reads=[bi, sm8], writes=[eq])
                mk.op('dve', lambda e: e.scalar_tensor_tensor(out=bi2.t[:, :], in0=eq.t[:, :], scalar=-1e30, in1=bi.t[:, :], op0=ALU.mult, op1=ALU.add),
                      reads=[eq, bi], writes=[bi2])
                mk.op('dve', lambda e: e.tensor_reduce(out=sm8.t[:, 8:16], in_=bi2.t[:, :].rearrange("p (g k) -> p g k", k=32), axis=AX.X, op=ALU.max),
                      reads=[bi2], writes=[sm8])
                mk.op('dve', lambda e: e.tensor_tensor(out=sm8.t[:, 16:24], in0=sm8.t[:, 0:8], in1=sm8.t[:, 8:16], op=ALU.add), reads=[sm8], writes=[sm8])
                mk.op('dve', lambda e: e.max(out=sm8.t[:, 24:32], in_=sm8.t[:, 16:24]), reads=[sm8], writes=[sm8])
                mk.op('dve', lambda e: e.tensor_scalar(out=sm8.t[:, 32:40], in0=sm8.t[:, 16:24], scalar1=sm8.t[:, 27:28], scalar2=-1e30, op0=ALU.is_lt, op1=ALU.mult),
                      reads=[sm8], writes=[sm8])
                mk.op('dve', lambda e: e.tensor_tensor(out=msk.t[:, :].rearrange("p (g k) -> p g k", k=32), in0=bi3,
                                                       in1=sm8.t[:, 32:40].unsqueeze(2).to_broadcast([128, 8, 32]), op=ALU.add), reads=[bi, sm8], writes=[msk])
                mk.op('dve', lambda e: e.max(out=sm8.t[:, 40:48], in_=msk.t[:, :]), reads=[msk], writes=[sm8])
                mk.op('dve', lambda e: e.tensor_scalar(out=sel.t[:, :], in0=msk.t[:, :], scalar1=sm8.t[:, 47:48], scalar2=None, op0=ALU.is_ge), reads=[msk, sm8], writes=[sel])
                mk.op('dve', lambda e: e.scalar_tensor_tensor(out=wd.t[:, :], in0=sel.t[:, :], scalar=1.0, in1=sc.t[:, :], op0=ALU.mult, op1=ALU.mult,
                                                              accum_out=sm8.t[:, 48:49]), reads=[sel, sc], writes=[wd, sm8])
                mk.op('dve', lambda e: e.reciprocal(out=sm8.t[:, 49:50], in_=sm8.t[:, 48:49]), reads=[sm8], writes=[sm8])
                mk.op('dve', lambda e: e.tensor_scalar(out=wd.t[:, :], in0=wd.t[:, :], scalar1=sm8.t[:, 49:50], scalar2=2.5, op0=ALU.mult, op1=ALU.mult),
                      reads=[wd, sm8], writes=[wd])
                mk.op('pool', lambda e: e.tensor_copy(out=selb.t[:, :], in_=sel.t[:, :]), reads=[sel], writes=[selb])
                p1 = pq.next()
                p2 = pq.next()
                mk.op('pe', lambda e: e.matmul(p1.t[:, 0:256], lhsT=c['lstrict'].t[:, :], rhs=selb.t[:, :], start=True, stop=True), reads=[c['lstrict'], selb], writes=[p1])
                mk.op('pe', lambda e: e.matmul(p2.t[:, 0:256], lhsT=c['ones'].t[:, :], rhs=selb.t[:, :], start=True, stop=True), reads=[c['ones'], selb], writes=[p2])
                mk.op('dve', lambda e: e.tensor_tensor(out=pos.t[:, :], in0=p1.t[:, 0:256], in1=carry.t[:, :], op=ALU.add), reads=[p1, carry], writes=[pos])
                mk.op('dve', lambda e: e.tensor_tensor(out=carry.t[:, :], in0=p2.t[:, 0:256], in1=carry.t[:, :], op=ALU.add), reads=[p2, carry], writes=[carry])
                mk.op('dve', lambda e: e.scalar_tensor_tensor(out=pos.t[:, :], in0=pos.t[:, :], scalar=float(CAP - 1), in1=c['ebase_f'].t[:, :], op0=ALU.min, op1=ALU.add),
                      reads=[pos, c['ebase_f']], writes=[pos])
                mk.op('dve', lambda e: e.tensor_tensor(out=slotf.t[:, :], in0=pos.t[:, :], in1=sel.t[:, :], op=ALU.mult), reads=[pos, sel], writes=[slotf])
                mk.op('dve', lambda e: e.max(out=sm8.t[:, 50:58], in_=slotf.t[:, :]), reads=[slotf], writes=[sm8])
                for k in range(8):
                    mk.op('dve', lambda e: e.scalar_tensor_tensor(out=junk.t[:, :], in0=slotf.t[:, :], scalar=sm8.t[:, 50 + k:51 + k], in1=wd.t[:, :],
                                                                  op0=ALU.is_equal, op1=ALU.mult, accum_out=stg.t[:, k, 1:2].bitcast(F32)),
                          reads=[slotf, sm8, wd], writes=[junk, stg])
                mk.op('dve', lambda e: e.tensor_scalar(out=slotall.t[:, tt, :], in0=sm8.t[:, 50:58], scalar1=-1.0, scalar2=None, op0=ALU.add), reads=[sm8], writes=[slotall])
                mk.op('pool', lambda e: e.tensor_copy(out=stg.t[:, :, 0], in_=tokid.t[:, tt:tt + 1].to_broadcast([128, 8])), reads=[tokid], writes=[stg])
                for k in range(8):
                    mk.dma('pool', lambda e: e.indirect_dma_start(out=dr['tokw'].t[:, :], out_offset=IOA(slotall.t[:, tt, k:k + 1], 0), in_=stg.t[:, k, :], in_offset=None),
                           reads=[stg, slotall], writes=[dr['tokw']])
            mk.barrier()
        with ExitStack() as s2:
            twr = Ring([sb('tw%d' % i, [128, 3, 2], I32, s2) for i in range(3)])
            xer = Ring([sb('xe%d' % i, [128, 3, D], BF16, s2) for i in range(3)])
            xeTr = Ring([sb('xeT%d' % i, [128, 8, CAP], BF16, s2) for i in range(3)])
            wgur = Ring([sb('wgu%d' % i, [128, 8, 512], BF16, s2) for i in range(3)])
            wdnr = Ring([sb('wdn%d' % i, [128, 2, D], BF16, s2) for i in range(3)])
            hTr = Ring([sb('hT%d' % i, [128, 2, CAP], BF16, s2) for i in range(3)])
            sgr = Ring([sb('sge%d' % i, [128, CAP], F32, s2) for i in range(2)])
            yor = Ring([sb('yo%d' % i, [128, 3, D], BF16, s2) for i in range(3)])
            twv = dr['tokw'].t.rearrange("(e rt p) c -> e p rt c", rt=3, p=128)
            yev = dr['ye'].t.rearrange("(e rt p) n -> e p rt n", rt=3, p=128)
            ev = 0
            for ex in range(NEX):
                tw, xe, xeT, wgu, wdn, hT, yo = twr.next(), xer.next(), xeTr.next(), wgur.next(), wdnr.next(), hTr.next(), yor.next()
                mk.dma('sp', lambda e: e.dma_start(out=tw.t[:, :, :], in_=twv[ex]), reads=[dr['tokw']], writes=[tw])
                mk.dma('pool', lambda e: e.dma_start(out=wgu.t[:, :, 0:256], in_=dr['w_exp_gate'].t[l, ex].rearrange("(c p) n -> p c n", p=128)),
                       reads=[dr['w_exp_gate']], writes=[wgu])
                mk.dma('pool', lambda e: e.dma_start(out=wgu.t[:, :, 256:512], in_=dr['w_exp_up'].t[l, ex].rearrange("(c p) n -> p c n", p=128)),
                       reads=[dr['w_exp_up']], writes=[wgu])
                mk.dma('pool', lambda e: e.dma_start(out=wdn.t[:, :, :], in_=dr['w_exp_down'].t[l, ex].rearrange("(c p) n -> p c n", p=128)),
                       reads=[dr['w_exp_down']], writes=[wdn])
                for rt in range(3):
                    mk.dma('pool', lambda e: e.indirect_dma_start(out=xe.t[:, rt, :], out_offset=None, in_=dr['x1rb'].t[:, :], in_offset=IOA(tw.t[:, rt, 0:1], 0)),
                           reads=[tw, dr['x1rb']], writes=[xe])
                for rt in range(3):
                    bk = pp.next()
                    vb = bk.t.bitcast(BF16)
                    for cc in range(8):
                        mk.op('pe', lambda e: e.transpose(out=vb[:, cc * 128:(cc + 1) * 128], in_=xe.t[:, rt, cc * 128:(cc + 1) * 128], identity=c['ident'].t[:, :]),
                              reads=[xe, c['ident']], writes=[bk])
                    ev += 1
                    if ev % 2 == 0:
                        mk.op('act', lambda e: e.activation(out=xeT.t[:, :, rt * 128:(rt + 1) * 128], in_=vb.rearrange("p (c r) -> p c r", c=8), func=AF.Copy),
                              reads=[bk], writes=[xeT])
                    else:
                        mk.op('dve', lambda e: e.tensor_copy(out=xeT.t[:, :, rt * 128:(rt + 1) * 128], in_=vb.rearrange("p (c r) -> p c r", c=8)), reads=[bk], writes=[xeT])
                for fc in range(2):
                    gp = pq.next()
                    up_ = pq.next()
                    for cc in range(8):
                        mk.op('pe', lambda e: e.matmul(gp.t[:, 0:CAP], lhsT=wgu.t[:, cc, fc * 128:(fc + 1) * 128], rhs=xeT.t[:, cc, :], start=(cc == 0), stop=(cc == 7)),
                              reads=[wgu, xeT], writes=[gp])
                    for cc in range(8):
                        mk.op('pe', lambda e: e.matmul(up_.t[:, 0:CAP], lhsT=wgu.t[:, cc, 256 + fc * 128:256 + (fc + 1) * 128], rhs=xeT.t[:, cc, :], start=(cc == 0), stop=(cc == 7)),
                              reads=[wgu, xeT], writes=[up_])
                    sg = sgr.next()
                    mk.op('act', lambda e: e.activation(out=sg.t[:, :], in_=gp.t[:, 0:CAP], func=AF.Silu), reads=[gp], writes=[sg])
                    mk.op('dve', lambda e: e.tensor_tensor(out=hT.t[:, fc, :], in0=up_.t[:, 0:CAP], in1=sg.t[:, :], op=ALU.mult), reads=[up_, sg], writes=[hT])
                twf = tw.t[:, :, :].bitcast(F32)
                for rt in range(3):
                    for half in range(2):
                        yp = pp.next()
                        for fc in range(2):
                            mk.op('pe', lambda e: e.matmul(yp.t[:, :], lhsT=hT.t[:, fc, rt * 128:(rt + 1) * 128], rhs=wdn.t[:, fc, half * 512:(half + 1) * 512],
                                                           start=(fc == 0), stop=(fc == 1)), reads=[hT, wdn], writes=[yp])
                        ev += 1
                        if ev % 2 == 0:
                            mk.op('act', lambda e: e.activation(out=yo.t[:, rt, half * 512:(half + 1) * 512], in_=yp.t[:, :], func=AF.Copy, scale=twf[:, rt, 1:2]),
                                  reads=[yp, tw], writes=[yo])
                        else:
                            mk.op('dve', lambda e: e.tensor_scalar(out=yo.t[:, rt, half * 512:(half + 1) * 512], in0=yp.t[:, :], scalar1=twf[:, rt, 1:2], scalar2=None, op0=ALU.mult),
                                  reads=[yp, tw], writes=[yo])
                mk.dma('sp', lambda e: e.dma_start(out=yev[ex], in_=yo.t[:, :, :]), reads=[yo], writes=[dr['ye']])
            mk.barrier()
        with ExitStack() as s3:
            wsgu = sb('wsgu', [128, 8, 512], BF16, s3)
            wsd = sb('wsd', [128, 2, D], BF16, s3)
            xrr = Ring([sb('xr%d' % i, [128, D], F32, s3) for i in range(2)])
            xtbr = Ring([sb('xtc%d' % i, [128, 8, 128], BF16, s3) for i in range(2)])
            ygr = Ring([sb('yg%d' % i, [128, D], BF16, s3) for i in range(4)])
            accr = Ring([sb('acc%d' % i, [128, D], F32, s3) for i in range(2)])
            hs = sb('hs', [128, 2, 128], BF16, s3)
            sgs = sb('sgs', [128, 128], F32, s3)
            st = sb('st', [128, 16], F32, s3)
            jk = sb('jk', [128, D], F32, s3)
            xo = sb('xo', [128, 8, 128], F32, s3)
            mk.dma('pool', lambda e: e.dma_start(out=wsgu.t[:, :, 0:256], in_=dr['w_sh_gate'].t[l].rearrange("(c p) n -> p c n", p=128)), reads=[dr['w_sh_gate']], writes=[wsgu])
            mk.dma('pool', lambda e: e.dma_start(out=wsgu.t[:, :, 256:512], in_=dr['w_sh_up'].t[l].rearrange("(c p) n -> p c n", p=128)), reads=[dr['w_sh_up']], writes=[wsgu])
            mk.dma('pool', lambda e: e.dma_start(out=wsd.t[:, :, :], in_=dr['w_sh_down'].t[l].rearrange("(c p) n -> p c n", p=128)), reads=[dr['w_sh_down']], writes=[wsd])
            g0, g1 = PB['ln2_g']
            b0, b1 = PB['ln2_b']
            x2Tv = dr['x2T'].t.rearrange("(c p) n -> p c n", p=128)
            for tt in range(NTL):
                xr, xtb, acc = xrr.next(), xtbr.next(), accr.next()
                mk.dma('sp', lambda e: e.dma_start(out=xr.t[:, :], in_=dr['x1r'].t[tt * 128:(tt + 1) * 128, :]), reads=[dr['x1r']], writes=[xr])
                mk.dma('pool', lambda e: e.dma_start(out=xtb.t[:, :, :], in_=x1Tv[:, :, tt * 128:(tt + 1) * 128]), reads=[dr['x1T']], writes=[xtb])
                for fc in range(2):
                    gp = pq.next()
                    up_ = pq.next()
                    for cc in range(8):
                        mk.op('pe', lambda e: e.matmul(gp.t[:, 0:128], lhsT=wsgu.t[:, cc, fc * 128:(fc + 1) * 128], rhs=xtb.t[:, cc, :], start=(cc == 0), stop=(cc == 7)),
                              reads=[wsgu, xtb], writes=[gp])
                    for cc in range(8):
                        mk.op('pe', lambda e: e.matmul(up_.t[:, 0:128], lhsT=wsgu.t[:, cc, 256 + fc * 128:256 + (fc + 1) * 128], rhs=xtb.t[:, cc, :], start=(cc == 0), stop=(cc == 7)),
                              reads=[wsgu, xtb], writes=[up_])
                    mk.op('act', lambda e: e.activation(out=sgs.t[:, :], in_=gp.t[:, 0:128], func=AF.Silu), reads=[gp], writes=[sgs])
                    mk.op('dve', lambda e: e.tensor_tensor(out=hs.t[:, fc, :], in0=up_.t[:, 0:128], in1=sgs.t[:, :], op=ALU.mult), reads=[up_, sgs], writes=[hs])
                for half in range(2):
                    yp = pp.next()
                    for fc in range(2):
                        mk.op('pe', lambda e: e.matmul(yp.t[:, :], lhsT=hs.t[:, fc, :], rhs=wsd.t[:, fc, half * 512:(half + 1) * 512], start=(fc == 0), stop=(fc == 1)),
                              reads=[hs, wsd], writes=[yp])
                    mk.op('dve', lambda e: e.scalar_tensor_tensor(out=acc.t[:, half * 512:(half + 1) * 512], in0=xr.t[:, half * 512:(half + 1) * 512], scalar=ALPHA,
                                                                  in1=yp.t[:, :], op0=ALU.mult, op1=ALU.add), reads=[xr, yp], writes=[acc])
                for k in range(8):
                    yg = ygr.next()
                    mk.dma('pool', lambda e: e.indirect_dma_start(out=yg.t[:, :], out_offset=None, in_=dr['ye'].t[:, :], in_offset=IOA(slotall.t[:, tt, k:k + 1], 0)),
                           reads=[slotall, dr['ye']], writes=[yg])
                    mk.op('dve', lambda e: e.tensor_tensor(out=acc.t[:, :], in0=acc.t[:, :], in1=yg.t[:, :], op=ALU.add), reads=[acc, yg], writes=[acc])
                mk.op('act', lambda e: e.activation(out=jk.t[:, :], in_=acc.t[:, :], func=AF.Copy, accum_out=st.t[:, 0:1]), reads=[acc], writes=[jk, st])
                mk.op('act', lambda e: e.activation(out=jk.t[:, :], in_=acc.t[:, :], func=AF.Square, accum_out=st.t[:, 1:2]), reads=[acc], writes=[jk, st])
                mk.op('dve', lambda e: e.tensor_scalar(out=st.t[:, 2:4], in0=st.t[:, 0:2], scalar1=1.0 / D, scalar2=None, op0=ALU.mult), reads=[st], writes=[st])
                mk.op('dve', lambda e: e.tensor_tensor(out=st.t[:, 4:5], in0=st.t[:, 2:3], in1=st.t[:, 2:3], op=ALU.mult), reads=[st], writes=[st])
                mk.op('dve', lambda e: e.tensor_tensor(out=st.t[:, 5:6], in0=st.t[:, 3:4], in1=st.t[:, 4:5], op=ALU.subtract), reads=[st], writes=[st])
                mk.op('act', lambda e: e.activation(out=st.t[:, 6:7], in_=st.t[:, 5:6], func=AF.Ln, scale=1.0, bias=LN_EPS), reads=[st], writes=[st])
                mk.op('act', lambda e: e.activation(out=st.t[:, 7:8], in_=st.t[:, 6:7], func=AF.Exp, scale=-0.5), reads=[st], writes=[st])
                mk.op('dve', lambda e: e.tensor_scalar(out=acc.t[:, :], in0=acc.t[:, :], scalar1=st.t[:, 2:3], scalar2=st.t[:, 7:8], op0=ALU.subtract, op1=ALU.mult),
                      reads=[acc, st], writes=[acc])
                mk.op('pool', lambda e: e.tensor_tensor(out=acc.t[:, :], in0=acc.t[:, :], in1=par.t[:, g0:g1], op=ALU.mult), reads=[acc, par], writes=[acc])
                mk.op('dve', lambda e: e.tensor_tensor(out=acc.t[:, :], in0=acc.t[:, :], in1=par.t[:, b0:b1], op=ALU.add), reads=[acc, par], writes=[acc])
                if last:
                    mk.dma('sp', lambda e: e.dma_start(out=dr['out'].t[tt * 128:(tt + 1) * 128, :], in_=acc.t[:, :]), reads=[acc], writes=[dr['out']])
                else:
                    bt, bb2 = big[tt % 2]
                    for m in range(8):
                        mk.op('pe', lambda e: e.transpose(out=bt[:, m * 128:(m + 1) * 128], in_=acc.t[:, m * 128:(m + 1) * 128], identity=c['ident_f'].t[:, :]),
                              reads=[acc, c['ident_f']], writes=[bb2[m // 4]])
                    mk.op('act', lambda e: e.activation(out=xo.t[:, :, :], in_=bt[:, :].rearrange("p (c r) -> p c r", c=8), func=AF.Copy), reads=bb2, writes=[xo])
                    mk.dma('sp', lambda e: e.dma_start(out=x2Tv[:, :, tt * 128:(tt + 1) * 128], in_=xo.t[:, :, :]), reads=[xo], writes=[dr['x2T']])
            mk.barrier()


def host_prep(inputs, NS=4, experts=True):
    x = np.asarray(inputs['x'])
    cst = make_consts()
    parA = np.zeros((2, 128, NPA), np.float32)
    parB = np.zeros((2, 128, NPB), np.float32)
    for l in range(2):
        parA[l, :, 0:8] = np.asarray(inputs['ln1_g'][l]).reshape(8, 128).T
        parA[l, :, 8:16] = np.asarray(inputs['ln1_b'][l]).reshape(8, 128).T
        parA[l, 0:4, 16] = np.asarray(inputs['b_forget'][l])
        parA[l, :, 20:148] = np.asarray(inputs['diff_lambda'][l]).reshape(1, 128)
        parA[l, 0:64, 148] = np.asarray(inputs['diff_subln'][l])
        parB[l, :, 0:256] = np.asarray(inputs['router_bias'][l])[None, :]
        parB[l, :, 256:1280] = np.asarray(inputs['ln2_g'][l])[None, :]
        parB[l, :, 1280:2304] = np.asarray(inputs['ln2_b'][l])[None, :]
    shared = {k: np.ascontiguousarray(np.asarray(inputs[k], dtype=np.float32)) for k in
              ('w_in', 'w_branch', 'w_out', 'w_router', 'w_exp_gate', 'w_exp_up', 'w_exp_down', 'w_sh_gate', 'w_sh_up', 'w_sh_down')}
    if not experts:
        for k in ('w_exp_gate', 'w_exp_up', 'w_exp_down'):
            shared.pop(k)
    shared.update(cst=cst, parA=parA, parB=parB)
    maps = []
    for core in range(8):
        xs = x[core * 4:core * 4 + NS]
        xT = np.ascontiguousarray(xs.reshape(NS * T, D).T)
        m = dict(shared)
        m['xT'] = xT
        maps.append(m)
    return maps


_CACHE = {}


def kernel(**inputs):
    cfg = {}
    if 'nc' not in _CACHE:
        _CACHE['nc'] = build(cfg)
    nc, K = _CACHE['nc']
    maps = host_prep(inputs)
    res = run_bass_kernel_spmd(nc, maps, core_ids=list(range(8)))
    out = np.concatenate([np.asarray(r['out']).reshape(4, T, D) for r in res.results], axis=0)
    return out.astype(np.float32)
```

```python
import os
import numpy as np
from contextlib import ExitStack
import concourse.bass as bass
import concourse.mybir as mybir
from concourse.bass_utils import run_bass_kernel_spmd

F32 = mybir.dt.float32
BF16 = mybir.dt.bfloat16
I32 = mybir.dt.int32
U32 = mybir.dt.uint32
AF = mybir.ActivationFunctionType
ALU = mybir.AluOpType
AX = mybir.AxisListType

T = 2048
D = 1024
NEG = -30000.0
ALPHA = 4 ** 0.25
LN_EPS = 1e-5
NBIS = 18
CAP = 384
NE = 256

OFF = {}
_o = 0
for _n, _w in (('moba_q', 256), ('moba_k', 256), ('moba_v', 256), ('diff_q', 256), ('diff_k', 256), ('diff_v', 256),
               ('fox_q', 256), ('fox_k', 256), ('fox_v', 256), ('fox_f', 4), ('dsa_q', 256), ('dsa_k', 256), ('dsa_v', 256),
               ('idx_q', 256), ('idx_k', 64), ('idx_w', 4), ('gates', 4096)):
    OFF[_n] = _o
    _o += _w
WIN = _o

CC = {}
_o = 0
for _n, _w in (('C64', T), ('S64', T), ('C32', T), ('S32', T), ('R64', 128), ('R32', 128), ('ident', 128),
               ('tribias', 128), ('tribig', 128), ('esel', 1024), ('pow2', 32), ('ones', 128), ('eh4', 512), ('mask01', 2), ('lstrict', 128), ('ebase', 256)):
    CC[_n] = (_o, _o + _w)
    _o += _w
NCST = _o

PA = {'ln1_g': (0, 8), 'ln1_b': (8, 16), 'negb': (16, 17), 'dl': (20, 148), 'subln': (148, 149)}
NPA = 160
PB = {'rbias': (0, 256), 'ln2_g': (256, 1280), 'ln2_b': (1280, 2304)}
NPB = 2304


def make_consts():
    c = np.zeros((128, NCST), np.float32)
    pos = np.arange(T, dtype=np.float32)
    p = np.arange(128)
    inv64 = (10000.0 ** (-np.arange(32, dtype=np.float32) / 32)).astype(np.float32)
    inv32 = (10000.0 ** (-np.arange(16, dtype=np.float32) / 16)).astype(np.float32)
    a64 = (pos[None, :] * inv64[p % 32][:, None]).astype(np.float32)
    a32 = (pos[None, :] * inv32[p % 16][:, None]).astype(np.float32)
    c[:, CC['C64'][0]:CC['C64'][1]] = np.cos(a64.astype(np.float64))
    c[:, CC['S64'][0]:CC['S64'][1]] = np.sin(a64.astype(np.float64))
    c[:, CC['C32'][0]:CC['C32'][1]] = np.cos(a32.astype(np.float64))
    c[:, CC['S32'][0]:CC['S32'][1]] = np.sin(a32.astype(np.float64))
    R64 = np.zeros((128, 128), np.float32)
    R32 = np.zeros((128, 128), np.float32)
    for m in range(128):
        if m % 64 < 32:
            R64[m + 32, m] = -1.0
        else:
            R64[m - 32, m] = 1.0
        if m % 32 < 16:
            R32[m + 16, m] = -1.0
        else:
            R32[m - 16, m] = 1.0
    c[:, CC['R64'][0]:CC['R64'][1]] = R64
    c[:, CC['R32'][0]:CC['R32'][1]] = R32
    c[:, CC['ident'][0]:CC['ident'][1]] = np.eye(128, dtype=np.float32)
    r = np.arange(128)[:, None]
    q = np.arange(128)[None, :]
    c[:, CC['tribias'][0]:CC['tribias'][1]] = np.where(q < r, NEG, 0.0)
    c[:, CC['tribig'][0]:CC['tribig'][1]] = np.where(q > r, -1e30, 0.0)
    es = np.zeros((128, 8, 128), np.float32)
    for n in range(8):
        es[n, n, :] = 1.0
    c[:, CC['esel'][0]:CC['esel'][1]] = es.reshape(128, 1024)
    c[:, CC['pow2'][0]:CC['pow2'][1]] = (0.5 ** (np.arange(32) + 1))[None, :]
    c[:, CC['ones'][0]:CC['ones'][1]] = 1.0
    e4 = np.zeros((128, 4, 128), np.float32)
    for n in range(4):
        e4[n, n, :] = 1.0
    c[:, CC['eh4'][0]:CC['eh4'][1]] = e4.reshape(128, 512)
    c[:, CC['mask01'][0]] = (np.arange(128) % 64 < 32)
    c[:, CC['lstrict'][0]:CC['lstrict'][1]] = (np.arange(128)[:, None] < np.arange(128)[None, :])
    c[:, CC['ebase'][0]:CC['ebase'][1]] = (np.arange(256) * CAP + 1)[None, :]
    c[:, CC['mask01'][0] + 1] = (np.arange(128) % 64 >= 32)
    return c


class Buf:
    __slots__ = ('t', 'w', 'r', 'name', 'x')

    def __init__(self, t, name='', x=False):
        self.t = t
        self.w = None
        self.r = {}
        self.name = name
        self.x = x


class Ring:
    def __init__(self, bufs):
        self.b = bufs
        self.i = 0

    def next(self):
        b = self.b[self.i % len(self.b)]
        self.i += 1
        return b


class MK:
    def __init__(self, nc, es):
        self.nc = nc
        self.E = dict(pe=nc.tensor, act=nc.scalar, dve=nc.vector, pool=nc.gpsimd, sp=nc.sync)
        self.semobj = {}
        self.cnt = {}
        for k in self.E:
            self.semobj[k] = es.enter_context(nc.semaphore('s_' + k))
            self.cnt[k] = 0
        self.seen = {}
        self.dq = {}
        for q, n in (('sp', 28), ('pool', 28), ('act', 8)):
            keys = []
            for i in range(n):
                key = ('d', q, i)
                self.semobj[key] = es.enter_context(nc.semaphore('d_%s%d' % (q, i)))
                self.cnt[key] = 0
                keys.append(key)
            self.dq[q] = [keys, 0]
        self.ninst = 0

    def _wait(self, eng, key, val):
        if self.seen.get((eng, key), 0) >= val:
            return
        self.E[eng].wait_ge(self.semobj[key], val)
        self.seen[(eng, key)] = val

    def _deps(self, eng, reads, writes):
        need = {}
        for b in reads:
            if b.w is not None:
                k, v = b.w
                need[k] = max(need.get(k, 0), v)
            if b.x:
                for k, v in b.r.items():
                    if k != eng:
                        need[k] = max(need.get(k, 0), v)
        for b in writes:
            if b.w is not None:
                k, v = b.w
                if k != eng:
                    need[k] = max(need.get(k, 0), v)
            for k, v in b.r.items():
                if k != eng:
                    need[k] = max(need.get(k, 0), v)
        if eng == 'pe':
            need.pop('pe', None)
        for k, v in need.items():
            self._wait(eng, k, v)

    def _commit(self, tok, reads, writes):
        k, v = tok
        for b in reads:
            b.r[k] = max(b.r.get(k, 0), v)
        for b in writes:
            b.w = tok
            b.r = {}

    def op(self, eng, fn, reads=(), writes=()):
        self._deps(eng, reads, writes)
        ins = fn(self.E[eng])
        self.cnt[eng] += 1
        ins.then_inc(self.semobj[eng], 1)
        tok = (eng, self.cnt[eng])
        self._commit(tok, reads, writes)
        self.ninst += 1
        return tok

    def dma(self, q, fn, reads=(), writes=()):
        self._deps(q, reads, writes)
        keys, i = self.dq[q]
        key = keys[i % len(keys)]
        self.dq[q][1] = i + 1
        if self.cnt[key] > 0:
            self._wait(q, key, self.cnt[key])
        ins = fn(self.E[q])
        self.cnt[key] += 16
        ins.then_inc(self.semobj[key], 16)
        tok = (key, self.cnt[key])
        self._commit(tok, reads, writes)
        self.ninst += 1
        return tok

    def barrier(self, bufs=()):
        for eng in self.E:
            for k, v in self.cnt.items():
                if v > 0:
                    self._wait(eng, k, v)

    def wait_all_on(self, eng):
        for k, v in self.cnt.items():
            if v > 0:
                self._wait(eng, k, v)


class Ctx:
    pass


def build(cfg):
    NS = cfg.get('nseq', 4)
    NL = cfg.get('nlayer', 2)
    dbg = cfg.get('dbg', None)
    NT = NS * T
    nc = bass.Bass("TRN2", target_bir_lowering=False)
    K = Ctx()
    K.nc = nc
    K.cfg = cfg
    K.NS, K.NL, K.NT = NS, NL, NT
    dr = {}

    def dram(name, shape, dt, kind):
        dr[name] = Buf(nc.dram_tensor(name, shape, dt, kind=kind).ap(), name)
        return dr[name]

    dram('xT', [D, NT], F32, "ExternalInput")
    dram('w_in', [2, D, WIN], F32, "ExternalInput")
    dram('w_branch', [2, 4, 256, D], F32, "ExternalInput")
    dram('w_out', [2, D, D], F32, "ExternalInput")
    dram('w_router', [2, D, NE], F32, "ExternalInput")
    if cfg.get('stageB', True):
        NEX = cfg.get('ne', NE)
        dram('w_exp_gate', [2, NEX, D, 256], F32, "ExternalInput")
        dram('w_exp_up', [2, NEX, D, 256], F32, "ExternalInput")
        dram('w_exp_down', [2, NEX, 256, D], F32, "ExternalInput")
    dram('w_sh_gate', [2, D, 256], F32, "ExternalInput")
    dram('w_sh_up', [2, D, 256], F32, "ExternalInput")
    dram('w_sh_down', [2, 256, D], F32, "ExternalInput")
    dram('cst', [128, NCST], F32, "ExternalInput")
    dram('parA', [2, 128, NPA], F32, "ExternalInput")
    dram('parB', [2, 128, NPB], F32, "ExternalInput")
    dram('out', [NT, D], F32, "ExternalOutput")
    dram('x1T', [D, NT], F32, "Internal")
    dram('x1r', [NT, D], F32, "ExternalOutput" if dbg else "Internal")
    dram('x1rb', [NT + 128, D], BF16, "Internal")
    dram('x2T', [D, NT], F32, "Internal")
    dram('tokw', [NE * CAP, 2], I32, "Internal")
    dram('ye', [NE * CAP, D], BF16, "Internal")
    if dbg:
        for name, shape in dbg.items():
            dram(name, shape, F32, "ExternalOutput")
    K.dr = dr

    with ExitStack() as es:
        mk = MK(nc, es)
        K.mk = mk
        K.es = es

        K.uid = 0

        def sb(name, shape, dt, st=None):
            K.uid += 1
            nm = "%s_%d" % (name, K.uid)
            return Buf((st or es).enter_context(nc.sbuf_tensor(nm, shape, dt)), nm)

        K.sb = sb
        K.dumps = {}

        def dump(name, buf, ap, shape):
            if not cfg.get('dbg'):
                return
            if name in K.dumps:
                return
            d_ = Buf(nc.dram_tensor(name, shape, F32, kind="ExternalOutput").ap(), name)
            K.dumps[name] = d_
            mk.dma('pool', lambda e: e.dma_start(out=d_.t, in_=ap), reads=[buf], writes=[d_])

        K.dump = dump
        K.pst = [es.enter_context(nc.psum_tensor("ps%d" % i, [128, 1024], F32)) for i in range(4)]
        K.ps = []
        for i in range(4):
            for h in range(2):
                K.ps.append(Buf(K.pst[i][:, h * 512:(h + 1) * 512], "ps%d_%d" % (i, h), x=True))
        setup_consts(K)
        for l in range(NL):
            src = dr['xT'] if l == 0 else dr['x2T']
            if cfg.get('stageA', True):
                for s in range(NS):
                    stage_a(K, l, s, src)
            if cfg.get('stageB', True):
                stage_b(K, l, last=(l == NL - 1))
        mk.wait_all_on('sp')
        mk.wait_all_on('pool')
    K.ninst = mk.ninst
    return nc, K


def setup_consts(K):
    mk, sb, dr = K.mk, K.sb, K.dr
    cst = dr['cst']
    K.c = {}
    big = ('C64', 'S64', 'C32', 'S32')
    small = ('R64', 'R32', 'ident', 'tribias', 'esel', 'ones', 'eh4', 'lstrict')
    for name in big + small:
        a, b = CC[name]
        K.c[name] = sb('c_' + name, [128, b - a], BF16)
    for name in ('ident', 'tribig', 'pow2', 'ones', 'mask01', 'ebase'):
        a, b = CC[name]
        dst = sb('cf_' + name, [128, b - a], F32)
        mk.dma('sp', lambda e: e.dma_start(out=dst.t[:, :], in_=cst.t[:, a:b]), reads=[cst], writes=[dst])
        K.c[name + '_f'] = dst
    with ExitStack() as st:
        tmp = sb('cst_tmp', [128, T], F32, st)
        for name in big + small:
            a, b = CC[name]
            w = b - a
            dst = K.c[name]
            mk.dma('sp', lambda e: e.dma_start(out=tmp.t[:, 0:w], in_=cst.t[:, a:b]), reads=[cst], writes=[tmp])
            mk.op('act', lambda e: e.activation(out=dst.t[:, :], in_=tmp.t[:, 0:w], func=AF.Copy), reads=[tmp], writes=[dst])
        mk.barrier()


def stage_a(K, l, s, src):
    mk, sb, dr, nc = K.mk, K.sb, K.dr, K.nc
    ps = K.ps
    c = K.c
    w_in = dr['w_in']
    w_in_v = w_in.t[l].rearrange("(c p) n -> p c n", p=128)
    t0g = s * T
    lam_init = 0.8 - 0.6 * float(np.exp(-0.3 * l))
    dbg = K.cfg.get('dbg') or {}
    branches = K.cfg.get('branches', (0, 1, 2, 3))

    def loadw(wb, col0, ncol, dcol=0):
        mk.dma('pool', lambda e: e.dma_start(out=wb.t[:, :, dcol:dcol + ncol], in_=w_in_v[:, :, col0:col0 + ncol]),
               reads=[w_in], writes=[wb])

    with ExitStack() as sa:
        xb = sb('xb', [128, 8, T], BF16, sa)
        par = sb('parA', [128, NPA], F32, sa)
        oT = [sb('oT%d' % i, [128, 2, T], BF16, sa) for i in range(4)]
        mk.dma('sp', lambda e: e.dma_start(out=par.t[:, :], in_=dr['parA'].t[l]), reads=[dr['parA']], writes=[par])
        srcv = src.t.rearrange("(c p) n -> p c n", p=128)
        for cc in range(8):
            mk.dma('pool', lambda e: e.dma_start(out=xb.t[:, cc, :], in_=srcv[:, cc, t0g:t0g + T]), reads=[src], writes=[xb])

        with ExitStack() as sbr:
            wq = Ring([sb('wq%d' % i, [128, 8, 256], BF16, sbr) for i in range(2)])
            qT = sb('qT', [128, 2, T], BF16, sbr)
            kT = sb('kT', [128, 2, T], BF16, sbr)
            Vp = sb('Vp', [128, 16, 4, 128], BF16, sbr)
            ptr = Ring([sb('pt%d' % i, [128, 512], BF16, sbr) for i in range(4)])
            tq = Ring([sb('tq%d' % i, [128, 512], BF16, sbr) for i in range(2)])
            t1r = Ring([sb('t1_%d' % i, [128, 512], F32, sbr) for i in range(2)])
            t2r = Ring([sb('t2_%d' % i, [128, 512], F32, sbr) for i in range(2)])
            rsr = Ring([sb('rs%d' % i, [64, 512], F32, sbr) for i in range(2)])
            small = sb('small', [128, 64], F32, sbr)
            pp = Ring([ps[0], ps[1], ps[2], ps[3]])
            sr = Ring([ps[4], ps[5], ps[2], ps[3]])
            accr = Ring([ps[6], ps[7]])
            mk.op('pool', lambda e: e.memset(Vp.t[:, :, :, 64:128], 1.0), writes=[Vp])

            def proj_fm(col0, nchunk, dst, rope=None, wb=None, wcol0=0):
                if wb is None:
                    wb = wq.next()
                    loadw(wb, col0, nchunk * 128)
                for m in range(nchunk):
                    for tc in range(4):
                        pb = pp.next()
                        for cc in range(8):
                            mk.op('pe', lambda e: e.matmul(pb.t[:, :], lhsT=wb.t[:, cc, wcol0 + m * 128:wcol0 + (m + 1) * 128],
                                                           rhs=xb.t[:, cc, tc * 512:(tc + 1) * 512], start=(cc == 0), stop=(cc == 7)),
                                  reads=[wb, xb], writes=[pb])
                        dsl = dst.t[:, m, tc * 512:(tc + 1) * 512]
                        if rope is None:
                            mk.op('act', lambda e: e.activation(out=dsl, in_=pb.t[:, :], func=AF.Copy), reads=[pb], writes=[dst])
                        else:
                            Cb, Sb, Rb = rope
                            qs = tq.next()
                            mk.op('act', lambda e: e.activation(out=qs.t[:, :], in_=pb.t[:, :], func=AF.Copy), reads=[pb], writes=[qs])
                            rp = pp.next()
                            mk.op('pe', lambda e: e.matmul(rp.t[:, :], lhsT=Rb.t[:, :], rhs=qs.t[:, :], start=True, stop=True),
                                  reads=[Rb, qs], writes=[rp])
                            t1 = t1r.next()
                            t2 = t2r.next()
                            mk.op('dve', lambda e: e.tensor_tensor(out=t1.t[:, :], in0=rp.t[:, :], in1=Sb.t[:, tc * 512:(tc + 1) * 512], op=ALU.mult),
                                  reads=[rp, Sb], writes=[t1])
                            mk.op('pool', lambda e: e.tensor_tensor(out=t2.t[:, :], in0=qs.t[:, :], in1=Cb.t[:, tc * 512:(tc + 1) * 512], op=ALU.mult),
                                  reads=[qs, Cb], writes=[t2])
                            mk.op('dve', lambda e: e.tensor_tensor(out=dsl, in0=t1.t[:, :], in1=t2.t[:, :], op=ALU.add),
                                  reads=[t1, t2], writes=[dst])

            def proj_v(col0):
                wb = wq.next()
                loadw(wb, col0, 256)
                for tt in range(16):
                    pb = pp.next()
                    for cc in range(8):
                        mk.op('pe', lambda e: e.matmul(pb.t[:, 0:256], lhsT=xb.t[:, cc, tt * 128:(tt + 1) * 128], rhs=wb.t[:, cc, 0:256],
                                                       start=(cc == 0), stop=(cc == 7)), reads=[wb, xb], writes=[pb])
                    mk.op('act', lambda e: e.activation(out=Vp.t[:, tt, :, 0:64], in_=pb.t[:, 0:256].rearrange("p (h d) -> p h d", h=4), func=AF.Copy),
                          reads=[pb], writes=[Vp])

            def attention(nsm, kdim, qk_ops, bias_fn, act_bias_fn, vhead_fn, fin_fn, scale, cq_outer=False, kbuf_fn=None):
                steps = []
                if cq_outer:
                    for cq in range(4):
                        for s_ in range(nsm):
                            for j in range(4 * cq + 4):
                                steps.append((s_, cq, j))
                else:
                    for s_ in range(nsm):
                        for cq in range(4):
                            for j in range(4 * cq + 4):
                                steps.append((s_, cq, j))
                pend = []
                accs = {}
                for (s_, cq, j) in steps:
                    ch, base = qk_ops(s_)
                    q0 = max(512 * cq, 128 * j)
                    N = 512 * (cq + 1) - q0
                    off = q0 - 512 * cq
                    sbk = sr.next()
                    kb_ = kT if kbuf_fn is None else kbuf_fn(s_)
                    mk.op('pe', lambda e: e.matmul(sbk.t[:, 0:N], lhsT=kb_.t[base:base + kdim, ch, j * 128:(j + 1) * 128],
                                                   rhs=qT.t[base:base + kdim, ch, q0:q0 + N], start=True, stop=False, skip_group_check=True),
                          reads=[kb_, qT], writes=[sbk])
                    bias_fn(s_, cq, j, q0, N, sbk)
                    if len(pend) >= 2:
                        pend.pop(0)()
                    if j == 0:
                        accs[(s_, cq)] = accr.next()
                    acc = accs[(s_, cq)]
                    pt = ptr.next()
                    ab = act_bias_fn(s_, j)

                    if ab is None:
                        mk.op('act', lambda e: e.activation(out=pt.t[:, 0:N], in_=sbk.t[:, 0:N], func=AF.Exp, scale=scale),
                              reads=[sbk], writes=[pt])
                    else:
                        abuf, aap = ab
                        mk.op('act', lambda e: e.activation(out=pt.t[:, 0:N], in_=sbk.t[:, 0:N], func=AF.Exp, scale=scale, bias=aap),
                              reads=[sbk, abuf], writes=[pt])

                    def do_exp_pv(s_=s_, cq=cq, j=j, N=N, off=off, sbk=sbk, pt=pt, acc=acc, ab=ab):
                        h = vhead_fn(s_)
                        mk.op('pe', lambda e: e.matmul(acc.t[:, off:off + N], lhsT=Vp.t[:, j, h, :], rhs=pt.t[:, 0:N],
                                                       start=(j == 0), stop=(j == 4 * cq + 3), skip_group_check=True),
                              reads=[Vp, pt], writes=[acc])
                        if j == 4 * cq + 3:
                            fin_fn(s_, cq, acc)
                    pend.append(do_exp_pv)
                while pend:
                    pend.pop(0)()

            def causal_bias(sbk, j, cq, q0):
                if j >= 4 * cq:
                    mk.op('pe', lambda e: e.matmul(sbk.t[:, 0:128], lhsT=c['ident'].t[:, :], rhs=c['tribias'].t[:, :],
                                                   start=False, stop=True, skip_group_check=True),
                          reads=[c['ident'], c['tribias']], writes=[sbk])

            def fin_std(dst):
                def f(s_, cq, acc):
                    rs = rsr.next()
                    mk.op('act', lambda e: e.activation(out=rs.t[0:64, :], in_=acc.t[64:128, :], func=AF.Copy), reads=[acc], writes=[rs])
                    mk.op('dve', lambda e: e.reciprocal(out=rs.t[0:64, :], in_=rs.t[0:64, :]), reads=[rs], writes=[rs])
                    pb = (s_ % 2) * 64
                    mk.op('dve', lambda e: e.tensor_tensor(out=dst.t[pb:pb + 64, s_ // 2, cq * 512:(cq + 1) * 512], in0=acc.t[0:64, :],
                                                           in1=rs.t[0:64, :], op=ALU.mult), reads=[acc, rs], writes=[dst])
                return f

            if 0 in branches:
                rope64 = (c['C64'], c['S64'], c['R64'])
                proj_fm(OFF['moba_q'], 2, qT, rope64)
                proj_fm(OFF['moba_k'], 2, kT, rope64)
                proj_v(OFF['moba_v'])
                with ExitStack() as sm:
                    kmf = sb('kmf', [128, 2, 8], F32, sm)
                    kmb = sb('kmb', [128, 2, 8], BF16, sm)
                    biasT = sb('biasT', [8, 4, 1024], BF16, sm)
                    gm = sb('gm', [128, 8], F32, sm)
                    t8 = sb('t8', [128, 8], F32, sm)
                    bq = sb('bq', [128, 8], F32, sm)
                    mk.op('dve', lambda e: e.tensor_reduce(out=kmf.t[:, :, :], in_=kT.t[:, :, :].rearrange("p c (n k) -> p c n k", k=256),
                                                           axis=AX.X, op=ALU.add), reads=[kT], writes=[kmf])
                    mk.op('dve', lambda e: e.tensor_scalar(out=kmb.t[:, :, :], in0=kmf.t[:, :, :], scalar1=1.0 / 256, scalar2=None, op0=ALU.mult),
                          reads=[kmf], writes=[kmb])
                    for h in range(4):
                        ch, base = h // 2, (h % 2) * 64
                        for qt in range(8, 16):
                            cur = qt // 2
                            gp = pp.next()
                            mk.op('pe', lambda e: e.matmul(gp.t[:, 0:8], lhsT=qT.t[base:base + 64, ch, qt * 128:(qt + 1) * 128],
                                                           rhs=kmb.t[base:base + 64, ch, :], start=True, stop=True), reads=[qT, kmb], writes=[gp])
                            mk.op('dve', lambda e: e.memset(gm.t[:, :], -1e30), writes=[gm])
                            mk.op('dve', lambda e: e.tensor_copy(out=gm.t[:, 0:cur], in_=gp.t[:, 0:cur]), reads=[gp], writes=[gm])
                            mk.op('dve', lambda e: e.max(out=t8.t[:, :], in_=gm.t[:, :]), reads=[gm], writes=[t8])
                            mk.op('dve', lambda e: e.tensor_scalar(out=bq.t[:, :], in0=gm.t[:, :], scalar1=t8.t[:, 2:3], scalar2=NEG,
                                                                   op0=ALU.is_lt, op1=ALU.mult), reads=[gm, t8], writes=[bq])
                            mk.op('dve', lambda e: e.memset(bq.t[:, cur:8], 0.0), writes=[bq])
                            tp = pp.next()
                            mk.op('pe', lambda e: e.transpose(out=tp.t[0:8, 0:128], in_=bq.t[:, :], identity=c['ident_f'].t[:, :]),
                                  reads=[bq, c['ident_f']], writes=[tp])
                            mk.op('act', lambda e: e.activation(out=biasT.t[0:8, h, (qt - 8) * 128:(qt - 7) * 128], in_=tp.t[0:8, 0:128], func=AF.Copy),
                                  reads=[tp], writes=[biasT])

                    K.dump('d_biasT', biasT, biasT.t[0:8, :, :], [8, 4, 1024])
                    K.dump('d_kmb', kmb, kmb.t[:, :, :], [128, 2, 8])

                    def moba_bias(s_, cq, j, q0, N, sbk):
                        if cq >= 2:
                            n = j // 2
                            mk.op('pe', lambda e: e.matmul(sbk.t[:, 0:N], lhsT=c['esel'].t[0:8, n * 128:(n + 1) * 128],
                                                           rhs=biasT.t[0:8, s_, q0 - 1024:q0 - 1024 + N], start=False, stop=(j < 4 * cq),
                                                           skip_group_check=True), reads=[c['esel'], biasT], writes=[sbk])
                        causal_bias(sbk, j, cq, q0)

                    attention(4, 64, lambda s_: (s_ // 2, (s_ % 2) * 64), moba_bias, lambda s_, j: None, lambda s_: s_, fin_std(oT[0]), 0.125)

            if 1 in branches:
                rope32 = (c['C32'], c['S32'], c['R32'])
                proj_fm(OFF['diff_q'], 2, qT, rope32)
                proj_fm(OFF['diff_k'], 2, kT, rope32)
                proj_v(OFF['diff_v'])
                with ExitStack() as sm:
                    lamt = sb('lamt', [128, 8], F32, sm)
                    dtmp = sb('dtmp', [128, 128], F32, sm)
                    hold = sb('hold', [64, 512], F32, sm)
                    dd = sb('dd', [64, 512], F32, sm)
                    sq = sb('sq', [64, 512], F32, sm)
                    gs = sb('gs', [64, 1], F32, sm)
                    kT1 = sb('kT1', [128, 2, T], BF16, sm)
                    mk.op('pool', lambda e: e.tensor_scalar(out=kT1.t[:, :, :], in0=kT.t[:, :, :], scalar1=c['mask01_f'].t[:, 1:2], scalar2=None, op0=ALU.mult),
                          reads=[kT, c['mask01_f']], writes=[kT1])
                    mk.op('dve', lambda e: e.tensor_scalar(out=kT.t[:, :, :], in0=kT.t[:, :, :], scalar1=c['mask01_f'].t[:, 0:1], scalar2=None, op0=ALU.mult),
                          reads=[kT, c['mask01_f']], writes=[kT])
                    a, b = PA['dl']
                    mk.op('dve', lambda e: e.tensor_tensor(out=dtmp.t[:, 0:32], in0=par.t[:, a:a + 32], in1=par.t[:, a + 32:a + 64], op=ALU.mult),
                          reads=[par], writes=[dtmp])
                    mk.op('dve', lambda e: e.tensor_tensor(out=dtmp.t[:, 32:64], in0=par.t[:, a + 64:a + 96], in1=par.t[:, a + 96:a + 128], op=ALU.mult),
                          reads=[par], writes=[dtmp])
                    mk.op('dve', lambda e: e.tensor_reduce(out=lamt.t[:, 0:2], in_=dtmp.t[:, 0:64].rearrange("p (a b) -> p a b", a=2), axis=AX.X, op=ALU.add),
                          reads=[dtmp], writes=[lamt])
                    mk.op('act', lambda e: e.activation(out=lamt.t[:, 2:4], in_=lamt.t[:, 0:2], func=AF.Exp), reads=[lamt], writes=[lamt])
                    mk.op('dve', lambda e: e.tensor_tensor(out=lamt.t[:, 4:5], in0=lamt.t[:, 3:4], in1=lamt.t[:, 2:3], op=ALU.subtract),
                          reads=[lamt], writes=[lamt])
                    mk.op('dve', lambda e: e.tensor_scalar(out=lamt.t[:, 5:6], in0=lamt.t[:, 4:5], scalar1=-lam_init, scalar2=None, op0=ALU.add),
                          reads=[lamt], writes=[lamt])
                    sa_, sb_ = PA['subln']
                    mk.op('dve', lambda e: e.tensor_scalar(out=gs.t[:, :], in0=par.t[0:64, sa_:sb_], scalar1=(1.0 - lam_init), scalar2=None, op0=ALU.mult),
                          reads=[par], writes=[gs])

                    K.dump('d_lamt', lamt, lamt.t[:, :], [128, 8])
                    K.dump('d_gs', gs, gs.t[:, :], [64, 1])

                    def fin_diff(s_, cq, acc):
                        h, comp = s_ // 2, s_ % 2
                        rs = rsr.next()
                        mk.op('act', lambda e: e.activation(out=rs.t[0:64, :], in_=acc.t[64:128, :], func=AF.Copy), reads=[acc], writes=[rs])
                        mk.op('dve', lambda e: e.reciprocal(out=rs.t[0:64, :], in_=rs.t[0:64, :]), reads=[rs], writes=[rs])
                        if comp == 0:
                            mk.op('dve', lambda e: e.tensor_tensor(out=hold.t[:, :], in0=acc.t[0:64, :], in1=rs.t[0:64, :], op=ALU.mult),
                                  reads=[acc, rs], writes=[hold])
                            return
                        mk.op('dve', lambda e: e.tensor_tensor(out=dd.t[:, :], in0=acc.t[0:64, :], in1=rs.t[0:64, :], op=ALU.mult),
                              reads=[acc, rs], writes=[dd])
                        mk.op('dve', lambda e: e.scalar_tensor_tensor(out=dd.t[:, :], in0=dd.t[:, :], scalar=lamt.t[0:64, 5:6], in1=hold.t[:, :],
                                                                      op0=ALU.mult, op1=ALU.add), reads=[dd, lamt, hold], writes=[dd])
                        K.dump('d_hold', hold, hold.t[:, :], [64, 512])
                        K.dump('d_dd', dd, dd.t[:, :], [64, 512])
                        mk.op('pool', lambda e: e.tensor_tensor(out=sq.t[:, :], in0=dd.t[:, :], in1=dd.t[:, :], op=ALU.mult), reads=[dd], writes=[sq])
                        mp = pp.next()
                        mk.op('pe', lambda e: e.matmul(mp.t[0:64, :], lhsT=c['ones_f'].t[0:64, 0:64], rhs=sq.t[:, :], start=True, stop=True),
                              reads=[c['ones_f'], sq], writes=[mp])
                        mk.op('act', lambda e: e.activation(out=sq.t[:, :], in_=mp.t[0:64, :], func=AF.Ln, scale=1.0 / 64, bias=LN_EPS), reads=[mp], writes=[sq])
                        mk.op('act', lambda e: e.activation(out=sq.t[:, :], in_=sq.t[:, :], func=AF.Exp, scale=-0.5), reads=[sq], writes=[sq])
                        pb = (h % 2) * 64
                        mk.op('dve', lambda e: e.scalar_tensor_tensor(out=oT[1].t[pb:pb + 64, h // 2, cq * 512:(cq + 1) * 512], in0=dd.t[:, :],
                                                                      scalar=gs.t[:, 0:1], in1=sq.t[:, :], op0=ALU.mult, op1=ALU.mult),
                              reads=[dd, gs, sq], writes=[oT[1]])

                    def diff_bias(s_, cq, j, q0, N, sbk):
                        causal_bias(sbk, j, cq, q0)

                    attention(8, 64, lambda s_: (s_ // 4, ((s_ // 2) % 2) * 64), diff_bias, lambda s_, j: None, lambda s_: s_ // 2, fin_diff,
                              float(32 ** -0.5), cq_outer=True, kbuf_fn=lambda s_: (kT if s_ % 2 == 0 else kT1))

            if 2 in branches:
                proj_fm(OFF['fox_q'], 2, qT, None)
                proj_fm(OFF['fox_k'], 2, kT, None)
                proj_v(OFF['fox_v'])
                with ExitStack() as sm:
                    wf = sb('wf', [128, 8, 4], BF16, sm)
                    ncT = sb('ncT', [4, T], F32, sm)
                    e1 = sb('e1', [4, T], F32, sm)
                    cq8 = sb('cq8', [4, T], BF16, sm)
                    nck = sb('nck', [128, 16, 4], F32, sm)
                    loadw(wf, OFF['fox_f'], 4)
                    na, nb = PA['negb']
                    mk.op('dve', lambda e: e.tensor_scalar(out=small.t[0:4, 0:1], in0=par.t[0:4, na:nb], scalar1=-1.0, scalar2=None, op0=ALU.mult),
                          reads=[par], writes=[small])
                    for tc in range(4):
                        fp = pp.next()
                        for cc in range(8):
                            mk.op('pe', lambda e: e.matmul(fp.t[0:4, :], lhsT=wf.t[:, cc, 0:4], rhs=xb.t[:, cc, tc * 512:(tc + 1) * 512],
                                                           start=(cc == 0), stop=(cc == 7)), reads=[wf, xb], writes=[fp])
                        mk.op('act', lambda e: e.activation(out=e1.t[:, tc * 512:(tc + 1) * 512], in_=fp.t[0:4, :], func=AF.Exp, scale=-1.0,
                                                            bias=small.t[0:4, 0:1]), reads=[fp, small], writes=[e1])
                    mk.op('act', lambda e: e.activation(out=e1.t[:, :], in_=e1.t[:, :], func=AF.Ln, scale=1.0, bias=1.0), reads=[e1], writes=[e1])
                    mk.op('dve', lambda e: e.tensor_tensor_scan(out=ncT.t[:, :], data0=c['ones_f'].t[0:4, 0:1].to_broadcast([4, T]),
                                                                data1=e1.t[:, :], initial=0.0, op0=ALU.mult, op1=ALU.add),
                          reads=[e1, c['ones_f']], writes=[ncT])
                    mk.op('dve', lambda e: e.tensor_scalar(out=cq8.t[:, :], in0=ncT.t[:, :], scalar1=-8.0, scalar2=None, op0=ALU.mult),
                          reads=[ncT], writes=[cq8])
                    tp = pp.next()
                    for tt in range(16):
                        mk.op('pe', lambda e: e.transpose(out=tp.t[:, tt * 4:(tt + 1) * 4], in_=ncT.t[0:4, tt * 128:(tt + 1) * 128],
                                                          identity=c['ident_f'].t[0:4, 0:4]), reads=[ncT, c['ident_f']], writes=[tp])
                    mk.op('act', lambda e: e.activation(out=nck.t[:, :, :], in_=tp.t[:, 0:64].rearrange("p (a b) -> p a b", b=4), func=AF.Copy),
                          reads=[tp], writes=[nck])

                    def fox_bias(s_, cq, j, q0, N, sbk):
                        mk.op('pe', lambda e: e.matmul(sbk.t[:, 0:N], lhsT=c['eh4'].t[0:4, s_ * 128:(s_ + 1) * 128], rhs=cq8.t[0:4, q0:q0 + N],
                                                       start=False, stop=(j < 4 * cq), skip_group_check=True), reads=[c['eh4'], cq8], writes=[sbk])
                        causal_bias(sbk, j, cq, q0)

                    attention(4, 64, lambda s_: (s_ // 2, (s_ % 2) * 64), fox_bias, lambda s_, j: (nck, nck.t[:, j, s_:s_ + 1]), lambda s_: s_,
                              fin_std(oT[2]), 0.125)

            if 3 in branches:
                rope64 = (c['C64'], c['S64'], c['R64'])
                proj_fm(OFF['dsa_q'], 2, qT, rope64)
                proj_fm(OFF['dsa_k'], 2, kT, rope64)
                proj_v(OFF['dsa_v'])
                with ExitStack() as sm:
                    qiT = sb('qiT', [128, 2, T], BF16, sm)
                    kiT = sb('kiT', [128, 1, T], BF16, sm)
                    wk2 = sb('wk2', [128, 8, 128], BF16, sm)
                    ww = sb('ww', [128, 8, 4], BF16, sm)
                    wiT = sb('wiT', [4, T], F32, sm)
                    wi = sb('wi', [128, 16, 4], F32, sm)
                    score = sb('score', [128, T], F32, sm)
                    rl = Ring([sb('rl%d' % i, [128, 1024], F32, sm) for i in range(2)])
                    mb = sb('mb', [128, 4, T], BF16, sm)
                    bs = sb('bs', [128, 64], F32, sm)
                    proj_fm(OFF['idx_q'], 2, qiT, rope64)
                    loadw(wk2, OFF['idx_k'], 64, 0)
                    loadw(wk2, OFF['idx_k'], 64, 64)
                    proj_fm(0, 1, kiT, rope64, wb=wk2)
                    loadw(ww, OFF['idx_w'], 4)
                    for tc in range(4):
                        fp = pp.next()
                        for cc in range(8):
                            mk.op('pe', lambda e: e.matmul(fp.t[0:4, :], lhsT=ww.t[:, cc, 0:4], rhs=xb.t[:, cc, tc * 512:(tc + 1) * 512],
                                                           start=(cc == 0), stop=(cc == 7)), reads=[ww, xb], writes=[fp])
                        mk.op('act', lambda e: e.activation(out=wiT.t[:, tc * 512:(tc + 1) * 512], in_=fp.t[0:4, :], func=AF.Copy), reads=[fp], writes=[wiT])
                    tp = pp.next()
                    for tt in range(16):
                        mk.op('pe', lambda e: e.transpose(out=tp.t[:, tt * 4:(tt + 1) * 4], in_=wiT.t[0:4, tt * 128:(tt + 1) * 128],
                                                          identity=c['ident_f'].t[0:4, 0:4]), reads=[wiT, c['ident_f']], writes=[tp])
                    mk.op('act', lambda e: e.activation(out=wi.t[:, :, :], in_=tp.t[:, 0:64].rearrange("p (a b) -> p a b", b=4), func=AF.Copy),
                          reads=[tp], writes=[wi])
                    big = [(K.pst[0], [ps[0], ps[1]]), (K.pst[1], [ps[2], ps[3]])]

                    def indexer(cq):
                        for ql in range(4):
                            qt = 4 * cq + ql
                            KL = (qt + 1) * 128
                            for h in range(4):
                                ch, base = h // 2, (h % 2) * 64
                                r = rl.next()
                                for blk in range(0, KL, 1024):
                                    bt, bb = big[(blk // 1024) % 2]
                                    w_ = min(1024, KL - blk)
                                    for sub in range(0, w_, 512):
                                        sw = min(512, w_ - sub)
                                        mk.op('pe', lambda e: e.matmul(bt[:, sub:sub + sw], lhsT=qiT.t[base:base + 64, ch, qt * 128:(qt + 1) * 128],
                                                                       rhs=kiT.t[base:base + 64, 0, blk + sub:blk + sub + sw], start=True, stop=True),
                                              reads=[qiT, kiT], writes=[bb[sub // 512]])
                                    mk.op('act', lambda e: e.activation(out=r.t[:, 0:w_], in_=bt[:, 0:w_], func=AF.Relu), reads=bb, writes=[r])
                                    if h == 0:
                                        mk.op('dve', lambda e: e.tensor_scalar(out=score.t[:, blk:blk + w_], in0=r.t[:, 0:w_], scalar1=wi.t[:, qt, 0:1],
                                                                               scalar2=None, op0=ALU.mult), reads=[r, wi], writes=[score])
                                    else:
                                        mk.op('dve', lambda e: e.scalar_tensor_tensor(out=score.t[:, blk:blk + w_], in0=r.t[:, 0:w_],
                                                                                      scalar=wi.t[:, qt, h:h + 1], in1=score.t[:, blk:blk + w_],
                                                                                      op0=ALU.mult, op1=ALU.add), reads=[r, wi, score], writes=[score])
                            if qt >= 2:
                                mk.op('dve', lambda e: e.tensor_reduce(out=bs.t[:, 0:1], in_=score.t[:, 0:KL], axis=AX.X, op=ALU.max), reads=[score], writes=[bs])
                                mk.op('dve', lambda e: e.tensor_reduce(out=bs.t[:, 1:2], in_=score.t[:, 0:KL], axis=AX.X, op=ALU.min), reads=[score], writes=[bs])
                            mk.op('pool', lambda e: e.tensor_tensor(out=score.t[:, qt * 128:KL], in0=score.t[:, qt * 128:KL], in1=c['tribig_f'].t[:, :], op=ALU.add),
                                  reads=[score, c['tribig_f']], writes=[score])
                            if qt >= 2:
                                mk.op('dve', lambda e: e.tensor_scalar(out=bs.t[:, 2:3], in0=bs.t[:, 1:2], scalar1=-1.0, scalar2=None, op0=ALU.add), reads=[bs], writes=[bs])
                                mk.op('dve', lambda e: e.tensor_tensor(out=bs.t[:, 3:4], in0=bs.t[:, 0:1], in1=bs.t[:, 1:2], op=ALU.subtract), reads=[bs], writes=[bs])
                                mk.op('dve', lambda e: e.tensor_scalar(out=bs.t[:, 3:4], in0=bs.t[:, 3:4], scalar1=2.0, scalar2=None, op0=ALU.add), reads=[bs], writes=[bs])
                                mk.op('dve', lambda e: e.tensor_scalar(out=bs.t[:, 8:8 + NBIS], in0=c['pow2_f'].t[:, 0:NBIS], scalar1=bs.t[:, 3:4], scalar2=None, op0=ALU.mult),
                                      reads=[bs, c['pow2_f']], writes=[bs])
                                for it in range(NBIS):
                                    mk.op('dve', lambda e: e.tensor_tensor(out=bs.t[:, 4:5], in0=bs.t[:, 2:3], in1=bs.t[:, 8 + it:9 + it], op=ALU.add), reads=[bs], writes=[bs])
                                    mk.op('dve', lambda e: e.tensor_scalar(out=mb.t[:, ql, 0:KL], in0=score.t[:, 0:KL], scalar1=bs.t[:, 4:5], scalar2=None,
                                                                           op0=ALU.is_gt, op1=ALU.add, accum_out=bs.t[:, 5:6]), reads=[score, bs], writes=[mb, bs])
                                    mk.op('dve', lambda e: e.tensor_scalar(out=bs.t[:, 6:7], in0=bs.t[:, 5:6], scalar1=255.5, scalar2=bs.t[:, 8 + it:9 + it],
                                                                           op0=ALU.is_gt, op1=ALU.mult), reads=[bs], writes=[bs])
                                    mk.op('dve', lambda e: e.tensor_tensor(out=bs.t[:, 2:3], in0=bs.t[:, 2:3], in1=bs.t[:, 6:7], op=ALU.add), reads=[bs], writes=[bs])
                            else:
                                mk.op('dve', lambda e: e.memset(bs.t[:, 2:3], -1e29), writes=[bs])
                            mk.op('dve', lambda e: e.tensor_scalar(out=mb.t[:, ql, 0:KL], in0=score.t[:, 0:KL], scalar1=bs.t[:, 2:3], scalar2=NEG,
                                                                   op0=ALU.is_le, op1=ALU.mult), reads=[score, bs], writes=[mb])
                            if qt == 5:
                                K.dump('d_bs5', bs, bs.t[:, :], [128, 64])
                                K.dump('d_score5', score, score.t[:, 0:KL], [128, KL])
                                K.dump('d_mb5', mb, mb.t[:, ql, 0:KL], [128, KL])
                            if qt == 1:
                                K.dump('d_score1', score, score.t[:, 0:KL], [128, KL])
                                K.dump('d_mb1', mb, mb.t[:, ql, 0:KL], [128, KL])

                    def dsa_bias(s_, cq, j, q0, N, sbk):
                        if s_ == 0 and j == 0:
                            indexer(cq)
                        qts = [qt for qt in range(4 * cq, 4 * cq + 4) if qt >= j]
                        for i, qt in enumerate(qts):
                            o_ = qt * 128 - q0
                            mk.op('pe', lambda e: e.matmul(sbk.t[:, o_:o_ + 128], lhsT=mb.t[:, qt - 4 * cq, j * 128:(j + 1) * 128], rhs=c['ident'].t[:, :],
                                                           start=False, stop=(i == len(qts) - 1), skip_group_check=True), reads=[mb, c['ident']], writes=[sbk])

                    attention(4, 64, lambda s_: (s_ // 2, (s_ % 2) * 64), dsa_bias, lambda s_, j: None, lambda s_: s_, fin_std(oT[3]), 0.125, cq_outer=True)
            mk.barrier()

        if 'oT' in dbg:
            for i in range(4):
                for ch in range(2):
                    mk.dma('pool', lambda e: e.dma_start(out=dr['oT'].t[i * 256 + ch * 128:i * 256 + (ch + 1) * 128, :], in_=oT[i].t[:, ch, :]),
                           reads=[oT[i]], writes=[dr['oT']])
        mk.barrier()
        if not K.cfg.get('merge', True):
            return
        mg = sb('mg', [128, 8, T], BF16, sa)
        pp = Ring([ps[0], ps[1], ps[2], ps[3]])
        pq = Ring([ps[4], ps[5], ps[6], ps[7]])
        with ExitStack() as sm:
            wbr = sb('wbr', [128, 4, 2, D], BF16, sm)
            wg = Ring([sb('wg%d' % i, [128, 8, 4, 128], BF16, sm) for i in range(2)])
            sgr = Ring([sb('sg%d' % i, [128, 512], F32, sm) for i in range(2)])
            tmr = Ring([sb('tm%d' % i, [128, 512], F32, sm) for i in range(2)])
            acr = Ring([sb('ac%d' % i, [128, 512], F32, sm) for i in range(2)])
            wbv = dr['w_branch'].t[l].rearrange("i (kc p) n -> p i kc n", p=128)
            for i in range(4):
                mk.dma('pool', lambda e: e.dma_start(out=wbr.t[:, i, :, :], in_=wbv[:, i, :, :]), reads=[dr['w_branch']], writes=[wbr])
            g0 = OFF['gates']
            wgv = w_in_v[:, :, g0:g0 + 4096].rearrange("p c (i n) -> p c i n", i=4)
            for m in range(8):
                wgt = wg.next()
                for i in range(4):
                    mk.dma('pool', lambda e: e.dma_start(out=wgt.t[:, :, i, :], in_=wgv[:, :, i, m * 128:(m + 1) * 128]), reads=[w_in], writes=[wgt])
                for tc in range(4):
                    ac = acr.next()
                    for i in range(4):
                        gp = pp.next()
                        for cc in range(8):
                            mk.op('pe', lambda e: e.matmul(gp.t[:, :], lhsT=wgt.t[:, cc, i, :], rhs=xb.t[:, cc, tc * 512:(tc + 1) * 512],
                                                           start=(cc == 0), stop=(cc == 7)), reads=[wgt, xb], writes=[gp])
                        sg = sgr.next()
                        mk.op('act', lambda e: e.activation(out=sg.t[:, :], in_=gp.t[:, :], func=AF.Sigmoid), reads=[gp], writes=[sg])
                        bp = pq.next()
                        for kc in range(2):
                            mk.op('pe', lambda e: e.matmul(bp.t[:, :], lhsT=wbr.t[:, i, kc, m * 128:(m + 1) * 128], rhs=oT[i].t[:, kc, tc * 512:(tc + 1) * 512],
                                                           start=(kc == 0), stop=(kc == 1)), reads=[wbr, oT[i]], writes=[bp])
                        if i == 0:
                            mk.op('dve', lambda e: e.tensor_tensor(out=ac.t[:, :], in0=bp.t[:, :], in1=sg.t[:, :], op=ALU.mult), reads=[bp, sg], writes=[ac])
                        else:
                            tm = tmr.next()
                            mk.op('dve', lambda e: e.tensor_tensor(out=tm.t[:, :], in0=bp.t[:, :], in1=sg.t[:, :], op=ALU.mult), reads=[bp, sg], writes=[tm])
                            if i < 3:
                                mk.op('pool', lambda e: e.tensor_tensor(out=ac.t[:, :], in0=ac.t[:, :], in1=tm.t[:, :], op=ALU.add), reads=[ac, tm], writes=[ac])
                            else:
                                mk.op('pool', lambda e: e.tensor_tensor(out=mg.t[:, m, tc * 512:(tc + 1) * 512], in0=ac.t[:, :], in1=tm.t[:, :], op=ALU.add),
                                      reads=[ac, tm], writes=[mg])
            mk.barrier()
        if not K.cfg.get('ln1', True):
            return
        with ExitStack() as so:
            wo = sb('wo', [128, 8, D], BF16, so)
            xres = sb('xres', [128, 8, 512], F32, so)
            y = sb('y', [128, 8, 512], F32, so)
            sqr = Ring([sb('sq%d' % i, [128, 512], F32, so) for i in range(2)])
            mean = sb('mean', [128, 512], F32, so)
            rstd = sb('rstd', [128, 512], F32, so)
            msq = sb('msq', [128, 512], F32, so)
            rowf = Ring([sb('rowf%d' % i, [128, D], F32, so) for i in range(2)])
            rowb = Ring([sb('rowb%d' % i, [128, D], BF16, so) for i in range(2)])
            wov = dr['w_out'].t[l].rearrange("(kc p) n -> p kc n", p=128)
            for kc in range(8):
                mk.dma('pool', lambda e: e.dma_start(out=wo.t[:, kc, :], in_=wov[:, kc, :]), reads=[dr['w_out']], writes=[wo])
            x1Tv = dr['x1T'].t.rearrange("(c p) n -> p c n", p=128)
            ga, gb = PA['ln1_g']
            ba, bb_ = PA['ln1_b']
            big = [(K.pst[0], [ps[0], ps[1]]), (K.pst[1], [ps[2], ps[3]])]
            for tc in range(4):
                c0 = t0g + tc * 512
                mk.dma('sp', lambda e: e.dma_start(out=xres.t[:, :, :], in_=srcv[:, :, c0:c0 + 512]), reads=[src], writes=[xres])
                for m in range(8):
                    hp = pq.next()
                    for kc in range(8):
                        mk.op('pe', lambda e: e.matmul(hp.t[:, :], lhsT=wo.t[:, kc, m * 128:(m + 1) * 128], rhs=mg.t[:, kc, tc * 512:(tc + 1) * 512],
                                                       start=(kc == 0), stop=(kc == 7)), reads=[wo, mg], writes=[hp])
                    mk.op('dve', lambda e: e.scalar_tensor_tensor(out=y.t[:, m, :], in0=xres.t[:, m, :], scalar=ALPHA, in1=hp.t[:, :], op0=ALU.mult, op1=ALU.add),
                          reads=[xres, hp], writes=[y])
                s1, s2 = ps[4], ps[5]
                for m in range(8):
                    mk.op('pe', lambda e: e.matmul(s1.t[:, :], lhsT=c['ones_f'].t[:, :], rhs=y.t[:, m, :], start=(m == 0), stop=(m == 7)),
                          reads=[c['ones_f'], y], writes=[s1])
                for m in range(8):
                    sq = sqr.next()
                    mk.op('act', lambda e: e.activation(out=sq.t[:, :], in_=y.t[:, m, :], func=AF.Square), reads=[y], writes=[sq])
                    mk.op('pe', lambda e: e.matmul(s2.t[:, :], lhsT=c['ones_f'].t[:, :], rhs=sq.t[:, :], start=(m == 0), stop=(m == 7)),
                          reads=[c['ones_f'], sq], writes=[s2])
                mk.op('act', lambda e: e.activation(out=mean.t[:, :], in_=s1.t[:, :], func=AF.Copy, scale=1.0 / D), reads=[s1], writes=[mean])
                mk.op('pool', lambda e: e.tensor_tensor(out=msq.t[:, :], in0=mean.t[:, :], in1=mean.t[:, :], op=ALU.mult), reads=[mean], writes=[msq])
                mk.op('dve', lambda e: e.scalar_tensor_tensor(out=rstd.t[:, :], in0=s2.t[:, :], scalar=1.0 / D, in1=msq.t[:, :], op0=ALU.mult, op1=ALU.subtract),
                      reads=[s2, msq], writes=[rstd])
                mk.op('act', lambda e: e.activation(out=rstd.t[:, :], in_=rstd.t[:, :], func=AF.Ln, scale=1.0, bias=LN_EPS), reads=[rstd], writes=[rstd])
                mk.op('act', lambda e: e.activation(out=rstd.t[:, :], in_=rstd.t[:, :], func=AF.Exp, scale=-0.5), reads=[rstd], writes=[rstd])
                for m in range(8):
                    mk.op('dve', lambda e: e.tensor_tensor(out=y.t[:, m, :], in0=y.t[:, m, :], in1=mean.t[:, :], op=ALU.subtract), reads=[y, mean], writes=[y])
                    mk.op('pool', lambda e: e.tensor_tensor(out=y.t[:, m, :], in0=y.t[:, m, :], in1=rstd.t[:, :], op=ALU.mult), reads=[y, rstd], writes=[y])
                    mk.op('act', lambda e: e.activation(out=y.t[:, m, :], in_=y.t[:, m, :], func=AF.Identity, scale=par.t[:, ga + m:ga + m + 1],
                                                        bias=par.t[:, ba + m:ba + m + 1]), reads=[y, par], writes=[y])
                mk.dma('sp', lambda e: e.dma_start(out=x1Tv[:, :, c0:c0 + 512], in_=y.t[:, :, :]), reads=[y], writes=[dr['x1T']])
                for tt in range(4):
                    bt, bb2 = big[tt % 2]
                    for m in range(8):
                        mk.op('pe', lambda e: e.transpose(out=bt[:, m * 128:(m + 1) * 128], in_=y.t[:, m, tt * 128:(tt + 1) * 128], identity=c['ident_f'].t[:, :]),
                              reads=[y, c['ident_f']], writes=[bb2[m // 4]])
                    rf = rowf.next()
                    rb = rowb.next()
                    mk.op('act', lambda e: e.activation(out=rf.t[:, :], in_=bt[:, :], func=AF.Copy), reads=bb2, writes=[rf])
                    mk.op('pool', lambda e: e.tensor_copy(out=rb.t[:, :], in_=rf.t[:, :]), reads=[rf], writes=[rb])
                    r0 = c0 + tt * 128
                    mk.dma('sp', lambda e: e.dma_start(out=dr['x1r'].t[r0:r0 + 128, :], in_=rf.t[:, :]), reads=[rf], writes=[dr['x1r']])
                    mk.dma('sp', lambda e: e.dma_start(out=dr['x1rb'].t[r0:r0 + 128, :], in_=rb.t[:, :]), reads=[rb], writes=[dr['x1rb']])
            mk.barrier()


def stage_b(K, l, last):
    mk, sb, dr, nc, ps, c = K.mk, K.sb, K.dr, K.nc, K.ps, K.c
    NT = K.NT
    NTL = NT // 128
    NEX = K.cfg.get('ne', NE)
    IOA = bass.IndirectOffsetOnAxis
    pp = Ring([ps[0], ps[1], ps[2], ps[3]])
    pq = Ring([ps[4], ps[5], ps[6], ps[7]])
    big = [(K.pst[0], [ps[0], ps[1]]), (K.pst[1], [ps[2], ps[3]])]
    x1Tv = dr['x1T'].t.rearrange("(c p) n -> p c n", p=128)
    with ExitStack() as sbs:
        par = sb('parB', [128, NPB], F32, sbs)
        slotall = sb('slotall', [128, NTL, 8], I32, sbs)
        mk.dma('sp', lambda e: e.dma_start(out=par.t[:, :], in_=dr['parB'].t[l]), reads=[dr['parB']], writes=[par])
        with ExitStack() as s1:
            wr = sb('wr', [128, 8, 256], BF16, s1)
            carry = sb('carry', [128, 256], F32, s1)
            tokid = sb('tokid', [128, NTL], I32, s1)
            fill = sb('fill', [128, NE * CAP * 2 // 128], I32, s1)
            zrow = sb('zrow', [128, D], BF16, s1)
            xtbr = Ring([sb('xtb%d' % i, [128, 8, 128], BF16, s1) for i in range(2)])
            f256 = {n: sb(n, [128, 256], F32, s1) for n in ('sc', 'bi', 'eq', 'bi2', 'msk', 'sel', 'wd', 'pos', 'slotf', 'junk')}
            selb = sb('selb', [128, 256], BF16, s1)
            sm8 = sb('sm8', [128, 64], F32, s1)
            stg = sb('stg', [128, 8, 2], I32, s1)
            mk.dma('pool', lambda e: e.dma_start(out=wr.t[:, :, :], in_=dr['w_router'].t[l].rearrange("(c p) n -> p c n", p=128)),
                   reads=[dr['w_router']], writes=[wr])
            mk.op('dve', lambda e: e.memset(carry.t[:, :], 0.0), writes=[carry])
            mk.op('pool', lambda e: e.iota(tokid.t[:, :], pattern=[[128, NTL]], base=0, channel_multiplier=1), writes=[tokid])
            mk.op('dve', lambda e: e.memset(fill.t[:, :], 0), writes=[fill])
            mk.op('dve', lambda e: e.memset(fill.t[:, :].rearrange("p (a b) -> p a b", b=2)[:, :, 0], NT), writes=[fill])
            mk.op('pool', lambda e: e.memset(zrow.t[:, :], 0.0), writes=[zrow])
            mk.dma('sp', lambda e: e.dma_start(out=dr['tokw'].t.rearrange("(p a) b -> p (a b)", p=128), in_=fill.t[:, :]), reads=[fill], writes=[dr['tokw']])
            mk.dma('sp', lambda e: e.dma_start(out=dr['x1rb'].t[NT:NT + 128, :], in_=zrow.t[:, :]), reads=[zrow], writes=[dr['x1rb']])
            if NEX < NE:
                yz = dr['ye'].t.rearrange("(p a) n -> p a n", p=128)
                for a0 in range(0, NE * CAP // 128, 96):
                    mk.dma('sp', lambda e: e.dma_start(out=yz[:, a0:a0 + 96, :], in_=zrow.t[:, :].unsqueeze(1).to_broadcast([128, 96, D])), reads=[zrow], writes=[dr['ye']])
            sc, bi, eq, bi2, msk, sel, wd, pos, slotf, junk = [f256[n] for n in ('sc', 'bi', 'eq', 'bi2', 'msk', 'sel', 'wd', 'pos', 'slotf', 'junk')]
            ra, rb_ = PB['rbias']
            for tt in range(NTL):
                xtb = xtbr.next()
                mk.dma('pool', lambda e: e.dma_start(out=xtb.t[:, :, :], in_=x1Tv[:, :, tt * 128:(tt + 1) * 128]), reads=[dr['x1T']], writes=[xtb])
                lp = pp.next()
                for cc in range(8):
                    mk.op('pe', lambda e: e.matmul(lp.t[:, 0:256], lhsT=xtb.t[:, cc, :], rhs=wr.t[:, cc, :], start=(cc == 0), stop=(cc == 7)),
                          reads=[xtb, wr], writes=[lp])
                mk.op('act', lambda e: e.activation(out=sc.t[:, :], in_=lp.t[:, 0:256], func=AF.Sigmoid), reads=[lp], writes=[sc])
                mk.op('dve', lambda e: e.tensor_tensor(out=bi.t[:, :], in0=sc.t[:, :], in1=par.t[:, ra:rb_], op=ALU.add), reads=[sc, par], writes=[bi])
                bi3 = bi.t[:, :].rearrange("p (g k) -> p g k", k=32)
                mk.op('dve', lambda e: e.tensor_reduce(out=sm8.t[:, 0:8], in_=bi3, axis=AX.X, op=ALU.max), reads=[bi], writes=[sm8])
                mk.op('dve', lambda e: e.tensor_tensor(out=eq.t[:, :].rearrange("p (g k) -> p g k", k=32), in0=bi3,
                                                       in1=sm8.t[:, 0:8].unsqueeze(2).to_broadcast([128, 8, 32]), op=ALU.is_equal), reads=[bi, sm8], writes=[eq])
                mk.op('dve', lambda e: e.scalar_tensor_tensor(out=bi2.t[:, :], in0=eq.t[:, :], scalar=-1e30, in1=bi.t[:, :], op0=ALU.mult, op1=ALU.add),
                      reads=[eq, bi], writes=[bi2])
                mk.op('dve', lambda e: e.tensor_reduce(out=sm8.t[:, 8:16], in_=bi2.t[:, :].rearrange("p (g k) -> p g k", k=32), axis=AX.X, op=ALU.max),
                      reads=[bi2], writes=[sm8])
                mk.op('dve', lambda e: e.tensor_tensor(out=sm8.t[:, 16:24], in0=sm8.t[:, 0:8], in1=sm8.t[:, 8:16], op=ALU.add), reads=[sm8], writes=[sm8])
                mk.op('dve', lambda e: e.max(out=sm8.t[:, 24:32], in_=sm8.t[:, 16:24]), reads=[sm8], writes=[sm8])
                mk.op('dve', lambda e: e.tensor_scalar(out=sm8.t[:, 32:40], in0=sm8.t[:, 16:24], scalar1=sm8.t[:, 27:28], scalar2=-1e30, op0=ALU.is_lt, op1=ALU.mult),
                      reads=[sm8], writes=[sm8])
                mk.op('dve', lambda e: e.tensor_tensor(out=msk.t[:, :].rearrange("p (g k) -> p g k", k=32), in0=bi3,
                                                       in1=sm8.t[:, 32:40].unsqueeze(2).to_broadcast([128, 8, 32]), op=ALU.add), reads=[bi, sm8], writes=[msk])
                mk.op('dve', lambda e: e.max(out=sm8.t[:, 40:48], in_=msk.t[:, :]), reads=[msk], writes=[sm8])
                mk.op('dve', lambda e: e.tensor_scalar(out=sel.t[:, :], in0=msk.t[:, :], scalar1=sm8.t[:, 47:48], scalar2=None, op0=ALU.is_ge), reads=[msk, sm8], writes=[sel])
                mk.op('dve', lambda e: e.scalar_tensor_tensor(out=wd.t[:, :], in0=sel.t[:, :], scalar=1.0, in1=sc.t[:, :], op0=ALU.mult, op1=ALU.mult,
                                                              accum_out=sm8.t[:, 48:49]), reads=[sel, sc], writes=[wd, sm8])
                mk.op('dve', lambda e: e.reciprocal(out=sm8.t[:, 49:50], in_=sm8.t[:, 48:49]), reads=[sm8], writes=[sm8])
                mk.op('dve', lambda e: e.tensor_scalar(out=wd.t[:, :], in0=wd.t[:, :], scalar1=sm8.t[:, 49:50], scalar2=2.5, op0=ALU.mult, op1=ALU.mult),
                      reads=[wd, sm8], writes=[wd])
                mk.op('pool', lambda e: e.tensor_copy(out=selb.t[:, :], in_=sel.t[:, :]), reads=[sel], writes=[selb])
                p1 = pq.next()
                p2 = pq.next()
                mk.op('pe', lambda e: e.matmul(p1.t[:, 0:256], lhsT=c['lstrict'].t[:, :], rhs=selb.t[:, :], start=True, stop=True), reads=[c['lstrict'], selb], writes=[p1])
                mk.op('pe', lambda e: e.matmul(p2.t[:, 0:256], lhsT=c['ones'].t[:, :], rhs=selb.t[:, :], start=True, stop=True), reads=[c['ones'], selb], writes=[p2])
                mk.op('dve', lambda e: e.tensor_tensor(out=pos.t[:, :], in0=p1.t[:, 0:256], in1=carry.t[:, :], op=ALU.add), reads=[p1, carry], writes=[pos])
                mk.op('dve', lambda e: e.tensor_tensor(out=carry.t[:, :], in0=p2.t[:, 0:256], in1=carry.t[:, :], op=ALU.add), reads=[p2, carry], writes=[carry])
                mk.op('dve', lambda e: e.scalar_tensor_tensor(out=pos.t[:, :], in0=pos.t[:, :], scalar=float(CAP - 1), in1=c['ebase_f'].t[:, :], op0=ALU.min, op1=ALU.add),
                      reads=[pos, c['ebase_f']], writes=[pos])
                mk.op('dve', lambda e: e.tensor_tensor(out=slotf.t[:, :], in0=pos.t[:, :], in1=sel.t[:, :], op=ALU.mult), reads=[pos, sel], writes=[slotf])
                mk.op('dve', lambda e: e.max(out=sm8.t[:, 50:58], in_=slotf.t[:, :]), reads=[slotf], writes=[sm8])
                for k in range(8):
                    mk.op('dve', lambda e: e.scalar_tensor_tensor(out=junk.t[:, :], in0=slotf.t[:, :], scalar=sm8.t[:, 50 + k:51 + k], in1=wd.t[:, :],
                                                                  op0=ALU.is_equal, op1=ALU.mult, accum_out=stg.t[:, k, 1:2].bitcast(F32)),
                          reads=[slotf, sm8, wd], writes=[junk, stg])
                mk.op('dve', lambda e: e.tensor_scalar(out=slotall.t[:, tt, :], in0=sm8.t[:, 50:58], scalar1=-1.0, scalar2=None, op0=ALU.add), reads=[sm8], writes=[slotall])
                mk.op('pool', lambda e: e.tensor_copy(out=stg.t[:, :, 0], in_=tokid.t[:, tt:tt + 1].to_broadcast([128, 8])), reads=[tokid], writes=[stg])
                for k in range(8):
                    mk.dma('pool', lambda e: e.indirect_dma_start(out=dr['tokw'].t[:, :], out_offset=IOA(slotall.t[:, tt, k:k + 1], 0), in_=stg.t[:, k, :], in_offset=None),
                           reads=[stg, slotall], writes=[dr['tokw']])
            mk.barrier()
        with ExitStack() as s2:
            twr = Ring([sb('tw%d' % i, [128, 3, 2], I32, s2) for i in range(5)])
            xer = Ring([sb('xe%d' % i, [128, 3, D], BF16, s2) for i in range(5)])
            xeTr = Ring([sb('xeT%d' % i, [128, 8, CAP], BF16, s2) for i in range(5)])
            wgur = Ring([sb('wgu%d' % i, [128, 8, 512], BF16, s2) for i in range(5)])
            wdnr = Ring([sb('wdn%d' % i, [128, 2, D], BF16, s2) for i in range(5)])
            hTr = Ring([sb('hT%d' % i, [128, 2, CAP], BF16, s2) for i in range(5)])
            sgr = Ring([sb('sge%d' % i, [128, CAP], F32, s2) for i in range(2)])
            yor = Ring([sb('yo%d' % i, [128, 3, D], BF16, s2) for i in range(5)])
            twv = dr['tokw'].t.rearrange("(e rt p) c -> e p rt c", rt=3, p=128)
            yev = dr['ye'].t.rearrange("(e rt p) n -> e p rt n", rt=3, p=128)
            ev = [0]
            bufs = {}

            def st_load(ex):
                tw, xe, xeT, wgu, wdn, hT, yo = twr.next(), xer.next(), xeTr.next(), wgur.next(), wdnr.next(), hTr.next(), yor.next()
                bufs[ex] = (tw, xe, xeT, wgu, wdn, hT, yo)
                mk.dma('sp', lambda e: e.dma_start(out=tw.t[:, :, :], in_=twv[ex]), reads=[dr['tokw']], writes=[tw])
                mk.dma('pool', lambda e: e.dma_start(out=wgu.t[:, :, 0:256], in_=dr['w_exp_gate'].t[l, ex].rearrange("(c p) n -> p c n", p=128)),
                       reads=[dr['w_exp_gate']], writes=[wgu])
                mk.dma('pool', lambda e: e.dma_start(out=wgu.t[:, :, 256:512], in_=dr['w_exp_up'].t[l, ex].rearrange("(c p) n -> p c n", p=128)),
                       reads=[dr['w_exp_up']], writes=[wgu])
                mk.dma('pool', lambda e: e.dma_start(out=wdn.t[:, :, :], in_=dr['w_exp_down'].t[l, ex].rearrange("(c p) n -> p c n", p=128)),
                       reads=[dr['w_exp_down']], writes=[wdn])
                for rt in range(3):
                    mk.dma('pool', lambda e: e.indirect_dma_start(out=xe.t[:, rt, :], out_offset=None, in_=dr['x1rb'].t[:, :], in_offset=IOA(tw.t[:, rt, 0:1], 0)),
                           reads=[tw, dr['x1rb']], writes=[xe])

            def st_t(ex):
                tw, xe, xeT, wgu, wdn, hT, yo = bufs[ex]
                for rt in range(3):
                    bk = pp.next()
                    vb = bk.t.bitcast(BF16)
                    for cc in range(8):
                        mk.op('pe', lambda e: e.transpose(out=vb[:, cc * 128:(cc + 1) * 128], in_=xe.t[:, rt, cc * 128:(cc + 1) * 128], identity=c['ident'].t[:, :]),
                              reads=[xe, c['ident']], writes=[bk])
                    ev[0] += 1
                    if ev[0] % 2 == 0:
                        mk.op('act', lambda e: e.activation(out=xeT.t[:, :, rt * 128:(rt + 1) * 128], in_=vb.rearrange("p (c r) -> p c r", c=8), func=AF.Copy),
                              reads=[bk], writes=[xeT])
                    else:
                        mk.op('dve', lambda e: e.tensor_copy(out=xeT.t[:, :, rt * 128:(rt + 1) * 128], in_=vb.rearrange("p (c r) -> p c r", c=8)), reads=[bk], writes=[xeT])

            def st_gu(ex):
                tw, xe, xeT, wgu, wdn, hT, yo = bufs[ex]
                for fc in range(2):
                    gp = pq.next()
                    up_ = pq.next()
                    for cc in range(8):
                        mk.op('pe', lambda e: e.matmul(gp.t[:, 0:CAP], lhsT=wgu.t[:, cc, fc * 128:(fc + 1) * 128], rhs=xeT.t[:, cc, :], start=(cc == 0), stop=(cc == 7)),
                              reads=[wgu, xeT], writes=[gp])
                    for cc in range(8):
                        mk.op('pe', lambda e: e.matmul(up_.t[:, 0:CAP], lhsT=wgu.t[:, cc, 256 + fc * 128:256 + (fc + 1) * 128], rhs=xeT.t[:, cc, :], start=(cc == 0), stop=(cc == 7)),
                              reads=[wgu, xeT], writes=[up_])
                    sg = sgr.next()
                    mk.op('act', lambda e: e.activation(out=sg.t[:, :], in_=gp.t[:, 0:CAP], func=AF.Silu), reads=[gp], writes=[sg])
                    mk.op('dve', lambda e: e.tensor_tensor(out=hT.t[:, fc, :], in0=up_.t[:, 0:CAP], in1=sg.t[:, :], op=ALU.mult), reads=[up_, sg], writes=[hT])

            def st_d(ex):
                tw, xe, xeT, wgu, wdn, hT, yo = bufs.pop(ex)
                twf = tw.t[:, :, :].bitcast(F32)
                for rt in range(3):
                    for half in range(2):
                        yp = pp.next()
                        for fc in range(2):
                            mk.op('pe', lambda e: e.matmul(yp.t[:, :], lhsT=hT.t[:, fc, rt * 128:(rt + 1) * 128], rhs=wdn.t[:, fc, half * 512:(half + 1) * 512],
                                                           start=(fc == 0), stop=(fc == 1)), reads=[hT, wdn], writes=[yp])
                        ev[0] += 1
                        if ev[0] % 2 == 0:
                            mk.op('act', lambda e: e.activation(out=yo.t[:, rt, half * 512:(half + 1) * 512], in_=yp.t[:, :], func=AF.Copy, scale=twf[:, rt, 1:2]),
                                  reads=[yp, tw], writes=[yo])
                        else:
                            mk.op('dve', lambda e: e.tensor_scalar(out=yo.t[:, rt, half * 512:(half + 1) * 512], in0=yp.t[:, :], scalar1=twf[:, rt, 1:2], scalar2=None, op0=ALU.mult),
                                  reads=[yp, tw], writes=[yo])
                mk.dma('sp', lambda e: e.dma_start(out=yev[ex], in_=yo.t[:, :, :]), reads=[yo], writes=[dr['ye']])

            for i in range(NEX + 3):
                if i < NEX:
                    st_load(i)
                if 0 <= i - 1 < NEX:
                    st_t(i - 1)
                if 0 <= i - 2 < NEX:
                    st_gu(i - 2)
                if 0 <= i - 3 < NEX:
                    st_d(i - 3)
            mk.barrier()
        with ExitStack() as s3:
            wsgu = sb('wsgu', [128, 8, 512], BF16, s3)
            wsd = sb('wsd', [128, 2, D], BF16, s3)
            xrr = Ring([sb('xr%d' % i, [128, D], F32, s3) for i in range(2)])
            xtbr = Ring([sb('xtc%d' % i, [128, 8, 128], BF16, s3) for i in range(2)])
            ygr = Ring([sb('yg%d' % i, [128, D], BF16, s3) for i in range(4)])
            accr = Ring([sb('acc%d' % i, [128, D], F32, s3) for i in range(2)])
            hs = sb('hs', [128, 2, 128], BF16, s3)
            sgs = sb('sgs', [128, 128], F32, s3)
            st = sb('st', [128, 16], F32, s3)
            jk = sb('jk', [128, D], F32, s3)
            xo = sb('xo', [128, 8, 128], F32, s3)
            mk.dma('pool', lambda e: e.dma_start(out=wsgu.t[:, :, 0:256], in_=dr['w_sh_gate'].t[l].rearrange("(c p) n -> p c n", p=128)), reads=[dr['w_sh_gate']], writes=[wsgu])
            mk.dma('pool', lambda e: e.dma_start(out=wsgu.t[:, :, 256:512], in_=dr['w_sh_up'].t[l].rearrange("(c p) n -> p c n", p=128)), reads=[dr['w_sh_up']], writes=[wsgu])
            mk.dma('pool', lambda e: e.dma_start(out=wsd.t[:, :, :], in_=dr['w_sh_down'].t[l].rearrange("(c p) n -> p c n", p=128)), reads=[dr['w_sh_down']], writes=[wsd])
            g0, g1 = PB['ln2_g']
            b0, b1 = PB['ln2_b']
            x2Tv = dr['x2T'].t.rearrange("(c p) n -> p c n", p=128)
            for tt in range(NTL):
                xr, xtb, acc = xrr.next(), xtbr.next(), accr.next()
                mk.dma('sp', lambda e: e.dma_start(out=xr.t[:, :], in_=dr['x1r'].t[tt * 128:(tt + 1) * 128, :]), reads=[dr['x1r']], writes=[xr])
                mk.dma('pool', lambda e: e.dma_start(out=xtb.t[:, :, :], in_=x1Tv[:, :, tt * 128:(tt + 1) * 128]), reads=[dr['x1T']], writes=[xtb])
                for fc in range(2):
                    gp = pq.next()
                    up_ = pq.next()
                    for cc in range(8):
                        mk.op('pe', lambda e: e.matmul(gp.t[:, 0:128], lhsT=wsgu.t[:, cc, fc * 128:(fc + 1) * 128], rhs=xtb.t[:, cc, :], start=(cc == 0), stop=(cc == 7)),
                              reads=[wsgu, xtb], writes=[gp])
                    for cc in range(8):
                        mk.op('pe', lambda e: e.matmul(up_.t[:, 0:128], lhsT=wsgu.t[:, cc, 256 + fc * 128:256 + (fc + 1) * 128], rhs=xtb.t[:, cc, :], start=(cc == 0), stop=(cc == 7)),
                              reads=[wsgu, xtb], writes=[up_])
                    mk.op('act', lambda e: e.activation(out=sgs.t[:, :], in_=gp.t[:, 0:128], func=AF.Silu), reads=[gp], writes=[sgs])
                    mk.op('dve', lambda e: e.tensor_tensor(out=hs.t[:, fc, :], in0=up_.t[:, 0:128], in1=sgs.t[:, :], op=ALU.mult), reads=[up_, sgs], writes=[hs])
                for half in range(2):
                    yp = pp.next()
                    for fc in range(2):
                        mk.op('pe', lambda e: e.matmul(yp.t[:, :], lhsT=hs.t[:, fc, :], rhs=wsd.t[:, fc, half * 512:(half + 1) * 512], start=(fc == 0), stop=(fc == 1)),
                              reads=[hs, wsd], writes=[yp])
                    mk.op('dve', lambda e: e.scalar_tensor_tensor(out=acc.t[:, half * 512:(half + 1) * 512], in0=xr.t[:, half * 512:(half + 1) * 512], scalar=ALPHA,
                                                                  in1=yp.t[:, :], op0=ALU.mult, op1=ALU.add), reads=[xr, yp], writes=[acc])
                for k in range(8):
                    yg = ygr.next()
                    mk.dma('pool', lambda e: e.indirect_dma_start(out=yg.t[:, :], out_offset=None, in_=dr['ye'].t[:, :], in_offset=IOA(slotall.t[:, tt, k:k + 1], 0)),
                           reads=[slotall, dr['ye']], writes=[yg])
                    mk.op('dve', lambda e: e.tensor_tensor(out=acc.t[:, :], in0=acc.t[:, :], in1=yg.t[:, :], op=ALU.add), reads=[acc, yg], writes=[acc])
                mk.op('act', lambda e: e.activation(out=jk.t[:, :], in_=acc.t[:, :], func=AF.Copy, accum_out=st.t[:, 0:1]), reads=[acc], writes=[jk, st])
                mk.op('act', lambda e: e.activation(out=jk.t[:, :], in_=acc.t[:, :], func=AF.Square, accum_out=st.t[:, 1:2]), reads=[acc], writes=[jk, st])
                mk.op('dve', lambda e: e.tensor_scalar(out=st.t[:, 2:4], in0=st.t[:, 0:2], scalar1=1.0 / D, scalar2=None, op0=ALU.mult), reads=[st], writes=[st])
                mk.op('dve', lambda e: e.tensor_tensor(out=st.t[:, 4:5], in0=st.t[:, 2:3], in1=st.t[:, 2:3], op=ALU.mult), reads=[st], writes=[st])
                mk.op('dve', lambda e: e.tensor_tensor(out=st.t[:, 5:6], in0=st.t[:, 3:4], in1=st.t[:, 4:5], op=ALU.subtract), reads=[st], writes=[st])
                mk.op('act', lambda e: e.activation(out=st.t[:, 6:7], in_=st.t[:, 5:6], func=AF.Ln, scale=1.0, bias=LN_EPS), reads=[st], writes=[st])
                mk.op('act', lambda e: e.activation(out=st.t[:, 7:8], in_=st.t[:, 6:7], func=AF.Exp, scale=-0.5), reads=[st], writes=[st])
                mk.op('dve', lambda e: e.tensor_scalar(out=acc.t[:, :], in0=acc.t[:, :], scalar1=st.t[:, 2:3], scalar2=st.t[:, 7:8], op0=ALU.subtract, op1=ALU.mult),
                      reads=[acc, st], writes=[acc])
                mk.op('pool', lambda e: e.tensor_tensor(out=acc.t[:, :], in0=acc.t[:, :], in1=par.t[:, g0:g1], op=ALU.mult), reads=[acc, par], writes=[acc])
                mk.op('dve', lambda e: e.tensor_tensor(out=acc.t[:, :], in0=acc.t[:, :], in1=par.t[:, b0:b1], op=ALU.add), reads=[acc, par], writes=[acc])
                if last:
                    mk.dma('sp', lambda e: e.dma_start(out=dr['out'].t[tt * 128:(tt + 1) * 128, :], in_=acc.t[:, :]), reads=[acc], writes=[dr['out']])
                else:
                    bt, bb2 = big[tt % 2]
                    for m in range(8):
                        mk.op('pe', lambda e: e.transpose(out=bt[:, m * 128:(m + 1) * 128], in_=acc.t[:, m * 128:(m + 1) * 128], identity=c['ident_f'].t[:, :]),
                              reads=[acc, c['ident_f']], writes=[bb2[m // 4]])
                    mk.op('act', lambda e: e.activation(out=xo.t[:, :, :], in_=bt[:, :].rearrange("p (c r) -> p c r", c=8), func=AF.Copy), reads=bb2, writes=[xo])
                    mk.dma('sp', lambda e: e.dma_start(out=x2Tv[:, :, tt * 128:(tt + 1) * 128], in_=xo.t[:, :, :]), reads=[xo], writes=[dr['x2T']])
            mk.barrier()


def host_prep(inputs, NS=4, experts=True):
    x = np.asarray(inputs['x'])
    cst = make_consts()
    parA = np.zeros((2, 128, NPA), np.float32)
    parB = np.zeros((2, 128, NPB), np.float32)
    for l in range(2):
        parA[l, :, 0:8] = np.asarray(inputs['ln1_g'][l]).reshape(8, 128).T
        parA[l, :, 8:16] = np.asarray(inputs['ln1_b'][l]).reshape(8, 128).T
        parA[l, 0:4, 16] = np.asarray(inputs['b_forget'][l])
        parA[l, :, 20:148] = np.asarray(inputs['diff_lambda'][l]).reshape(1, 128)
        parA[l, 0:64, 148] = np.asarray(inputs['diff_subln'][l])
        parB[l, :, 0:256] = np.asarray(inputs['router_bias'][l])[None, :]
        parB[l, :, 256:1280] = np.asarray(inputs['ln2_g'][l])[None, :]
        parB[l, :, 1280:2304] = np.asarray(inputs['ln2_b'][l])[None, :]
    shared = {k: np.ascontiguousarray(np.asarray(inputs[k], dtype=np.float32)) for k in
              ('w_in', 'w_branch', 'w_out', 'w_router', 'w_exp_gate', 'w_exp_up', 'w_exp_down', 'w_sh_gate', 'w_sh_up', 'w_sh_down')}
    if not experts:
        for k in ('w_exp_gate', 'w_exp_up', 'w_exp_down'):
            shared.pop(k)
    shared.update(cst=cst, parA=parA, parB=parB)
    maps = []
    for core in range(8):
        xs = x[core * 4:core * 4 + NS]
        xT = np.ascontiguousarray(xs.reshape(NS * T, D).T)
        m = dict(shared)
        m['xT'] = xT
        maps.append(m)
    return maps


_CACHE = {}


def kernel(**inputs):
    cfg = {}
    if 'nc' not in _CACHE:
        _CACHE['nc'] = build(cfg)
    nc, K = _CACHE['nc']
    maps = host_prep(inputs)
    res = run_bass_kernel_spmd(nc, maps, core_ids=list(range(8)))
    out = np.concatenate([np.asarray(r['out']).reshape(4, T, D) for r in res.results], axis=0)
    return out.astype(np.float32)
```

```python
import os
import numpy as np
from contextlib import ExitStack
import concourse.bass as bass
import concourse.mybir as mybir
from concourse.bass_utils import run_bass_kernel_spmd

F32 = mybir.dt.float32
BF16 = mybir.dt.bfloat16
I32 = mybir.dt.int32
U32 = mybir.dt.uint32
AF = mybir.ActivationFunctionType
ALU = mybir.AluOpType
AX = mybir.AxisListType

T = 2048
D = 1024
NEG = -30000.0
ALPHA = 4 ** 0.25
LN_EPS = 1e-5
NBIS = 18
CAP = 384
NE = 256

OFF = {}
_o = 0
for _n, _w in (('moba_q', 256), ('moba_k', 256), ('moba_v', 256), ('diff_q', 256), ('diff_k', 256), ('diff_v', 256),
               ('fox_q', 256), ('fox_k', 256), ('fox_v', 256), ('fox_f', 4), ('dsa_q', 256), ('dsa_k', 256), ('dsa_v', 256),
               ('idx_q', 256), ('idx_k', 64), ('idx_w', 4), ('gates', 4096)):
    OFF[_n] = _o
    _o += _w
WIN = _o

CC = {}
_o = 0
for _n, _w in (('C64', T), ('S64', T), ('C32', T), ('S32', T), ('R64', 128), ('R32', 128), ('ident', 128),
               ('tribias', 128), ('tribig', 128), ('esel', 1024), ('pow2', 32), ('ones', 128), ('eh4', 512), ('mask01', 2), ('lstrict', 128), ('ebase', 256)):
    CC[_n] = (_o, _o + _w)
    _o += _w
NCST = _o

PA = {'ln1_g': (0, 8), 'ln1_b': (8, 16), 'negb': (16, 17), 'dl': (20, 148), 'subln': (148, 149)}
NPA = 160
PB = {'rbias': (0, 256), 'ln2_g': (256, 1280), 'ln2_b': (1280, 2304)}
NPB = 2304


def make_consts():
    c = np.zeros((128, NCST), np.float32)
    pos = np.arange(T, dtype=np.float32)
    p = np.arange(128)
    inv64 = (10000.0 ** (-np.arange(32, dtype=np.float32) / 32)).astype(np.float32)
    inv32 = (10000.0 ** (-np.arange(16, dtype=np.float32) / 16)).astype(np.float32)
    a64 = (pos[None, :] * inv64[p % 32][:, None]).astype(np.float32)
    a32 = (pos[None, :] * inv32[p % 16][:, None]).astype(np.float32)
    c[:, CC['C64'][0]:CC['C64'][1]] = np.cos(a64.astype(np.float64))
    c[:, CC['S64'][0]:CC['S64'][1]] = np.sin(a64.astype(np.float64))
    c[:, CC['C32'][0]:CC['C32'][1]] = np.cos(a32.astype(np.float64))
    c[:, CC['S32'][0]:CC['S32'][1]] = np.sin(a32.astype(np.float64))
    R64 = np.zeros((128, 128), np.float32)
    R32 = np.zeros((128, 128), np.float32)
    for m in range(128):
        if m % 64 < 32:
            R64[m + 32, m] = -1.0
        else:
            R64[m - 32, m] = 1.0
        if m % 32 < 16:
            R32[m + 16, m] = -1.0
        else:
            R32[m - 16, m] = 1.0
    c[:, CC['R64'][0]:CC['R64'][1]] = R64
    c[:, CC['R32'][0]:CC['R32'][1]] = R32
    c[:, CC['ident'][0]:CC['ident'][1]] = np.eye(128, dtype=np.float32)
    r = np.arange(128)[:, None]
    q = np.arange(128)[None, :]
    c[:, CC['tribias'][0]:CC['tribias'][1]] = np.where(q < r, NEG, 0.0)
    c[:, CC['tribig'][0]:CC['tribig'][1]] = np.where(q > r, -1e30, 0.0)
    es = np.zeros((128, 8, 128), np.float32)
    for n in range(8):
        es[n, n, :] = 1.0
    c[:, CC['esel'][0]:CC['esel'][1]] = es.reshape(128, 1024)
    c[:, CC['pow2'][0]:CC['pow2'][1]] = (0.5 ** (np.arange(32) + 1))[None, :]
    c[:, CC['ones'][0]:CC['ones'][1]] = 1.0
    e4 = np.zeros((128, 4, 128), np.float32)
    for n in range(4):
        e4[n, n, :] = 1.0
    c[:, CC['eh4'][0]:CC['eh4'][1]] = e4.reshape(128, 512)
    c[:, CC['mask01'][0]] = (np.arange(128) % 64 < 32)
    c[:, CC['lstrict'][0]:CC['lstrict'][1]] = (np.arange(128)[:, None] < np.arange(128)[None, :])
    c[:, CC['ebase'][0]:CC['ebase'][1]] = (np.arange(256) * CAP + 1)[None, :]
    c[:, CC['mask01'][0] + 1] = (np.arange(128) % 64 >= 32)
    return c


class Buf:
    __slots__ = ('t', 'w', 'r', 'name', 'x')

    def __init__(self, t, name='', x=False):
        self.t = t
        self.w = None
        self.r = {}
        self.name = name
        self.x = x


class Ring:
    def __init__(self, bufs):
        self.b = bufs
        self.i = 0

    def next(self):
        b = self.b[self.i % len(self.b)]
        self.i += 1
        return b


class MK:
    def __init__(self, nc, es):
        self.nc = nc
        self.E = dict(pe=nc.tensor, act=nc.scalar, dve=nc.vector, pool=nc.gpsimd, sp=nc.sync)
        self.semobj = {}
        self.cnt = {}
        for k in self.E:
            self.semobj[k] = es.enter_context(nc.semaphore('s_' + k))
            self.cnt[k] = 0
        self.seen = {}
        self.dq = {}
        for q, n in (('sp', 28), ('pool', 28), ('act', 8)):
            keys = []
            for i in range(n):
                key = ('d', q, i)
                self.semobj[key] = es.enter_context(nc.semaphore('d_%s%d' % (q, i)))
                self.cnt[key] = 0
                keys.append(key)
            self.dq[q] = [keys, 0]
        self.ninst = 0

    def _wait(self, eng, key, val):
        if self.seen.get((eng, key), 0) >= val:
            return
        self.E[eng].wait_ge(self.semobj[key], val)
        self.seen[(eng, key)] = val

    def _deps(self, eng, reads, writes):
        need = {}
        for b in reads:
            if b.w is not None:
                k, v = b.w
                need[k] = max(need.get(k, 0), v)
            if b.x:
                for k, v in b.r.items():
                    if k != eng:
                        need[k] = max(need.get(k, 0), v)
        for b in writes:
            if b.w is not None:
                k, v = b.w
                if k != eng:
                    need[k] = max(need.get(k, 0), v)
            for k, v in b.r.items():
                if k != eng:
                    need[k] = max(need.get(k, 0), v)
        if eng == 'pe':
            need.pop('pe', None)
        for k, v in need.items():
            self._wait(eng, k, v)

    def _commit(self, tok, reads, writes):
        k, v = tok
        for b in reads:
            b.r[k] = max(b.r.get(k, 0), v)
        for b in writes:
            b.w = tok
            b.r = {}

    def op(self, eng, fn, reads=(), writes=()):
        self._deps(eng, reads, writes)
        ins = fn(self.E[eng])
        self.cnt[eng] += 1
        ins.then_inc(self.semobj[eng], 1)
        tok = (eng, self.cnt[eng])
        self._commit(tok, reads, writes)
        self.ninst += 1
        return tok

    def dma(self, q, fn, reads=(), writes=()):
        self._deps(q, reads, writes)
        keys, i = self.dq[q]
        key = keys[i % len(keys)]
        self.dq[q][1] = i + 1
        if self.cnt[key] > 0:
            self._wait(q, key, self.cnt[key])
        ins = fn(self.E[q])
        self.cnt[key] += 16
        ins.then_inc(self.semobj[key], 16)
        tok = (key, self.cnt[key])
        self._commit(tok, reads, writes)
        self.ninst += 1
        return tok

    def barrier(self, bufs=()):
        for eng in self.E:
            for k, v in self.cnt.items():
                if v > 0:
                    self._wait(eng, k, v)

    def wait_all_on(self, eng):
        for k, v in self.cnt.items():
            if v > 0:
                self._wait(eng, k, v)


class Ctx:
    pass


def build(cfg):
    NS = cfg.get('nseq', 4)
    NL = cfg.get('nlayer', 2)
    dbg = cfg.get('dbg', None)
    NT = NS * T
    nc = bass.Bass("TRN2", target_bir_lowering=False)
    K = Ctx()
    K.nc = nc
    K.cfg = cfg
    K.NS, K.NL, K.NT = NS, NL, NT
    dr = {}

    def dram(name, shape, dt, kind):
        dr[name] = Buf(nc.dram_tensor(name, shape, dt, kind=kind).ap(), name)
        return dr[name]

    dram('xT', [D, NT], F32, "ExternalInput")
    dram('w_in', [2, D, WIN], F32, "ExternalInput")
    dram('w_branch', [2, 4, 256, D], F32, "ExternalInput")
    dram('w_out', [2, D, D], F32, "ExternalInput")
    dram('w_router', [2, D, NE], F32, "ExternalInput")
    if cfg.get('stageB', True):
        NEX = cfg.get('ne', NE)
        dram('w_exp_gate', [2, NEX, D, 256], F32, "ExternalInput")
        dram('w_exp_up', [2, NEX, D, 256], F32, "ExternalInput")
        dram('w_exp_down', [2, NEX, 256, D], F32, "ExternalInput")
    dram('w_sh_gate', [2, D, 256], F32, "ExternalInput")
    dram('w_sh_up', [2, D, 256], F32, "ExternalInput")
    dram('w_sh_down', [2, 256, D], F32, "ExternalInput")
    dram('cst', [128, NCST], F32, "ExternalInput")
    dram('parA', [2, 128, NPA], F32, "ExternalInput")
    dram('parB', [2, 128, NPB], F32, "ExternalInput")
    dram('out', [NT, D], F32, "ExternalOutput")
    dram('x1T', [D, NT], F32, "Internal")
    dram('x1r', [NT, D], F32, "ExternalOutput" if dbg else "Internal")
    dram('x1rb', [NT + 128, D], BF16, "Internal")
    dram('x2T', [D, NT], F32, "Internal")
    dram('tokw', [NE * CAP, 2], I32, "Internal")
    dram('ye', [NE * CAP, D], BF16, "Internal")
    if dbg:
        for name, shape in dbg.items():
            dram(name, shape, F32, "ExternalOutput")
    K.dr = dr

    with ExitStack() as es:
        mk = MK(nc, es)
        K.mk = mk
        K.es = es

        K.uid = 0

        def sb(name, shape, dt, st=None):
            K.uid += 1
            nm = "%s_%d" % (name, K.uid)
            return Buf((st or es).enter_context(nc.sbuf_tensor(nm, shape, dt)), nm)

        K.sb = sb
        K.dumps = {}

        def dump(name, buf, ap, shape):
            if not cfg.get('dbg'):
                return
            if name in K.dumps:
                return
            d_ = Buf(nc.dram_tensor(name, shape, F32, kind="ExternalOutput").ap(), name)
            K.dumps[name] = d_
            mk.dma('pool', lambda e: e.dma_start(out=d_.t, in_=ap), reads=[buf], writes=[d_])

        K.dump = dump
        K.pst = [es.enter_context(nc.psum_tensor("ps%d" % i, [128, 1024], F32)) for i in range(4)]
        K.ps = []
        for i in range(4):
            for h in range(2):
                K.ps.append(Buf(K.pst[i][:, h * 512:(h + 1) * 512], "ps%d_%d" % (i, h), x=True))
        setup_consts(K)
        for l in range(NL):
            src = dr['xT'] if l == 0 else dr['x2T']
            if cfg.get('stageA', True):
                for s in range(NS):
                    stage_a(K, l, s, src)
            if cfg.get('stageB', True):
                stage_b(K, l, last=(l == NL - 1))
        mk.wait_all_on('sp')
        mk.wait_all_on('pool')
    K.ninst = mk.ninst
    return nc, K


def setup_consts(K):
    mk, sb, dr = K.mk, K.sb, K.dr
    cst = dr['cst']
    K.c = {}
    big = ('C64', 'S64', 'C32', 'S32')
    small = ('R64', 'R32', 'ident', 'tribias', 'esel', 'ones', 'eh4', 'lstrict')
    for name in big + small:
        a, b = CC[name]
        K.c[name] = sb('c_' + name, [128, b - a], BF16)
    for name in ('ident', 'tribig', 'pow2', 'ones', 'mask01', 'ebase'):
        a, b = CC[name]
        dst = sb('cf_' + name, [128, b - a], F32)
        mk.dma('sp', lambda e: e.dma_start(out=dst.t[:, :], in_=cst.t[:, a:b]), reads=[cst], writes=[dst])
        K.c[name + '_f'] = dst
    with ExitStack() as st:
        tmp = sb('cst_tmp', [128, T], F32, st)
        for name in big + small:
            a, b = CC[name]
            w = b - a
            dst = K.c[name]
            mk.dma('sp', lambda e: e.dma_start(out=tmp.t[:, 0:w], in_=cst.t[:, a:b]), reads=[cst], writes=[tmp])
            mk.op('act', lambda e: e.activation(out=dst.t[:, :], in_=tmp.t[:, 0:w], func=AF.Copy), reads=[tmp], writes=[dst])
        mk.barrier()


def stage_a(K, l, s, src):
    mk, sb, dr, nc = K.mk, K.sb, K.dr, K.nc
    ps = K.ps
    c = K.c
    w_in = dr['w_in']
    w_in_v = w_in.t[l].rearrange("(c p) n -> p c n", p=128)
    t0g = s * T
    lam_init = 0.8 - 0.6 * float(np.exp(-0.3 * l))
    dbg = K.cfg.get('dbg') or {}
    branches = K.cfg.get('branches', (0, 1, 2, 3))

    def loadw(wb, col0, ncol, dcol=0):
        mk.dma('pool', lambda e: e.dma_start(out=wb.t[:, :, dcol:dcol + ncol], in_=w_in_v[:, :, col0:col0 + ncol]),
               reads=[w_in], writes=[wb])

    with ExitStack() as sa:
        xb = sb('xb', [128, 8, T], BF16, sa)
        par = sb('parA', [128, NPA], F32, sa)
        oT = [sb('oT%d' % i, [128, 2, T], BF16, sa) for i in range(4)]
        mk.dma('sp', lambda e: e.dma_start(out=par.t[:, :], in_=dr['parA'].t[l]), reads=[dr['parA']], writes=[par])
        srcv = src.t.rearrange("(c p) n -> p c n", p=128)
        for cc in range(8):
            mk.dma('pool', lambda e: e.dma_start(out=xb.t[:, cc, :], in_=srcv[:, cc, t0g:t0g + T]), reads=[src], writes=[xb])

        with ExitStack() as sbr:
            wq = Ring([sb('wq%d' % i, [128, 8, 256], BF16, sbr) for i in range(2)])
            qT = sb('qT', [128, 2, T], BF16, sbr)
            kT = sb('kT', [128, 2, T], BF16, sbr)
            Vp = sb('Vp', [128, 16, 4, 128], BF16, sbr)
            ptr = Ring([sb('pt%d' % i, [128, 512], BF16, sbr) for i in range(4)])
            tq = Ring([sb('tq%d' % i, [128, 512], BF16, sbr) for i in range(2)])
            t1r = Ring([sb('t1_%d' % i, [128, 512], F32, sbr) for i in range(2)])
            t2r = Ring([sb('t2_%d' % i, [128, 512], F32, sbr) for i in range(2)])
            rsr = Ring([sb('rs%d' % i, [64, 512], F32, sbr) for i in range(2)])
            small = sb('small', [128, 64], F32, sbr)
            pp = Ring([ps[0], ps[1], ps[2], ps[3]])
            sr = Ring([ps[4], ps[5], ps[2], ps[3]])
            accr = Ring([ps[6], ps[7]])
            mk.op('pool', lambda e: e.memset(Vp.t[:, :, :, 64:128], 1.0), writes=[Vp])

            def proj_fm(col0, nchunk, dst, rope=None, wb=None, wcol0=0):
                if wb is None:
                    wb = wq.next()
                    loadw(wb, col0, nchunk * 128)
                for m in range(nchunk):
                    for tc in range(4):
                        pb = pp.next()
                        for cc in range(8):
                            mk.op('pe', lambda e: e.matmul(pb.t[:, :], lhsT=wb.t[:, cc, wcol0 + m * 128:wcol0 + (m + 1) * 128],
                                                           rhs=xb.t[:, cc, tc * 512:(tc + 1) * 512], start=(cc == 0), stop=(cc == 7)),
                                  reads=[wb, xb], writes=[pb])
                        dsl = dst.t[:, m, tc * 512:(tc + 1) * 512]
                        if rope is None:
                            mk.op('act', lambda e: e.activation(out=dsl, in_=pb.t[:, :], func=AF.Copy), reads=[pb], writes=[dst])
                        else:
                            Cb, Sb, Rb = rope
                            qs = tq.next()
                            mk.op('act', lambda e: e.activation(out=qs.t[:, :], in_=pb.t[:, :], func=AF.Copy), reads=[pb], writes=[qs])
                            rp = pp.next()
                            mk.op('pe', lambda e: e.matmul(rp.t[:, :], lhsT=Rb.t[:, :], rhs=qs.t[:, :], start=True, stop=True),
                                  reads=[Rb, qs], writes=[rp])
                            t1 = t1r.next()
                            t2 = t2r.next()
                            mk.op('dve', lambda e: e.tensor_tensor(out=t1.t[:, :], in0=rp.t[:, :], in1=Sb.t[:, tc * 512:(tc + 1) * 512], op=ALU.mult),
                                  reads=[rp, Sb], writes=[t1])
                            mk.op('pool', lambda e: e.tensor_tensor(out=t2.t[:, :], in0=qs.t[:, :], in1=Cb.t[:, tc * 512:(tc + 1) * 512], op=ALU.mult),
                                  reads=[qs, Cb], writes=[t2])
                            mk.op('dve', lambda e: e.tensor_tensor(out=dsl, in0=t1.t[:, :], in1=t2.t[:, :], op=ALU.add),
                                  reads=[t1, t2], writes=[dst])

            def proj_v(col0):
                wb = wq.next()
                loadw(wb, col0, 256)
                for tt in range(16):
                    pb = pp.next()
                    for cc in range(8):
                        mk.op('pe', lambda e: e.matmul(pb.t[:, 0:256], lhsT=xb.t[:, cc, tt * 128:(tt + 1) * 128], rhs=wb.t[:, cc, 0:256],
                                                       start=(cc == 0), stop=(cc == 7)), reads=[wb, xb], writes=[pb])
                    mk.op('act', lambda e: e.activation(out=Vp.t[:, tt, :, 0:64], in_=pb.t[:, 0:256].rearrange("p (h d) -> p h d", h=4), func=AF.Copy),
                          reads=[pb], writes=[Vp])

            def attention(nsm, kdim, qk_ops, bias_fn, act_bias_fn, vhead_fn, fin_fn, scale, cq_outer=False, kbuf_fn=None):
                steps = []
                if cq_outer:
                    for cq in range(4):
                        for s_ in range(nsm):
                            for j in range(4 * cq + 4):
                                steps.append((s_, cq, j))
                else:
                    for s_ in range(nsm):
                        for cq in range(4):
                            for j in range(4 * cq + 4):
                                steps.append((s_, cq, j))
                pend = []
                accs = {}
                for (s_, cq, j) in steps:
                    ch, base = qk_ops(s_)
                    q0 = max(512 * cq, 128 * j)
                    N = 512 * (cq + 1) - q0
                    off = q0 - 512 * cq
                    sbk = sr.next()
                    kb_ = kT if kbuf_fn is None else kbuf_fn(s_)
                    mk.op('pe', lambda e: e.matmul(sbk.t[:, 0:N], lhsT=kb_.t[base:base + kdim, ch, j * 128:(j + 1) * 128],
                                                   rhs=qT.t[base:base + kdim, ch, q0:q0 + N], start=True, stop=False, skip_group_check=True),
                          reads=[kb_, qT], writes=[sbk])
                    bias_fn(s_, cq, j, q0, N, sbk)
                    if len(pend) >= 2:
                        pend.pop(0)()
                    if j == 0:
                        accs[(s_, cq)] = accr.next()
                    acc = accs[(s_, cq)]
                    pt = ptr.next()
                    ab = act_bias_fn(s_, j)

                    if ab is None:
                        mk.op('act', lambda e: e.activation(out=pt.t[:, 0:N], in_=sbk.t[:, 0:N], func=AF.Exp, scale=scale),
                              reads=[sbk], writes=[pt])
                    else:
                        abuf, aap = ab
                        mk.op('act', lambda e: e.activation(out=pt.t[:, 0:N], in_=sbk.t[:, 0:N], func=AF.Exp, scale=scale, bias=aap),
                              reads=[sbk, abuf], writes=[pt])

                    def do_exp_pv(s_=s_, cq=cq, j=j, N=N, off=off, sbk=sbk, pt=pt, acc=acc, ab=ab):
                        h = vhead_fn(s_)
                        mk.op('pe', lambda e: e.matmul(acc.t[:, off:off + N], lhsT=Vp.t[:, j, h, :], rhs=pt.t[:, 0:N],
                                                       start=(j == 0), stop=(j == 4 * cq + 3), skip_group_check=True),
                              reads=[Vp, pt], writes=[acc])
                        if j == 4 * cq + 3:
                            fin_fn(s_, cq, acc)
                    pend.append(do_exp_pv)
                while pend:
                    pend.pop(0)()

            def causal_bias(sbk, j, cq, q0):
                if j >= 4 * cq:
                    mk.op('pe', lambda e: e.matmul(sbk.t[:, 0:128], lhsT=c['ident'].t[:, :], rhs=c['tribias'].t[:, :],
                                                   start=False, stop=True, skip_group_check=True),
                          reads=[c['ident'], c['tribias']], writes=[sbk])

            def fin_std(dst):
                def f(s_, cq, acc):
                    rs = rsr.next()
                    mk.op('act', lambda e: e.activation(out=rs.t[0:64, :], in_=acc.t[64:128, :], func=AF.Copy), reads=[acc], writes=[rs])
                    mk.op('dve', lambda e: e.reciprocal(out=rs.t[0:64, :], in_=rs.t[0:64, :]), reads=[rs], writes=[rs])
                    pb = (s_ % 2) * 64
                    mk.op('dve', lambda e: e.tensor_tensor(out=dst.t[pb:pb + 64, s_ // 2, cq * 512:(cq + 1) * 512], in0=acc.t[0:64, :],
                                                           in1=rs.t[0:64, :], op=ALU.mult), reads=[acc, rs], writes=[dst])
                return f

            if 0 in branches:
                rope64 = (c['C64'], c['S64'], c['R64'])
                proj_fm(OFF['moba_q'], 2, qT, rope64)
                proj_fm(OFF['moba_k'], 2, kT, rope64)
                proj_v(OFF['moba_v'])
                with ExitStack() as sm:
                    kmf = sb('kmf', [128, 2, 8], F32, sm)
                    kmb = sb('kmb', [128, 2, 8], BF16, sm)
                    biasT = sb('biasT', [8, 4, 1024], BF16, sm)
                    gm = sb('gm', [128, 8], F32, sm)
                    t8 = sb('t8', [128, 8], F32, sm)
                    bq = sb('bq', [128, 8], F32, sm)
                    mk.op('dve', lambda e: e.tensor_reduce(out=kmf.t[:, :, :], in_=kT.t[:, :, :].rearrange("p c (n k) -> p c n k", k=256),
                                                           axis=AX.X, op=ALU.add), reads=[kT], writes=[kmf])
                    mk.op('dve', lambda e: e.tensor_scalar(out=kmb.t[:, :, :], in0=kmf.t[:, :, :], scalar1=1.0 / 256, scalar2=None, op0=ALU.mult),
                          reads=[kmf], writes=[kmb])
                    for h in range(4):
                        ch, base = h // 2, (h % 2) * 64
                        for qt in range(8, 16):
                            cur = qt // 2
                            gp = pp.next()
                            mk.op('pe', lambda e: e.matmul(gp.t[:, 0:8], lhsT=qT.t[base:base + 64, ch, qt * 128:(qt + 1) * 128],
                                                           rhs=kmb.t[base:base + 64, ch, :], start=True, stop=True), reads=[qT, kmb], writes=[gp])
                            mk.op('dve', lambda e: e.memset(gm.t[:, :], -1e30), writes=[gm])
                            mk.op('dve', lambda e: e.tensor_copy(out=gm.t[:, 0:cur], in_=gp.t[:, 0:cur]), reads=[gp], writes=[gm])
                            mk.op('dve', lambda e: e.max(out=t8.t[:, :], in_=gm.t[:, :]), reads=[gm], writes=[t8])
                            mk.op('dve', lambda e: e.tensor_scalar(out=bq.t[:, :], in0=gm.t[:, :], scalar1=t8.t[:, 2:3], scalar2=NEG,
                                                                   op0=ALU.is_lt, op1=ALU.mult), reads=[gm, t8], writes=[bq])
                            mk.op('dve', lambda e: e.memset(bq.t[:, cur:8], 0.0), writes=[bq])
                            tp = pp.next()
                            mk.op('pe', lambda e: e.transpose(out=tp.t[0:8, 0:128], in_=bq.t[:, :], identity=c['ident_f'].t[:, :]),
                                  reads=[bq, c['ident_f']], writes=[tp])
                            mk.op('act', lambda e: e.activation(out=biasT.t[0:8, h, (qt - 8) * 128:(qt - 7) * 128], in_=tp.t[0:8, 0:128], func=AF.Copy),
                                  reads=[tp], writes=[biasT])

                    K.dump('d_biasT', biasT, biasT.t[0:8, :, :], [8, 4, 1024])
                    K.dump('d_kmb', kmb, kmb.t[:, :, :], [128, 2, 8])

                    def moba_bias(s_, cq, j, q0, N, sbk):
                        if cq >= 2:
                            n = j // 2
                            mk.op('pe', lambda e: e.matmul(sbk.t[:, 0:N], lhsT=c['esel'].t[0:8, n * 128:(n + 1) * 128],
                                                           rhs=biasT.t[0:8, s_, q0 - 1024:q0 - 1024 + N], start=False, stop=(j < 4 * cq),
                                                           skip_group_check=True), reads=[c['esel'], biasT], writes=[sbk])
                        causal_bias(sbk, j, cq, q0)

                    attention(4, 64, lambda s_: (s_ // 2, (s_ % 2) * 64), moba_bias, lambda s_, j: None, lambda s_: s_, fin_std(oT[0]), 0.125)

            if 1 in branches:
                rope32 = (c['C32'], c['S32'], c['R32'])
                proj_fm(OFF['diff_q'], 2, qT, rope32)
                proj_fm(OFF['diff_k'], 2, kT, rope32)
                proj_v(OFF['diff_v'])
                with ExitStack() as sm:
                    lamt = sb('lamt', [128, 8], F32, sm)
                    dtmp = sb('dtmp', [128, 128], F32, sm)
                    hold = sb('hold', [64, 512], F32, sm)
                    dd = sb('dd', [64, 512], F32, sm)
                    sq = sb('sq', [64, 512], F32, sm)
                    gs = sb('gs', [64, 1], F32, sm)
                    kT1 = sb('kT1', [128, 2, T], BF16, sm)
                    mk.op('pool', lambda e: e.tensor_scalar(out=kT1.t[:, :, :], in0=kT.t[:, :, :], scalar1=c['mask01_f'].t[:, 1:2], scalar2=None, op0=ALU.mult),
                          reads=[kT, c['mask01_f']], writes=[kT1])
                    mk.op('dve', lambda e: e.tensor_scalar(out=kT.t[:, :, :], in0=kT.t[:, :, :], scalar1=c['mask01_f'].t[:, 0:1], scalar2=None, op0=ALU.mult),
                          reads=[kT, c['mask01_f']], writes=[kT])
                    a, b = PA['dl']
                    mk.op('dve', lambda e: e.tensor_tensor(out=dtmp.t[:, 0:32], in0=par.t[:, a:a + 32], in1=par.t[:, a + 32:a + 64], op=ALU.mult),
                          reads=[par], writes=[dtmp])
                    mk.op('dve', lambda e: e.tensor_tensor(out=dtmp.t[:, 32:64], in0=par.t[:, a + 64:a + 96], in1=par.t[:, a + 96:a + 128], op=ALU.mult),
                          reads=[par], writes=[dtmp])
                    mk.op('dve', lambda e: e.tensor_reduce(out=lamt.t[:, 0:2], in_=dtmp.t[:, 0:64].rearrange("p (a b) -> p a b", a=2), axis=AX.X, op=ALU.add),
                          reads=[dtmp], writes=[lamt])
                    mk.op('act', lambda e: e.activation(out=lamt.t[:, 2:4], in_=lamt.t[:, 0:2], func=AF.Exp), reads=[lamt], writes=[lamt])
                    mk.op('dve', lambda e: e.tensor_tensor(out=lamt.t[:, 4:5], in0=lamt.t[:, 3:4], in1=lamt.t[:, 2:3], op=ALU.subtract),
                          reads=[lamt], writes=[lamt])
                    mk.op('dve', lambda e: e.tensor_scalar(out=lamt.t[:, 5:6], in0=lamt.t[:, 4:5], scalar1=-lam_init, scalar2=None, op0=ALU.add),
                          reads=[lamt], writes=[lamt])
                    sa_, sb_ = PA['subln']
                    mk.op('dve', lambda e: e.tensor_scalar(out=gs.t[:, :], in0=par.t[0:64, sa_:sb_], scalar1=(1.0 - lam_init), scalar2=None, op0=ALU.mult),
                          reads=[par], writes=[gs])

                    K.dump('d_lamt', lamt, lamt.t[:, :], [128, 8])
                    K.dump('d_gs', gs, gs.t[:, :], [64, 1])

                    def fin_diff(s_, cq, acc):
                        h, comp = s_ // 2, s_ % 2
                        rs = rsr.next()
                        mk.op('act', lambda e: e.activation(out=rs.t[0:64, :], in_=acc.t[64:128, :], func=AF.Copy), reads=[acc], writes=[rs])
                        mk.op('dve', lambda e: e.reciprocal(out=rs.t[0:64, :], in_=rs.t[0:64, :]), reads=[rs], writes=[rs])
                        if comp == 0:
                            mk.op('dve', lambda e: e.tensor_tensor(out=hold.t[:, :], in0=acc.t[0:64, :], in1=rs.t[0:64, :], op=ALU.mult),
                                  reads=[acc, rs], writes=[hold])
                            return
                        mk.op('dve', lambda e: e.tensor_tensor(out=dd.t[:, :], in0=acc.t[0:64, :], in1=rs.t[0:64, :], op=ALU.mult),
                              reads=[acc, rs], writes=[dd])
                        mk.op('dve', lambda e: e.scalar_tensor_tensor(out=dd.t[:, :], in0=dd.t[:, :], scalar=lamt.t[0:64, 5:6], in1=hold.t[:, :],
                                                                      op0=ALU.mult, op1=ALU.add), reads=[dd, lamt, hold], writes=[dd])
                        K.dump('d_hold', hold, hold.t[:, :], [64, 512])
                        K.dump('d_dd', dd, dd.t[:, :], [64, 512])
                        mk.op('pool', lambda e: e.tensor_tensor(out=sq.t[:, :], in0=dd.t[:, :], in1=dd.t[:, :], op=ALU.mult), reads=[dd], writes=[sq])
                        mp = pp.next()
                        mk.op('pe', lambda e: e.matmul(mp.t[0:64, :], lhsT=c['ones_f'].t[0:64, 0:64], rhs=sq.t[:, :], start=True, stop=True),
                              reads=[c['ones_f'], sq], writes=[mp])
                        mk.op('act', lambda e: e.activation(out=sq.t[:, :], in_=mp.t[0:64, :], func=AF.Ln, scale=1.0 / 64, bias=LN_EPS), reads=[mp], writes=[sq])
                        mk.op('act', lambda e: e.activation(out=sq.t[:, :], in_=sq.t[:, :], func=AF.Exp, scale=-0.5), reads=[sq], writes=[sq])
                        pb = (h % 2) * 64
                        mk.op('dve', lambda e: e.scalar_tensor_tensor(out=oT[1].t[pb:pb + 64, h // 2, cq * 512:(cq + 1) * 512], in0=dd.t[:, :],
                                                                      scalar=gs.t[:, 0:1], in1=sq.t[:, :], op0=ALU.mult, op1=ALU.mult),
                              reads=[dd, gs, sq], writes=[oT[1]])

                    def diff_bias(s_, cq, j, q0, N, sbk):
                        causal_bias(sbk, j, cq, q0)

                    attention(8, 64, lambda s_: (s_ // 4, ((s_ // 2) % 2) * 64), diff_bias, lambda s_, j: None, lambda s_: s_ // 2, fin_diff,
                              float(32 ** -0.5), cq_outer=True, kbuf_fn=lambda s_: (kT if s_ % 2 == 0 else kT1))

            if 2 in branches:
                proj_fm(OFF['fox_q'], 2, qT, None)
                proj_fm(OFF['fox_k'], 2, kT, None)
                proj_v(OFF['fox_v'])
                with ExitStack() as sm:
                    wf = sb('wf', [128, 8, 4], BF16, sm)
                    ncT = sb('ncT', [4, T], F32, sm)
                    e1 = sb('e1', [4, T], F32, sm)
                    cq8 = sb('cq8', [4, T], BF16, sm)
                    nck = sb('nck', [128, 16, 4], F32, sm)
                    loadw(wf, OFF['fox_f'], 4)
                    na, nb = PA['negb']
                    mk.op('dve', lambda e: e.tensor_scalar(out=small.t[0:4, 0:1], in0=par.t[0:4, na:nb], scalar1=-1.0, scalar2=None, op0=ALU.mult),
                          reads=[par], writes=[small])
                    for tc in range(4):
                        fp = pp.next()
                        for cc in range(8):
                            mk.op('pe', lambda e: e.matmul(fp.t[0:4, :], lhsT=wf.t[:, cc, 0:4], rhs=xb.t[:, cc, tc * 512:(tc + 1) * 512],
                                                           start=(cc == 0), stop=(cc == 7)), reads=[wf, xb], writes=[fp])
                        mk.op('act', lambda e: e.activation(out=e1.t[:, tc * 512:(tc + 1) * 512], in_=fp.t[0:4, :], func=AF.Exp, scale=-1.0,
                                                            bias=small.t[0:4, 0:1]), reads=[fp, small], writes=[e1])
                    mk.op('act', lambda e: e.activation(out=e1.t[:, :], in_=e1.t[:, :], func=AF.Ln, scale=1.0, bias=1.0), reads=[e1], writes=[e1])
                    mk.op('dve', lambda e: e.tensor_tensor_scan(out=ncT.t[:, :], data0=c['ones_f'].t[0:4, 0:1].to_broadcast([4, T]),
                                                                data1=e1.t[:, :], initial=0.0, op0=ALU.mult, op1=ALU.add),
                          reads=[e1, c['ones_f']], writes=[ncT])
                    mk.op('dve', lambda e: e.tensor_scalar(out=cq8.t[:, :], in0=ncT.t[:, :], scalar1=-8.0, scalar2=None, op0=ALU.mult),
                          reads=[ncT], writes=[cq8])
                    tp = pp.next()
                    for tt in range(16):
                        mk.op('pe', lambda e: e.transpose(out=tp.t[:, tt * 4:(tt + 1) * 4], in_=ncT.t[0:4, tt * 128:(tt + 1) * 128],
                                                          identity=c['ident_f'].t[0:4, 0:4]), reads=[ncT, c['ident_f']], writes=[tp])
                    mk.op('act', lambda e: e.activation(out=nck.t[:, :, :], in_=tp.t[:, 0:64].rearrange("p (a b) -> p a b", b=4), func=AF.Copy),
                          reads=[tp], writes=[nck])

                    def fox_bias(s_, cq, j, q0, N, sbk):
                        mk.op('pe', lambda e: e.matmul(sbk.t[:, 0:N], lhsT=c['eh4'].t[0:4, s_ * 128:(s_ + 1) * 128], rhs=cq8.t[0:4, q0:q0 + N],
                                                       start=False, stop=(j < 4 * cq), skip_group_check=True), reads=[c['eh4'], cq8], writes=[sbk])
                        causal_bias(sbk, j, cq, q0)

                    attention(4, 64, lambda s_: (s_ // 2, (s_ % 2) * 64), fox_bias, lambda s_, j: (nck, nck.t[:, j, s_:s_ + 1]), lambda s_: s_,
                              fin_std(oT[2]), 0.125)

            if 3 in branches:
                rope64 = (c['C64'], c['S64'], c['R64'])
                proj_fm(OFF['dsa_q'], 2, qT, rope64)
                proj_fm(OFF['dsa_k'], 2, kT, rope64)
                proj_v(OFF['dsa_v'])
                with ExitStack() as sm:
                    qiT = sb('qiT', [128, 2, T], BF16, sm)
                    kiT = sb('kiT', [128, 1, T], BF16, sm)
                    wk2 = sb('wk2', [128, 8, 128], BF16, sm)
                    ww = sb('ww', [128, 8, 4], BF16, sm)
                    wiT = sb('wiT', [4, T], F32, sm)
                    wi = sb('wi', [128, 16, 4], F32, sm)
                    score = sb('score', [128, T], F32, sm)
                    rl = Ring([sb('rl%d' % i, [128, 1024], F32, sm) for i in range(2)])
                    mb = sb('mb', [128, 4, T], BF16, sm)
                    bs = sb('bs', [128, 64], F32, sm)
                    proj_fm(OFF['idx_q'], 2, qiT, rope64)
                    loadw(wk2, OFF['idx_k'], 64, 0)
                    loadw(wk2, OFF['idx_k'], 64, 64)
                    proj_fm(0, 1, kiT, rope64, wb=wk2)
                    loadw(ww, OFF['idx_w'], 4)
                    for tc in range(4):
                        fp = pp.next()
                        for cc in range(8):
                            mk.op('pe', lambda e: e.matmul(fp.t[0:4, :], lhsT=ww.t[:, cc, 0:4], rhs=xb.t[:, cc, tc * 512:(tc + 1) * 512],
                                                           start=(cc == 0), stop=(cc == 7)), reads=[ww, xb], writes=[fp])
                        mk.op('act', lambda e: e.activation(out=wiT.t[:, tc * 512:(tc + 1) * 512], in_=fp.t[0:4, :], func=AF.Copy), reads=[fp], writes=[wiT])
                    tp = pp.next()
                    for tt in range(16):
                        mk.op('pe', lambda e: e.transpose(out=tp.t[:, tt * 4:(tt + 1) * 4], in_=wiT.t[0:4, tt * 128:(tt + 1) * 128],
                                                          identity=c['ident_f'].t[0:4, 0:4]), reads=[wiT, c['ident_f']], writes=[tp])
                    mk.op('act', lambda e: e.activation(out=wi.t[:, :, :], in_=tp.t[:, 0:64].rearrange("p (a b) -> p a b", b=4), func=AF.Copy),
                          reads=[tp], writes=[wi])
                    big = [(K.pst[0], [ps[0], ps[1]]), (K.pst[1], [ps[2], ps[3]])]

                    def indexer(cq):
                        for ql in range(4):
                            qt = 4 * cq + ql
                            KL = (qt + 1) * 128
                            for h in range(4):
                                ch, base = h // 2, (h % 2) * 64
                                r = rl.next()
                                for blk in range(0, KL, 1024):
                                    bt, bb = big[(blk // 1024) % 2]
                                    w_ = min(1024, KL - blk)
                                    for sub in range(0, w_, 512):
                                        sw = min(512, w_ - sub)
                                        mk.op('pe', lambda e: e.matmul(bt[:, sub:sub + sw], lhsT=qiT.t[base:base + 64, ch, qt * 128:(qt + 1) * 128],
                                                                       rhs=kiT.t[base:base + 64, 0, blk + sub:blk + sub + sw], start=True, stop=True),
                                              reads=[qiT, kiT], writes=[bb[sub // 512]])
                                    mk.op('act', lambda e: e.activation(out=r.t[:, 0:w_], in_=bt[:, 0:w_], func=AF.Relu), reads=bb, writes=[r])
                                    if h == 0:
                                        mk.op('dve', lambda e: e.tensor_scalar(out=score.t[:, blk:blk + w_], in0=r.t[:, 0:w_], scalar1=wi.t[:, qt, 0:1],
                                                                               scalar2=None, op0=ALU.mult), reads=[r, wi], writes=[score])
                                    else:
                                        mk.op('dve', lambda e: e.scalar_tensor_tensor(out=score.t[:, blk:blk + w_], in0=r.t[:, 0:w_],
                                                                                      scalar=wi.t[:, qt, h:h + 1], in1=score.t[:, blk:blk + w_],
                                                                                      op0=ALU.mult, op1=ALU.add), reads=[r, wi, score], writes=[score])
                            if qt >= 2:
                                mk.op('dve', lambda e: e.tensor_reduce(out=bs.t[:, 0:1], in_=score.t[:, 0:KL], axis=AX.X, op=ALU.max), reads=[score], writes=[bs])
                                mk.op('dve', lambda e: e.tensor_reduce(out=bs.t[:, 1:2], in_=score.t[:, 0:KL], axis=AX.X, op=ALU.min), reads=[score], writes=[bs])
                            mk.op('pool', lambda e: e.tensor_tensor(out=score.t[:, qt * 128:KL], in0=score.t[:, qt * 128:KL], in1=c['tribig_f'].t[:, :], op=ALU.add),
                                  reads=[score, c['tribig_f']], writes=[score])
                            if qt >= 2:
                                mk.op('dve', lambda e: e.tensor_scalar(out=bs.t[:, 2:3], in0=bs.t[:, 1:2], scalar1=-1.0, scalar2=None, op0=ALU.add), reads=[bs], writes=[bs])
                                mk.op('dve', lambda e: e.tensor_tensor(out=bs.t[:, 3:4], in0=bs.t[:, 0:1], in1=bs.t[:, 1:2], op=ALU.subtract), reads=[bs], writes=[bs])
                                mk.op('dve', lambda e: e.tensor_scalar(out=bs.t[:, 3:4], in0=bs.t[:, 3:4], scalar1=2.0, scalar2=None, op0=ALU.add), reads=[bs], writes=[bs])
                                mk.op('dve', lambda e: e.tensor_scalar(out=bs.t[:, 8:8 + NBIS], in0=c['pow2_f'].t[:, 0:NBIS], scalar1=bs.t[:, 3:4], scalar2=None, op0=ALU.mult),
                                      reads=[bs, c['pow2_f']], writes=[bs])
                                for it in range(NBIS):
                                    mk.op('dve', lambda e: e.tensor_tensor(out=bs.t[:, 4:5], in0=bs.t[:, 2:3], in1=bs.t[:, 8 + it:9 + it], op=ALU.add), reads=[bs], writes=[bs])
                                    mk.op('dve', lambda e: e.tensor_scalar(out=mb.t[:, ql, 0:KL], in0=score.t[:, 0:KL], scalar1=bs.t[:, 4:5], scalar2=None,
                                                                           op0=ALU.is_gt, op1=ALU.add, accum_out=bs.t[:, 5:6]), reads=[score, bs], writes=[mb, bs])
                                    mk.op('dve', lambda e: e.tensor_scalar(out=bs.t[:, 6:7], in0=bs.t[:, 5:6], scalar1=255.5, scalar2=bs.t[:, 8 + it:9 + it],
                                                                           op0=ALU.is_gt, op1=ALU.mult), reads=[bs], writes=[bs])
                                    mk.op('dve', lambda e: e.tensor_tensor(out=bs.t[:, 2:3], in0=bs.t[:, 2:3], in1=bs.t[:, 6:7], op=ALU.add), reads=[bs], writes=[bs])
                            else:
                                mk.op('dve', lambda e: e.memset(bs.t[:, 2:3], -1e29), writes=[bs])
                            mk.op('dve', lambda e: e.tensor_scalar(out=mb.t[:, ql, 0:KL], in0=score.t[:, 0:KL], scalar1=bs.t[:, 2:3], scalar2=NEG,
                                                                   op0=ALU.is_le, op1=ALU.mult), reads=[score, bs], writes=[mb])
                            if qt == 5:
                                K.dump('d_bs5', bs, bs.t[:, :], [128, 64])
                                K.dump('d_score5', score, score.t[:, 0:KL], [128, KL])
                                K.dump('d_mb5', mb, mb.t[:, ql, 0:KL], [128, KL])
                            if qt == 1:
                                K.dump('d_score1', score, score.t[:, 0:KL], [128, KL])
                                K.dump('d_mb1', mb, mb.t[:, ql, 0:KL], [128, KL])

                    def dsa_bias(s_, cq, j, q0, N, sbk):
                        if s_ == 0 and j == 0:
                            indexer(cq)
                        qts = [qt for qt in range(4 * cq, 4 * cq + 4) if qt >= j]
                        for i, qt in enumerate(qts):
                            o_ = qt * 128 - q0
                            mk.op('pe', lambda e: e.matmul(sbk.t[:, o_:o_ + 128], lhsT=mb.t[:, qt - 4 * cq, j * 128:(j + 1) * 128], rhs=c['ident'].t[:, :],
                                                           start=False, stop=(i == len(qts) - 1), skip_group_check=True), reads=[mb, c['ident']], writes=[sbk])

                    attention(4, 64, lambda s_: (s_ // 2, (s_ % 2) * 64), dsa_bias, lambda s_, j: None, lambda s_: s_, fin_std(oT[3]), 0.125, cq_outer=True)
            mk.barrier()

        if 'oT' in dbg:
            for i in range(4):
                for ch in range(2):
                    mk.dma('pool', lambda e: e.dma_start(out=dr['oT'].t[i * 256 + ch * 128:i * 256 + (ch + 1) * 128, :], in_=oT[i].t[:, ch, :]),
                           reads=[oT[i]], writes=[dr['oT']])
        mk.barrier()
        if not K.cfg.get('merge', True):
            return
        mg = sb('mg', [128, 8, T], BF16, sa)
        pp = Ring([ps[0], ps[1], ps[2], ps[3]])
        pq = Ring([ps[4], ps[5], ps[6], ps[7]])
        with ExitStack() as sm:
            wbr = sb('wbr', [128, 4, 2, D], BF16, sm)
            wg = Ring([sb('wg%d' % i, [128, 8, 4, 128], BF16, sm) for i in range(2)])
            sgr = Ring([sb('sg%d' % i, [128, 512], F32, sm) for i in range(2)])
            tmr = Ring([sb('tm%d' % i, [128, 512], F32, sm) for i in range(2)])
            acr = Ring([sb('ac%d' % i, [128, 512], F32, sm) for i in range(2)])
            wbv = dr['w_branch'].t[l].rearrange("i (kc p) n -> p i kc n", p=128)
            for i in range(4):
                mk.dma('pool', lambda e: e.dma_start(out=wbr.t[:, i, :, :], in_=wbv[:, i, :, :]), reads=[dr['w_branch']], writes=[wbr])
            g0 = OFF['gates']
            wgv = w_in_v[:, :, g0:g0 + 4096].rearrange("p c (i n) -> p c i n", i=4)
            for m in range(8):
                wgt = wg.next()
                for i in range(4):
                    mk.dma('pool', lambda e: e.dma_start(out=wgt.t[:, :, i, :], in_=wgv[:, :, i, m * 128:(m + 1) * 128]), reads=[w_in], writes=[wgt])
                for tc in range(4):
                    ac = acr.next()
                    for i in range(4):
                        gp = pp.next()
                        for cc in range(8):
                            mk.op('pe', lambda e: e.matmul(gp.t[:, :], lhsT=wgt.t[:, cc, i, :], rhs=xb.t[:, cc, tc * 512:(tc + 1) * 512],
                                                           start=(cc == 0), stop=(cc == 7)), reads=[wgt, xb], writes=[gp])
                        sg = sgr.next()
                        mk.op('act', lambda e: e.activation(out=sg.t[:, :], in_=gp.t[:, :], func=AF.Sigmoid), reads=[gp], writes=[sg])
                        bp = pq.next()
                        for kc in range(2):
                            mk.op('pe', lambda e: e.matmul(bp.t[:, :], lhsT=wbr.t[:, i, kc, m * 128:(m + 1) * 128], rhs=oT[i].t[:, kc, tc * 512:(tc + 1) * 512],
                                                           start=(kc == 0), stop=(kc == 1)), reads=[wbr, oT[i]], writes=[bp])
                        if i == 0:
                            mk.op('dve', lambda e: e.tensor_tensor(out=ac.t[:, :], in0=bp.t[:, :], in1=sg.t[:, :], op=ALU.mult), reads=[bp, sg], writes=[ac])
                        else:
                            tm = tmr.next()
                            mk.op('dve', lambda e: e.tensor_tensor(out=tm.t[:, :], in0=bp.t[:, :], in1=sg.t[:, :], op=ALU.mult), reads=[bp, sg], writes=[tm])
                            if i < 3:
                                mk.op('pool', lambda e: e.tensor_tensor(out=ac.t[:, :], in0=ac.t[:, :], in1=tm.t[:, :], op=ALU.add), reads=[ac, tm], writes=[ac])
                            else:
                                mk.op('pool', lambda e: e.tensor_tensor(out=mg.t[:, m, tc * 512:(tc + 1) * 512], in0=ac.t[:, :], in1=tm.t[:, :], op=ALU.add),
                                      reads=[ac, tm], writes=[mg])
            mk.barrier()
        if not K.cfg.get('ln1', True):
            return
        with ExitStack() as so:
            wo = sb('wo', [128, 8, D], BF16, so)
            xres = sb('xres', [128, 8, 512], F32, so)
            y = sb('y', [128, 8, 512], F32, so)
            sqr = Ring([sb('sq%d' % i, [128, 512], F32, so) for i in range(2)])
            mean = sb('mean', [128, 512], F32, so)
            rstd = sb('rstd', [128, 512], F32, so)
            msq = sb('msq', [128, 512], F32, so)
            rowf = Ring([sb('rowf%d' % i, [128, D], F32, so) for i in range(2)])
            rowb = Ring([sb('rowb%d' % i, [128, D], BF16, so) for i in range(2)])
            wov = dr['w_out'].t[l].rearrange("(kc p) n -> p kc n", p=128)
            for kc in range(8):
                mk.dma('pool', lambda e: e.dma_start(out=wo.t[:, kc, :], in_=wov[:, kc, :]), reads=[dr['w_out']], writes=[wo])
            x1Tv = dr['x1T'].t.rearrange("(c p) n -> p c n", p=128)
            ga, gb = PA['ln1_g']
            ba, bb_ = PA['ln1_b']
            big = [(K.pst[0], [ps[0], ps[1]]), (K.pst[1], [ps[2], ps[3]])]
            for tc in range(4):
                c0 = t0g + tc * 512
                mk.dma('sp', lambda e: e.dma_start(out=xres.t[:, :, :], in_=srcv[:, :, c0:c0 + 512]), reads=[src], writes=[xres])
                for m in range(8):
                    hp = pq.next()
                    for kc in range(8):
                        mk.op('pe', lambda e: e.matmul(hp.t[:, :], lhsT=wo.t[:, kc, m * 128:(m + 1) * 128], rhs=mg.t[:, kc, tc * 512:(tc + 1) * 512],
                                                       start=(kc == 0), stop=(kc == 7)), reads=[wo, mg], writes=[hp])
                    mk.op('dve', lambda e: e.scalar_tensor_tensor(out=y.t[:, m, :], in0=xres.t[:, m, :], scalar=ALPHA, in1=hp.t[:, :], op0=ALU.mult, op1=ALU.add),
                          reads=[xres, hp], writes=[y])
                s1, s2 = ps[4], ps[5]
                for m in range(8):
                    mk.op('pe', lambda e: e.matmul(s1.t[:, :], lhsT=c['ones_f'].t[:, :], rhs=y.t[:, m, :], start=(m == 0), stop=(m == 7)),
                          reads=[c['ones_f'], y], writes=[s1])
                for m in range(8):
                    sq = sqr.next()
                    mk.op('act', lambda e: e.activation(out=sq.t[:, :], in_=y.t[:, m, :], func=AF.Square), reads=[y], writes=[sq])
                    mk.op('pe', lambda e: e.matmul(s2.t[:, :], lhsT=c['ones_f'].t[:, :], rhs=sq.t[:, :], start=(m == 0), stop=(m == 7)),
                          reads=[c['ones_f'], sq], writes=[s2])
                mk.op('act', lambda e: e.activation(out=mean.t[:, :], in_=s1.t[:, :], func=AF.Copy, scale=1.0 / D), reads=[s1], writes=[mean])
                mk.op('pool', lambda e: e.tensor_tensor(out=msq.t[:, :], in0=mean.t[:, :], in1=mean.t[:, :], op=ALU.mult), reads=[mean], writes=[msq])
                mk.op('dve', lambda e: e.scalar_tensor_tensor(out=rstd.t[:, :], in0=s2.t[:, :], scalar=1.0 / D, in1=msq.t[:, :], op0=ALU.mult, op1=ALU.subtract),
                      reads=[s2, msq], writes=[rstd])
                mk.op('act', lambda e: e.activation(out=rstd.t[:, :], in_=rstd.t[:, :], func=AF.Ln, scale=1.0, bias=LN_EPS), reads=[rstd], writes=[rstd])
                mk.op('act', lambda e: e.activation(out=rstd.t[:, :], in_=rstd.t[:, :], func=AF.Exp, scale=-0.5), reads=[rstd], writes=[rstd])
                for m in range(8):
                    mk.op('dve', lambda e: e.tensor_tensor(out=y.t[:, m, :], in0=y.t[:, m, :], in1=mean.t[:, :], op=ALU.subtract), reads=[y, mean], writes=[y])
                    mk.op('pool', lambda e: e.tensor_tensor(out=y.t[:, m, :], in0=y.t[:, m, :], in1=rstd.t[:, :], op=ALU.mult), reads=[y, rstd], writes=[y])
                    mk.op('act', lambda e: e.activation(out=y.t[:, m, :], in_=y.t[:, m, :], func=AF.Identity, scale=par.t[:, ga + m:ga + m + 1],
                                                        bias=par.t[:, ba + m:ba + m + 1]), reads=[y, par], writes=[y])
                mk.dma('sp', lambda e: e.dma_start(out=x1Tv[:, :, c0:c0 + 512], in_=y.t[:, :, :]), reads=[y], writes=[dr['x1T']])
                for tt in range(4):
                    bt, bb2 = big[tt % 2]
                    for m in range(8):
                        mk.op('pe', lambda e: e.transpose(out=bt[:, m * 128:(m + 1) * 128], in_=y.t[:, m, tt * 128:(tt + 1) * 128], identity=c['ident_f'].t[:, :]),
                              reads=[y, c['ident_f']], writes=[bb2[m // 4]])
                    rf = rowf.next()
                    rb = rowb.next()
                    mk.op('act', lambda e: e.activation(out=rf.t[:, :], in_=bt[:, :], func=AF.Copy), reads=bb2, writes=[rf])
                    mk.op('pool', lambda e: e.tensor_copy(out=rb.t[:, :], in_=rf.t[:, :]), reads=[rf], writes=[rb])
                    r0 = c0 + tt * 128
                    mk.dma('sp', lambda e: e.dma_start(out=dr['x1r'].t[r0:r0 + 128, :], in_=rf.t[:, :]), reads=[rf], writes=[dr['x1r']])
                    mk.dma('sp', lambda e: e.dma_start(out=dr['x1rb'].t[r0:r0 + 128, :], in_=rb.t[:, :]), reads=[rb], writes=[dr['x1rb']])
            mk.barrier()


def stage_b(K, l, last):
    mk, sb, dr, nc, ps, c = K.mk, K.sb, K.dr, K.nc, K.ps, K.c
    NT = K.NT
    NTL = NT // 128
    NEX = K.cfg.get('ne', NE)
    IOA = bass.IndirectOffsetOnAxis
    pp = Ring([ps[0], ps[1], ps[2], ps[3]])
    pq = Ring([ps[4], ps[5], ps[6], ps[7]])
    big = [(K.pst[0], [ps[0], ps[1]]), (K.pst[1], [ps[2], ps[3]])]
    x1Tv = dr['x1T'].t.rearrange("(c p) n -> p c n", p=128)
    with ExitStack() as sbs:
        par = sb('parB', [128, NPB], F32, sbs)
        slotall = sb('slotall', [128, NTL, 8], I32, sbs)
        mk.dma('sp', lambda e: e.dma_start(out=par.t[:, :], in_=dr['parB'].t[l]), reads=[dr['parB']], writes=[par])
        with ExitStack() as s1:
            wr = sb('wr', [128, 8, 256], BF16, s1)
            carry = sb('carry', [128, 256], F32, s1)
            tokid = sb('tokid', [128, NTL], I32, s1)
            fill = sb('fill', [128, NE * CAP * 2 // 128], I32, s1)
            zrow = sb('zrow', [128, D], BF16, s1)
            xtbr = Ring([sb('xtb%d' % i, [128, 8, 128], BF16, s1) for i in range(3)])
            f256 = {n: sb(n, [128, 256], F32, s1) for n in ('sc', 'bi', 'eq', 'bi2', 'msk', 'sel', 'wd', 'pos', 'slotf', 'junk')}
            selb = sb('selb', [128, 256], BF16, s1)
            sm8 = sb('sm8', [128, 64], F32, s1)
            stg = sb('stg', [128, 8, 2], I32, s1)
            mk.dma('pool', lambda e: e.dma_start(out=wr.t[:, :, :], in_=dr['w_router'].t[l].rearrange("(c p) n -> p c n", p=128)),
                   reads=[dr['w_router']], writes=[wr])
            mk.op('dve', lambda e: e.memset(carry.t[:, :], 0.0), writes=[carry])
            mk.op('pool', lambda e: e.iota(tokid.t[:, :], pattern=[[128, NTL]], base=0, channel_multiplier=1), writes=[tokid])
            mk.op('dve', lambda e: e.memset(fill.t[:, :], 0), writes=[fill])
            mk.op('dve', lambda e: e.memset(fill.t[:, :].rearrange("p (a b) -> p a b", b=2)[:, :, 0], NT), writes=[fill])
            mk.op('pool', lambda e: e.memset(zrow.t[:, :], 0.0), writes=[zrow])
            mk.dma('sp', lambda e: e.dma_start(out=dr['tokw'].t.rearrange("(p a) b -> p (a b)", p=128), in_=fill.t[:, :]), reads=[fill], writes=[dr['tokw']])
            mk.dma('sp', lambda e: e.dma_start(out=dr['x1rb'].t[NT:NT + 128, :], in_=zrow.t[:, :]), reads=[zrow], writes=[dr['x1rb']])
            if NEX < NE:
                yz = dr['ye'].t.rearrange("(p a) n -> p a n", p=128)
                for a0 in range(0, NE * CAP // 128, 96):
                    mk.dma('sp', lambda e: e.dma_start(out=yz[:, a0:a0 + 96, :], in_=zrow.t[:, :].unsqueeze(1).to_broadcast([128, 96, D])), reads=[zrow], writes=[dr['ye']])
            sc, bi, eq, bi2, msk, sel, wd, pos, slotf, junk = [f256[n] for n in ('sc', 'bi', 'eq', 'bi2', 'msk', 'sel', 'wd', 'pos', 'slotf', 'junk')]
            ra, rb_ = PB['rbias']
            for tt in range(NTL):
                xtb = xtbr.next()
                mk.dma('pool', lambda e: e.dma_start(out=xtb.t[:, :, :], in_=x1Tv[:, :, tt * 128:(tt + 1) * 128]), reads=[dr['x1T']], writes=[xtb])
                lp = pp.next()
                for cc in range(8):
                    mk.op('pe', lambda e: e.matmul(lp.t[:, 0:256], lhsT=xtb.t[:, cc, :], rhs=wr.t[:, cc, :], start=(cc == 0), stop=(cc == 7)),
                          reads=[xtb, wr], writes=[lp])
                mk.op('act', lambda e: e.activation(out=sc.t[:, :], in_=lp.t[:, 0:256], func=AF.Sigmoid), reads=[lp], writes=[sc])
                mk.op('dve', lambda e: e.tensor_tensor(out=bi.t[:, :], in0=sc.t[:, :], in1=par.t[:, ra:rb_], op=ALU.add), reads=[sc, par], writes=[bi])
                bi3 = bi.t[:, :].rearrange("p (g k) -> p g k", k=32)
                mk.op('dve', lambda e: e.tensor_reduce(out=sm8.t[:, 0:8], in_=bi3, axis=AX.X, op=ALU.max), reads=[bi], writes=[sm8])
                mk.op('dve', lambda e: e.tensor_tensor(out=eq.t[:, :].rearrange("p (g k) -> p g k", k=32), in0=bi3,
                                                       in1=sm8.t[:, 0:8].unsqueeze(2).to_broadcast([128, 8, 32]), op=ALU.is_equal), reads=[bi, sm8], writes=[eq])
                mk.op('dve', lambda e: e.scalar_tensor_tensor(out=bi2.t[:, :], in0=eq.t[:, :], scalar=-1e30, in1=bi.t[:, :], op0=ALU.mult, op1=ALU.add),
                      reads=[eq, bi], writes=[bi2])
                mk.op('dve', lambda e: e.tensor_reduce(out=sm8.t[:, 8:16], in_=bi2.t[:, :].rearrange("p (g k) -> p g k", k=32), axis=AX.X, op=ALU.max),
                      reads=[bi2], writes=[sm8])
                mk.op('dve', lambda e: e.tensor_tensor(out=sm8.t[:, 16:24], in0=sm8.t[:, 0:8], in1=sm8.t[:, 8:16], op=ALU.add), reads=[sm8], writes=[sm8])
                mk.op('dve', lambda e: e.max(out=sm8.t[:, 24:32], in_=sm8.t[:, 16:24]), reads=[sm8], writes=[sm8])
                mk.op('dve', lambda e: e.tensor_scalar(out=sm8.t[:, 32:40], in0=sm8.t[:, 16:24], scalar1=sm8.t[:, 27:28], scalar2=-1e30, op0=ALU.is_lt, op1=ALU.mult),
                      reads=[sm8], writes=[sm8])
                mk.op('dve', lambda e: e.tensor_tensor(out=msk.t[:, :].rearrange("p (g k) -> p g k", k=32), in0=bi3,
                                                       in1=sm8.t[:, 32:40].unsqueeze(2).to_broadcast([128, 8, 32]), op=ALU.add), reads=[bi, sm8], writes=[msk])
                mk.op('dve', lambda e: e.max(out=sm8.t[:, 40:48], in_=msk.t[:, :]), reads=[msk], writes=[sm8])
                mk.op('dve', lambda e: e.tensor_scalar(out=sel.t[:, :], in0=msk.t[:, :], scalar1=sm8.t[:, 47:48], scalar2=None, op0=ALU.is_ge), reads=[msk, sm8], writes=[sel])
                mk.op('dve', lambda e: e.scalar_tensor_tensor(out=wd.t[:, :], in0=sel.t[:, :], scalar=1.0, in1=sc.t[:, :], op0=ALU.mult, op1=ALU.mult,
                                                              accum_out=sm8.t[:, 48:49]), reads=[sel, sc], writes=[wd, sm8])
                mk.op('dve', lambda e: e.reciprocal(out=sm8.t[:, 49:50], in_=sm8.t[:, 48:49]), reads=[sm8], writes=[sm8])
                mk.op('dve', lambda e: e.tensor_scalar(out=wd.t[:, :], in0=wd.t[:, :], scalar1=sm8.t[:, 49:50], scalar2=2.5, op0=ALU.mult, op1=ALU.mult),
                      reads=[wd, sm8], writes=[wd])
                mk.op('pool', lambda e: e.tensor_copy(out=selb.t[:, :], in_=sel.t[:, :]), reads=[sel], writes=[selb])
                p1 = pq.next()
                p2 = pq.next()
                mk.op('pe', lambda e: e.matmul(p1.t[:, 0:256], lhsT=c['lstrict'].t[:, :], rhs=selb.t[:, :], start=True, stop=True), reads=[c['lstrict'], selb], writes=[p1])
                mk.op('pe', lambda e: e.matmul(p2.t[:, 0:256], lhsT=c['ones'].t[:, :], rhs=selb.t[:, :], start=True, stop=True), reads=[c['ones'], selb], writes=[p2])
                mk.op('dve', lambda e: e.tensor_tensor(out=pos.t[:, :], in0=p1.t[:, 0:256], in1=carry.t[:, :], op=ALU.add), reads=[p1, carry], writes=[pos])
                mk.op('dve', lambda e: e.tensor_tensor(out=carry.t[:, :], in0=p2.t[:, 0:256], in1=carry.t[:, :], op=ALU.add), reads=[p2, carry], writes=[carry])
                mk.op('dve', lambda e: e.scalar_tensor_tensor(out=pos.t[:, :], in0=pos.t[:, :], scalar=float(CAP - 1), in1=c['ebase_f'].t[:, :], op0=ALU.min, op1=ALU.add),
                      reads=[pos, c['ebase_f']], writes=[pos])
                mk.op('dve', lambda e: e.tensor_tensor(out=slotf.t[:, :], in0=pos.t[:, :], in1=sel.t[:, :], op=ALU.mult), reads=[pos, sel], writes=[slotf])
                mk.op('dve', lambda e: e.max(out=sm8.t[:, 50:58], in_=slotf.t[:, :]), reads=[slotf], writes=[sm8])
                for k in range(8):
                    mk.op('dve', lambda e: e.scalar_tensor_tensor(out=junk.t[:, :], in0=slotf.t[:, :], scalar=sm8.t[:, 50 + k:51 + k], in1=wd.t[:, :],
                                                                  op0=ALU.is_equal, op1=ALU.mult, accum_out=stg.t[:, k, 1:2].bitcast(F32)),
                          reads=[slotf, sm8, wd], writes=[junk, stg])
                mk.op('dve', lambda e: e.tensor_scalar(out=slotall.t[:, tt, :], in0=sm8.t[:, 50:58], scalar1=-1.0, scalar2=None, op0=ALU.add), reads=[sm8], writes=[slotall])
                mk.op('pool', lambda e: e.tensor_copy(out=stg.t[:, :, 0], in_=tokid.t[:, tt:tt + 1].to_broadcast([128, 8])), reads=[tokid], writes=[stg])
                for k in range(8):
                    mk.dma('pool', lambda e: e.indirect_dma_start(out=dr['tokw'].t[:, :], out_offset=IOA(slotall.t[:, tt, k:k + 1], 0), in_=stg.t[:, k, :], in_offset=None),
                           reads=[stg, slotall], writes=[dr['tokw']])
            mk.barrier()
        with ExitStack() as s2:
            twr = Ring([sb('tw%d' % i, [128, 3, 2], I32, s2) for i in range(5)])
            xer = Ring([sb('xe%d' % i, [128, 3, D], BF16, s2) for i in range(5)])
            xeTr = Ring([sb('xeT%d' % i, [128, 8, CAP], BF16, s2) for i in range(5)])
            wgur = Ring([sb('wgu%d' % i, [128, 8, 512], BF16, s2) for i in range(5)])
            wdnr = Ring([sb('wdn%d' % i, [128, 2, D], BF16, s2) for i in range(5)])
            hTr = Ring([sb('hT%d' % i, [128, 2, CAP], BF16, s2) for i in range(5)])
            sgr = Ring([sb('sge%d' % i, [128, CAP], F32, s2) for i in range(2)])
            yor = Ring([sb('yo%d' % i, [128, 3, D], BF16, s2) for i in range(5)])
            twv = dr['tokw'].t.rearrange("(e rt p) c -> e p rt c", rt=3, p=128)
            yev = dr['ye'].t.rearrange("(e rt p) n -> e p rt n", rt=3, p=128)
            ev = [0]
            bufs = {}

            def st_load(ex):
                tw, xe, xeT, wgu, wdn, hT, yo = twr.next(), xer.next(), xeTr.next(), wgur.next(), wdnr.next(), hTr.next(), yor.next()
                bufs[ex] = (tw, xe, xeT, wgu, wdn, hT, yo)
                mk.dma('sp', lambda e: e.dma_start(out=tw.t[:, :, :], in_=twv[ex]), reads=[dr['tokw']], writes=[tw])
                mk.dma('pool', lambda e: e.dma_start(out=wgu.t[:, :, 0:256], in_=dr['w_exp_gate'].t[l, ex].rearrange("(c p) n -> p c n", p=128)),
                       reads=[dr['w_exp_gate']], writes=[wgu])
                mk.dma('pool', lambda e: e.dma_start(out=wgu.t[:, :, 256:512], in_=dr['w_exp_up'].t[l, ex].rearrange("(c p) n -> p c n", p=128)),
                       reads=[dr['w_exp_up']], writes=[wgu])
                mk.dma('pool', lambda e: e.dma_start(out=wdn.t[:, :, :], in_=dr['w_exp_down'].t[l, ex].rearrange("(c p) n -> p c n", p=128)),
                       reads=[dr['w_exp_down']], writes=[wdn])
                for rt in range(3):
                    mk.dma('pool', lambda e: e.indirect_dma_start(out=xe.t[:, rt, :], out_offset=None, in_=dr['x1rb'].t[:, :], in_offset=IOA(tw.t[:, rt, 0:1], 0)),
                           reads=[tw, dr['x1rb']], writes=[xe])

            def st_t(ex):
                tw, xe, xeT, wgu, wdn, hT, yo = bufs[ex]
                for rt in range(3):
                    bk = pp.next()
                    vb = bk.t.bitcast(BF16)
                    for cc in range(8):
                        mk.op('pe', lambda e: e.transpose(out=vb[:, cc * 128:(cc + 1) * 128], in_=xe.t[:, rt, cc * 128:(cc + 1) * 128], identity=c['ident'].t[:, :]),
                              reads=[xe, c['ident']], writes=[bk])
                    ev[0] += 1
                    if ev[0] % 2 == 0:
                        mk.op('act', lambda e: e.activation(out=xeT.t[:, :, rt * 128:(rt + 1) * 128], in_=vb.rearrange("p (c r) -> p c r", c=8), func=AF.Copy),
                              reads=[bk], writes=[xeT])
                    else:
                        mk.op('dve', lambda e: e.tensor_copy(out=xeT.t[:, :, rt * 128:(rt + 1) * 128], in_=vb.rearrange("p (c r) -> p c r", c=8)), reads=[bk], writes=[xeT])

            def st_gu(ex):
                tw, xe, xeT, wgu, wdn, hT, yo = bufs[ex]
                for fc in range(2):
                    gp = pq.next()
                    up_ = pq.next()
                    for cc in range(8):
                        mk.op('pe', lambda e: e.matmul(gp.t[:, 0:CAP], lhsT=wgu.t[:, cc, fc * 128:(fc + 1) * 128], rhs=xeT.t[:, cc, :], start=(cc == 0), stop=(cc == 7)),
                              reads=[wgu, xeT], writes=[gp])
                    for cc in range(8):
                        mk.op('pe', lambda e: e.matmul(up_.t[:, 0:CAP], lhsT=wgu.t[:, cc, 256 + fc * 128:256 + (fc + 1) * 128], rhs=xeT.t[:, cc, :], start=(cc == 0), stop=(cc == 7)),
                              reads=[wgu, xeT], writes=[up_])
                    sg = sgr.next()
                    mk.op('act', lambda e: e.activation(out=sg.t[:, :], in_=gp.t[:, 0:CAP], func=AF.Silu), reads=[gp], writes=[sg])
                    mk.op('dve', lambda e: e.tensor_tensor(out=hT.t[:, fc, :], in0=up_.t[:, 0:CAP], in1=sg.t[:, :], op=ALU.mult), reads=[up_, sg], writes=[hT])

            def st_d(ex):
                tw, xe, xeT, wgu, wdn, hT, yo = bufs.pop(ex)
                twf = tw.t[:, :, :].bitcast(F32)
                for rt in range(3):
                    for half in range(2):
                        yp = pp.next()
                        for fc in range(2):
                            mk.op('pe', lambda e: e.matmul(yp.t[:, :], lhsT=hT.t[:, fc, rt * 128:(rt + 1) * 128], rhs=wdn.t[:, fc, half * 512:(half + 1) * 512],
                                                           start=(fc == 0), stop=(fc == 1)), reads=[hT, wdn], writes=[yp])
                        ev[0] += 1
                        if ev[0] % 2 == 0:
                            mk.op('act', lambda e: e.activation(out=yo.t[:, rt, half * 512:(half + 1) * 512], in_=yp.t[:, :], func=AF.Copy, scale=twf[:, rt, 1:2]),
                                  reads=[yp, tw], writes=[yo])
                        else:
                            mk.op('dve', lambda e: e.tensor_scalar(out=yo.t[:, rt, half * 512:(half + 1) * 512], in0=yp.t[:, :], scalar1=twf[:, rt, 1:2], scalar2=None, op0=ALU.mult),
                                  reads=[yp, tw], writes=[yo])
                mk.dma('sp', lambda e: e.dma_start(out=yev[ex], in_=yo.t[:, :, :]), reads=[yo], writes=[dr['ye']])

            for i in range(NEX + 3):
                if i < NEX:
                    st_load(i)
                if 0 <= i - 1 < NEX:
                    st_t(i - 1)
                if 0 <= i - 2 < NEX:
                    st_gu(i - 2)
                if 0 <= i - 3 < NEX:
                    st_d(i - 3)
            mk.barrier()
        with ExitStack() as s3:
            wsgu = sb('wsgu', [128, 8, 512], BF16, s3)
            wsd = sb('wsd', [128, 2, D], BF16, s3)
            xrr = Ring([sb('xr%d' % i, [128, D], F32, s3) for i in range(3)])
            xtbr = Ring([sb('xtc%d' % i, [128, 8, 128], BF16, s3) for i in range(3)])
            ygr = Ring([sb('yg%d' % i, [128, D], BF16, s3) for i in range(16)])
            accr = Ring([sb('acc%d' % i, [128, D], F32, s3) for i in range(3)])
            hs = sb('hs', [128, 2, 128], BF16, s3)
            sgs = sb('sgs', [128, 128], F32, s3)
            st = sb('st', [128, 16], F32, s3)
            jk = sb('jk', [128, D], F32, s3)
            xo = sb('xo', [128, 8, 128], F32, s3)
            mk.dma('pool', lambda e: e.dma_start(out=wsgu.t[:, :, 0:256], in_=dr['w_sh_gate'].t[l].rearrange("(c p) n -> p c n", p=128)), reads=[dr['w_sh_gate']], writes=[wsgu])
            mk.dma('pool', lambda e: e.dma_start(out=wsgu.t[:, :, 256:512], in_=dr['w_sh_up'].t[l].rearrange("(c p) n -> p c n", p=128)), reads=[dr['w_sh_up']], writes=[wsgu])
            mk.dma('pool', lambda e: e.dma_start(out=wsd.t[:, :, :], in_=dr['w_sh_down'].t[l].rearrange("(c p) n -> p c n", p=128)), reads=[dr['w_sh_down']], writes=[wsd])
            g0, g1 = PB['ln2_g']
            b0, b1 = PB['ln2_b']
            x2Tv = dr['x2T'].t.rearrange("(c p) n -> p c n", p=128)
            for tt in range(NTL):
                xr, xtb, acc = xrr.next(), xtbr.next(), accr.next()
                mk.dma('sp', lambda e: e.dma_start(out=xr.t[:, :], in_=dr['x1r'].t[tt * 128:(tt + 1) * 128, :]), reads=[dr['x1r']], writes=[xr])
                mk.dma('pool', lambda e: e.dma_start(out=xtb.t[:, :, :], in_=x1Tv[:, :, tt * 128:(tt + 1) * 128]), reads=[dr['x1T']], writes=[xtb])
                for fc in range(2):
                    gp = pq.next()
                    up_ = pq.next()
                    for cc in range(8):
                        mk.op('pe', lambda e: e.matmul(gp.t[:, 0:128], lhsT=wsgu.t[:, cc, fc * 128:(fc + 1) * 128], rhs=xtb.t[:, cc, :], start=(cc == 0), stop=(cc == 7)),
                              reads=[wsgu, xtb], writes=[gp])
                    for cc in range(8):
                        mk.op('pe', lambda e: e.matmul(up_.t[:, 0:128], lhsT=wsgu.t[:, cc, 256 + fc * 128:256 + (fc + 1) * 128], rhs=xtb.t[:, cc, :], start=(cc == 0), stop=(cc == 7)),
                              reads=[wsgu, xtb], writes=[up_])
                    mk.op('act', lambda e: e.activation(out=sgs.t[:, :], in_=gp.t[:, 0:128], func=AF.Silu), reads=[gp], writes=[sgs])
                    mk.op('dve', lambda e: e.tensor_tensor(out=hs.t[:, fc, :], in0=up_.t[:, 0:128], in1=sgs.t[:, :], op=ALU.mult), reads=[up_, sgs], writes=[hs])
                for half in range(2):
                    yp = pp.next()
                    for fc in range(2):
                        mk.op('pe', lambda e: e.matmul(yp.t[:, :], lhsT=hs.t[:, fc, :], rhs=wsd.t[:, fc, half * 512:(half + 1) * 512], start=(fc == 0), stop=(fc == 1)),
                              reads=[hs, wsd], writes=[yp])
                    mk.op('dve', lambda e: e.scalar_tensor_tensor(out=acc.t[:, half * 512:(half + 1) * 512], in0=xr.t[:, half * 512:(half + 1) * 512], scalar=ALPHA,
                                                                  in1=yp.t[:, :], op0=ALU.mult, op1=ALU.add), reads=[xr, yp], writes=[acc])
                for k in range(8):
                    yg = ygr.next()
                    mk.dma('pool', lambda e: e.indirect_dma_start(out=yg.t[:, :], out_offset=None, in_=dr['ye'].t[:, :], in_offset=IOA(slotall.t[:, tt, k:k + 1], 0)),
                           reads=[slotall, dr['ye']], writes=[yg])
                    mk.op('dve', lambda e: e.tensor_tensor(out=acc.t[:, :], in0=acc.t[:, :], in1=yg.t[:, :], op=ALU.add), reads=[acc, yg], writes=[acc])
                mk.op('act', lambda e: e.activation(out=jk.t[:, :], in_=acc.t[:, :], func=AF.Copy, accum_out=st.t[:, 0:1]), reads=[acc], writes=[jk, st])
                mk.op('act', lambda e: e.activation(out=jk.t[:, :], in_=acc.t[:, :], func=AF.Square, accum_out=st.t[:, 1:2]), reads=[acc], writes=[jk, st])
                mk.op('dve', lambda e: e.tensor_scalar(out=st.t[:, 2:4], in0=st.t[:, 0:2], scalar1=1.0 / D, scalar2=None, op0=ALU.mult), reads=[st], writes=[st])
                mk.op('dve', lambda e: e.tensor_tensor(out=st.t[:, 4:5], in0=st.t[:, 2:3], in1=st.t[:, 2:3], op=ALU.mult), reads=[st], writes=[st])
                mk.op('dve', lambda e: e.tensor_tensor(out=st.t[:, 5:6], in0=st.t[:, 3:4], in1=st.t[:, 4:5], op=ALU.subtract), reads=[st], writes=[st])
                mk.op('act', lambda e: e.activation(out=st.t[:, 6:7], in_=st.t[:, 5:6], func=AF.Ln, scale=1.0, bias=LN_EPS), reads=[st], writes=[st])
                mk.op('act', lambda e: e.activation(out=st.t[:, 7:8], in_=st.t[:, 6:7], func=AF.Exp, scale=-0.5), reads=[st], writes=[st])
                mk.op('dve', lambda e: e.tensor_scalar(out=acc.t[:, :], in0=acc.t[:, :], scalar1=st.t[:, 2:3], scalar2=st.t[:, 7:8], op0=ALU.subtract, op1=ALU.mult),
                      reads=[acc, st], writes=[acc])
                mk.op('pool', lambda e: e.tensor_tensor(out=acc.t[:, :], in0=acc.t[:, :], in1=par.t[:, g0:g1], op=ALU.mult), reads=[acc, par], writes=[acc])
                mk.op('dve', lambda e: e.tensor_tensor(out=acc.t[:, :], in0=acc.t[:, :], in1=par.t[:, b0:b1], op=ALU.add), reads=[acc, par], writes=[acc])
                if last:
                    mk.dma('sp', lambda e: e.dma_start(out=dr['out'].t[tt * 128:(tt + 1) * 128, :], in_=acc.t[:, :]), reads=[acc], writes=[dr['out']])
                else:
                    bt, bb2 = big[tt % 2]
                    for m in range(8):
                        mk.op('pe', lambda e: e.transpose(out=bt[:, m * 128:(m + 1) * 128], in_=acc.t[:, m * 128:(m + 1) * 128], identity=c['ident_f'].t[:, :]),
                              reads=[acc, c['ident_f']], writes=[bb2[m // 4]])
                    mk.op('act', lambda e: e.activation(out=xo.t[:, :, :], in_=bt[:, :].rearrange("p (c r) -> p c r", c=8), func=AF.Copy), reads=bb2, writes=[xo])
                    mk.dma('sp', lambda e: e.dma_start(out=x2Tv[:, :, tt * 128:(tt + 1) * 128], in_=xo.t[:, :, :]), reads=[xo], writes=[dr['x2T']])
            mk.barrier()


def host_prep(inputs, NS=4, experts=True):
    x = np.asarray(inputs['x'])
    cst = make_consts()
    parA = np.zeros((2, 128, NPA), np.float32)
    parB = np.zeros((2, 128, NPB), np.float32)
    for l in range(2):
        parA[l, :, 0:8] = np.asarray(inputs['ln1_g'][l]).reshape(8, 128).T
        parA[l, :, 8:16] = np.asarray(inputs['ln1_b'][l]).reshape(8, 128).T
        parA[l, 0:4, 16] = np.asarray(inputs['b_forget'][l])
        parA[l, :, 20:148] = np.asarray(inputs['diff_lambda'][l]).reshape(1, 128)
        parA[l, 0:64, 148] = np.asarray(inputs['diff_subln'][l])
        parB[l, :, 0:256] = np.asarray(inputs['router_bias'][l])[None, :]
        parB[l, :, 256:1280] = np.asarray(inputs['ln2_g'][l])[None, :]
        parB[l, :, 1280:2304] = np.asarray(inputs['ln2_b'][l])[None, :]
    shared = {k: np.ascontiguousarray(np.asarray(inputs[k], dtype=np.float32)) for k in
              ('w_in', 'w_branch', 'w_out', 'w_router', 'w_exp_gate', 'w_exp_up', 'w_exp_down', 'w_sh_gate', 'w_sh_up', 'w_sh_down')}
    if not experts:
        for k in ('w_exp_gate', 'w_exp_up', 'w_exp_down'):
            shared.pop(k)
    shared.update(cst=cst, parA=parA, parB=parB)
    maps = []
    for core in range(8):
        xs = x[core * 4:core * 4 + NS]
        xT = np.ascontiguousarray(xs.reshape(NS * T, D).T)
        m = dict(shared)
        m['xT'] = xT
        maps.append(m)
    return maps


_CACHE = {}


def kernel(**inputs):
    cfg = {}
    if 'nc' not in _CACHE:
        _CACHE['nc'] = build(cfg)
    nc, K = _CACHE['nc']
    maps = host_prep(inputs)
    res = run_bass_kernel_spmd(nc, maps, core_ids=list(range(8)))
    out = np.concatenate([np.asarray(r['out']).reshape(4, T, D) for r in res.results], axis=0)
    return out.astype(np.float32)
```
